# Optimizing a Trainium2 kernel written in Bass

```python
import math
import jax, jax.numpy as jnp
from jax import lax
import numpy as np

D_MODEL = 2048
BATCH = 4
SEQ = 4096
DEPTH = 4

GRID_W = 64
CTX_LEN = 256
N_MIXERS = 3
DN_ALPHA = (2 * DEPTH) ** 0.25
DN_BETA = (8 * DEPTH) ** -0.25
LN_EPS = 1e-5
N_CONV_LAYERS = (DEPTH + 2) // 3
N_HGRN_LAYERS = (DEPTH + 1) // 3
N_SSD_LAYERS = DEPTH // 3

CONV_K = 31
CONV_SPLIT = D_MODEL // 2

HG_DK = 128
HG_HEADS = D_MODEL // HG_DK
HG_KD = HG_HEADS * HG_DK
HG_DV = D_MODEL // HG_HEADS
HG_CHUNK = 64

M_EXPAND = 2
M_DINNER = M_EXPAND * D_MODEL
M_HEADDIM = 64
M_HEADS = M_DINNER // M_HEADDIM
M_GROUPS = 8
M_HPG = M_HEADS // M_GROUPS
M_DSTATE = 128
M_CONV = 5
M_CONV_DIM = M_DINNER + 2 * M_GROUPS * M_DSTATE
M_IN_DIM = M_DINNER + M_CONV_DIM + 2 * M_HEADS
M_CHUNK = 64

PEER_HEADS = 8
PEER_NKEYS = 128
PEER_EXPERTS = PEER_NKEYS * PEER_NKEYS
PEER_DK = 128
PEER_TOPK = 16
PEER_BLOCK = 128

kernel_name = 'hybrid_conv_hgrn2_ssd_peer_diffusion_trunk'

F32 = jnp.float32


def layer_norm(x, g, b):
    xf = x.astype(F32)
    mu = jnp.mean(xf, -1, keepdims=True)
    var = jnp.mean(jnp.square(xf - mu), -1, keepdims=True)
    return ((xf - mu) * lax.rsqrt(var + LN_EPS) * g + b).astype(x.dtype)


def rms_norm(x, eps=1e-6):
    xf = x.astype(F32)
    return xf * lax.rsqrt(jnp.mean(jnp.square(xf), -1, keepdims=True) + eps)


def dwconv1d(u, w):
    k = w.shape[0]
    return lax.conv_general_dilated(u, w[:, None, :].astype(u.dtype), (1,), [(k // 2, k // 2)],
                                    dimension_numbers=('NWC', 'WIO', 'NWC'),
                                    feature_group_count=u.shape[-1])


def axial_dwconv(u, w):
    b, n, ch = u.shape
    rows = n // GRID_W
    uh = u[..., :CONV_SPLIT].reshape(b * rows, GRID_W, CONV_SPLIT)
    yh = dwconv1d(uh, w[:, :CONV_SPLIT]).reshape(b, n, CONV_SPLIT)
    cv = ch - CONV_SPLIT
    uv = u[..., CONV_SPLIT:].reshape(b, rows, GRID_W, cv).transpose(0, 2, 1, 3).reshape(b * GRID_W, rows, cv)
    yv = dwconv1d(uv, w[:, CONV_SPLIT:]).reshape(b, GRID_W, rows, cv).transpose(0, 2, 1, 3).reshape(b, n, cv)
    return jnp.concatenate([yh, yv], axis=-1)


def conv_mixer(h, w_in, w_dw, ln_g, ln_b, w_out, on_grid):
    a = h @ w_in
    u = a[..., :D_MODEL] * jax.nn.sigmoid(a[..., D_MODEL:])
    u = axial_dwconv(u, w_dw) if on_grid else dwconv1d(u, w_dw)
    u = jax.nn.silu(layer_norm(u, ln_g, ln_b))
    return u @ w_out


def hgrn_chunk_scan(q, k, v, log_f, s0):
    b, h, l, _ = q.shape
    nc = l // HG_CHUNK

    def chunks(t):
        return jnp.moveaxis(t.reshape(b, h, nc, HG_CHUNK, t.shape[-1]), 2, 0)

    tri = jnp.tril(jnp.ones((HG_CHUNK, HG_CHUNK), bool))[:, :, None]

    def step(s, inp):
        qc, kc, vc, gc = inp
        bc = jnp.cumsum(gc, axis=2)
        rel = jnp.exp(jnp.where(tri, bc[:, :, :, None, :] - bc[:, :, None, :, :], -jnp.inf))
        att = jnp.einsum('bhtk,bhsk,bhtsk->bhts', qc, kc, rel)
        o = jnp.einsum('bhts,bhsv->bhtv', att, vc) + jnp.einsum('bhtk,bhkv->bhtv', qc * jnp.exp(bc), s)
        b_end = bc[:, :, -1:, :]
        s_new = s * jnp.exp(b_end[:, :, 0, :, None]) + jnp.einsum('bhsk,bhsv->bhkv', kc * jnp.exp(b_end - bc), vc)
        return s_new, o

    s_fin, o = lax.scan(step, s0, (chunks(q), chunks(k), chunks(v), chunks(log_f)))
    o = jnp.moveaxis(o, 0, 2).reshape(b, h, l, v.shape[-1])
    return o, s_fin


def hgrn_mixer(h_ctx, h_lat, w_q, w_f, w_i, w_g, norm_g, w_o, lb, ctx_out):
    def to_heads(t):
        b, l, _ = t.shape
        return t.reshape(b, l, HG_HEADS, -1).transpose(0, 2, 1, 3)

    def project(h):
        q = to_heads(jax.nn.silu(h @ w_q))
        v = to_heads(h @ w_i)
        dirs = []
        for d in range(2):
            z = to_heads(h @ w_f[d]).astype(F32)
            lbd = lb[d].reshape(HG_HEADS, 1, HG_DK)
            log_f = jnp.logaddexp(jnp.log(lbd), jnp.log1p(-lbd) + jax.nn.log_sigmoid(z))
            k = (1.0 - lbd) * jax.nn.sigmoid(-z)
            dirs.append((k, log_f))
        return q, v, dirs

    def flip(t):
        return jnp.flip(t, axis=2)

    def readout(o, h):
        b, _, l, _ = o.shape
        o = rms_norm(o.transpose(0, 2, 1, 3)).reshape(b, l, D_MODEL) * norm_g
        return (o * jax.nn.silu(h @ w_g)).astype(h.dtype) @ w_o

    qc, vc, ((kc0, gc0), (kc1, gc1)) = project(h_ctx)
    ql, vl, ((kl0, gl0), (kl1, gl1)) = project(h_lat)
    s0 = jnp.zeros((h_lat.shape[0], HG_HEADS, HG_DK, HG_DV), F32)
    oc_f, s_f = hgrn_chunk_scan(qc, kc0, vc, gc0, s0)
    ol_f, _ = hgrn_chunk_scan(ql, kl0, vl, gl0, s_f)
    oc_b, s_b = hgrn_chunk_scan(flip(qc), flip(kc1), flip(vc), flip(gc1), s0)
    ol_b, _ = hgrn_chunk_scan(flip(ql), flip(kl1), flip(vl), flip(gl1), s_b)
    y_lat = readout(ol_f + flip(ol_b), h_lat)
    y_ctx = readout(oc_f + flip(oc_b), h_ctx) if ctx_out else None
    return y_ctx, y_lat


def ssd_chunked(xdt, da, bm, cm, s0):
    b, l = xdt.shape[:2]
    nc = l // M_CHUNK
    X = xdt.reshape(b, nc, M_CHUNK, M_GROUPS, M_HPG, M_HEADDIM)
    A = jnp.cumsum(da.reshape(b, nc, M_CHUNK, M_GROUPS, M_HPG), axis=2)
    Bc = bm.reshape(b, nc, M_CHUNK, M_GROUPS, M_DSTATE)
    Cc = cm.reshape(b, nc, M_CHUNK, M_GROUPS, M_DSTATE)
    tri = jnp.tril(jnp.ones((M_CHUNK, M_CHUNK), bool))[:, :, None, None]
    seg = jnp.exp(jnp.where(tri, A[:, :, :, None] - A[:, :, None, :], -jnp.inf))
    cb = jnp.einsum('bclgn,bcsgn->bclsg', Cc, Bc)
    y_diag = jnp.einsum('bclsgj,bcsgjp->bclgjp', cb[..., None] * seg, X)
    a_end = A[:, :, -1]
    states = jnp.einsum('bclgn,bclgjp->bcgjpn', Bc, X * jnp.exp(a_end[:, :, None] - A)[..., None])

    def step(s, inp):
        st, dec = inp
        return s * dec[..., None, None] + st, s

    s_fin, s_prev = lax.scan(step, s0, (jnp.moveaxis(states, 1, 0), jnp.moveaxis(jnp.exp(a_end), 1, 0)))
    s_prev = jnp.moveaxis(s_prev, 0, 1)
    y_off = jnp.einsum('bclgn,bcgjpn->bclgjp', Cc, s_prev) * jnp.exp(A)[..., None]
    return (y_diag + y_off).reshape(b, l, M_GROUPS, M_HPG, M_HEADDIM), s_fin


def ssd_mixer(h_ctx, h_lat, w_in, conv_w, conv_b, dt_bias, a_log, d_skip, norm_g, w_out, ctx_out):
    def project(h):
        b, l, _ = h.shape
        zxbcdt = h @ w_in
        z = zxbcdt[..., :M_DINNER]
        xbc = jax.nn.silu(dwconv1d(zxbcdt[..., M_DINNER:M_DINNER + M_CONV_DIM], conv_w) + conv_b)
        dt = zxbcdt[..., M_DINNER + M_CONV_DIM:].astype(F32).reshape(b, l, 2, M_GROUPS, M_HPG)
        xs = xbc[..., :M_DINNER].reshape(b, l, M_GROUPS, M_HPG, M_HEADDIM)
        bm = xbc[..., M_DINNER:M_DINNER + M_GROUPS * M_DSTATE].reshape(b, l, M_GROUPS, M_DSTATE)
        cm = xbc[..., M_DINNER + M_GROUPS * M_DSTATE:].reshape(b, l, M_GROUPS, M_DSTATE)
        return z, xs, bm, cm, dt

    a = -jnp.exp(a_log.astype(F32)).reshape(2, M_GROUPS, M_HPG)
    dtb = dt_bias.astype(F32).reshape(2, M_GROUPS, M_HPG)

    def scan_dir(d, xs, bm, cm, dt_raw, s0):
        dt = jax.nn.softplus(dt_raw[:, :, d] + dtb[d])
        return ssd_chunked(xs * dt[..., None], dt * a[d], bm, cm, s0)

    def flip(t):
        return jnp.flip(t, axis=1)

    def readout(y, xs, z):
        b, l = z.shape[:2]
        y = (y + d_skip.reshape(M_GROUPS, M_HPG)[..., None] * xs).reshape(b, l, M_DINNER)
        y = rms_norm((y * jax.nn.silu(z)).reshape(b, l, M_GROUPS, -1)).reshape(b, l, M_DINNER) * norm_g
        return y.astype(z.dtype) @ w_out

    zc, xc, bc, cc, dtc = project(h_ctx)
    zl, xl, bl, cl, dtl = project(h_lat)
    s0 = jnp.zeros((h_lat.shape[0], M_GROUPS, M_HPG, M_HEADDIM, M_DSTATE), F32)
    yc_f, s_f = scan_dir(0, xc, bc, cc, dtc, s0)
    yl_f, _ = scan_dir(0, xl, bl, cl, dtl, s_f)
    yc_b, s_b = scan_dir(1, flip(xc), flip(bc), flip(cc), flip(dtc), s0)
    yl_b, _ = scan_dir(1, flip(xl), flip(bl), flip(cl), flip(dtl), s_b)
    y_lat = readout(yl_f + flip(yl_b), xl, zl)
    y_ctx = readout(yc_f + flip(yc_b), xc, zc) if ctx_out else None
    return y_ctx, y_lat


def peer_ffn(h, w_q, keys, u, v):
    b, l, _ = h.shape
    t = h.reshape(b * l, D_MODEL)
    q = (t @ w_q).reshape(-1, PEER_HEADS, 2, PEER_DK // 2)
    s = jnp.einsum('thpd,hpnd->thpn', q, keys).astype(F32)
    sv, si = lax.top_k(s, PEER_TOPK)
    cand_s = (sv[:, :, 0, :, None] + sv[:, :, 1, None, :]).reshape(-1, PEER_HEADS, PEER_TOPK * PEER_TOPK)
    cand_i = (si[:, :, 0, :, None] * PEER_NKEYS + si[:, :, 1, None, :]).reshape(-1, PEER_HEADS, PEER_TOPK * PEER_TOPK)
    top_s, pos = lax.top_k(cand_s, PEER_TOPK)
    idx = jnp.take_along_axis(cand_i, pos, axis=-1).reshape(-1, PEER_HEADS * PEER_TOPK)
    gate = jax.nn.softmax(top_s, axis=-1).reshape(-1, PEER_HEADS * PEER_TOPK)
    nb = t.shape[0] // PEER_BLOCK

    def block(args):
        tb, ib, gb = args
        act = jax.nn.gelu(jnp.einsum('tkd,td->tk', u[ib], tb).astype(F32), approximate=False)
        return jnp.einsum('tk,tkd->td', (gb * act).astype(tb.dtype), v[ib])

    out = lax.map(block, (t.reshape(nb, PEER_BLOCK, D_MODEL), idx.reshape(nb, PEER_BLOCK, -1),
                          gate.reshape(nb, PEER_BLOCK, -1)))
    return out.reshape(b, l, D_MODEL)


def setup_inputs(seed: int = 0) -> dict:
    key = jax.random.key(seed)
    ks = iter(jax.random.split(key, 48))

    def nrm(shape, scale):
        return jax.random.normal(next(ks), shape, jnp.float32) * scale

    D = D_MODEL
    dt0 = jnp.exp(jax.random.uniform(next(ks), (N_SSD_LAYERS, 2, M_HEADS), jnp.float32,
                                     math.log(1e-3), math.log(1e-1)))
    return {
        'x': nrm((BATCH, SEQ, D), 1.0),
        'c': nrm((BATCH, D), 1.0),
        'ctx': nrm((BATCH, CTX_LEN, D), 1.0),
        'c_ctx': nrm((D,), 1.0),
        'mod_w': nrm((DEPTH, D, 6 * D), D ** -0.5),
        'mod_b': nrm((DEPTH, 6 * D), 0.02),
        'ln_g': 1.0 + nrm((DEPTH, 2, D), 0.02),
        'ln_b': nrm((DEPTH, 2, D), 0.02),
        'conv_w_in': nrm((N_CONV_LAYERS, D, 2 * D), D ** -0.5),
        'conv_w_dw': nrm((N_CONV_LAYERS, CONV_K, D), CONV_K ** -0.5),
        'conv_ln_g': 1.0 + nrm((N_CONV_LAYERS, D), 0.02),
        'conv_ln_b': nrm((N_CONV_LAYERS, D), 0.02),
        'conv_w_out': nrm((N_CONV_LAYERS, D, D), DN_BETA * D ** -0.5),
        'hg_w_q': nrm((N_HGRN_LAYERS, D, HG_KD), D ** -0.5),
        'hg_w_f': nrm((N_HGRN_LAYERS, 2, D, HG_KD), D ** -0.5),
        'hg_w_i': nrm((N_HGRN_LAYERS, D, D), D ** -0.5),
        'hg_w_g': nrm((N_HGRN_LAYERS, D, D), D ** -0.5),
        'hg_norm_g': 1.0 + nrm((N_HGRN_LAYERS, D), 0.02),
        'hg_w_o': nrm((N_HGRN_LAYERS, D, D), DN_BETA * D ** -0.5),
        'hg_lb_logits': nrm((2, DEPTH, HG_KD), 0.1),
        'm_w_in': nrm((N_SSD_LAYERS, D, M_IN_DIM), D ** -0.5),
        'm_conv_w': nrm((N_SSD_LAYERS, M_CONV, M_CONV_DIM), M_CONV ** -0.5),
        'm_conv_b': nrm((N_SSD_LAYERS, M_CONV_DIM), 0.02),
        'm_dt_bias': dt0 + jnp.log(-jnp.expm1(-dt0)),
        'm_A_log': jnp.log(jax.random.uniform(next(ks), (N_SSD_LAYERS, 2, M_HEADS), jnp.float32, 1.0, 16.0)),
        'm_D': 1.0 + nrm((N_SSD_LAYERS, M_HEADS), 0.02),
        'm_norm_g': 1.0 + nrm((N_SSD_LAYERS, M_DINNER), 0.02),
        'm_w_out': nrm((N_SSD_LAYERS, M_DINNER, D), DN_BETA * M_DINNER ** -0.5),
        'peer_w_q': nrm((DEPTH, D, PEER_HEADS * PEER_DK), D ** -0.5),
        'peer_keys': nrm((DEPTH, PEER_HEADS, 2, PEER_NKEYS, PEER_DK // 2), (PEER_DK // 2) ** -0.5),
        'peer_u': nrm((DEPTH, PEER_EXPERTS, D), D ** -0.5),
        'peer_v': nrm((DEPTH, PEER_EXPERTS, D), DN_BETA * PEER_HEADS ** -0.5),
    }


def reference(x, c, ctx, c_ctx, mod_w, mod_b, ln_g, ln_b, conv_w_in, conv_w_dw, conv_ln_g, conv_ln_b,
              conv_w_out, hg_w_q, hg_w_f, hg_w_i, hg_w_g, hg_norm_g, hg_w_o, hg_lb_logits, m_w_in, m_conv_w,
              m_conv_b, m_dt_bias, m_A_log, m_D, m_norm_g, m_w_out, peer_w_q, peer_keys, peer_u, peer_v):
    reads_ctx = (False, True, True)
    last_ctx = max([i for i in range(DEPTH) if reads_ctx[i % N_MIXERS]], default=-1)
    lb_all = jnp.cumsum(jax.nn.softmax(hg_lb_logits.astype(F32), axis=1), axis=1)
    lb_all = lb_all - lb_all[:, :1]
    silu_c = jax.nn.silu(c)
    silu_cc = jax.nn.silu(c_ctx)
    for i in range(DEPTH):
        kind = i % N_MIXERS
        j = i // N_MIXERS
        run_ctx = i <= last_ctx
        upd_ctx = i < last_ctx
        m_lat = jnp.split((silu_c @ mod_w[i] + mod_b[i])[:, None, :], 6, axis=-1)
        m_ctx = jnp.split(silu_cc @ mod_w[i] + mod_b[i], 6, axis=-1)
        h_lat = x * (1.0 + m_lat[1]) + m_lat[0]
        h_ctx = ctx * (1.0 + m_ctx[1]) + m_ctx[0] if run_ctx else None
        if kind == 0:
            cw = (conv_w_in[j], conv_w_dw[j], conv_ln_g[j], conv_ln_b[j], conv_w_out[j])
            y_lat = conv_mixer(h_lat, *cw, on_grid=True)
            y_ctx = conv_mixer(h_ctx, *cw, on_grid=False) if upd_ctx else None
        elif kind == 1:
            y_ctx, y_lat = hgrn_mixer(h_ctx, h_lat, hg_w_q[j], hg_w_f[j], hg_w_i[j], hg_w_g[j], hg_norm_g[j],
                                      hg_w_o[j], lb_all[:, i], upd_ctx)
        else:
            y_ctx, y_lat = ssd_mixer(h_ctx, h_lat, m_w_in[j], m_conv_w[j], m_conv_b[j], m_dt_bias[j], m_A_log[j],
                                     m_D[j], m_norm_g[j], m_w_out[j], upd_ctx)
        pw = (peer_w_q[i], peer_keys[i], peer_u[i], peer_v[i])
        x = layer_norm(DN_ALPHA * x + m_lat[2] * y_lat, ln_g[i, 0], ln_b[i, 0])
        x = layer_norm(DN_ALPHA * x + m_lat[5] * peer_ffn(x * (1.0 + m_lat[4]) + m_lat[3], *pw),
                       ln_g[i, 1], ln_b[i, 1])
        if upd_ctx:
            ctx = layer_norm(DN_ALPHA * ctx + m_ctx[2] * y_ctx, ln_g[i, 0], ln_b[i, 0])
            ctx = layer_norm(DN_ALPHA * ctx + m_ctx[5] * peer_ffn(ctx * (1.0 + m_ctx[4]) + m_ctx[3], *pw),
                             ln_g[i, 1], ln_b[i, 1])
    return x
```

```python
import numpy as np
from contextlib import ExitStack
import concourse.bass as bass
import concourse.mybir as mybir
from concourse.bass_utils import run_bass_kernel_spmd

F32 = mybir.dt.float32
BF16 = mybir.dt.bfloat16
U32 = mybir.dt.uint32
ALU = mybir.AluOpType
AF = mybir.ActivationFunctionType

D = 2048
KC = 16
TL = 4096
TC = 256
T = TL + TC
DEPTH = 4
DN_ALPHA = (2 * DEPTH) ** 0.25
LN_EPS = 1e-5
NDMA_SEM = 6
N_CORES = 4
import os
P1_STOP = int(os.environ.get('P1_STOP', '99'))
NE = 16384


class Buf:
    def __init__(self, t, name):
        self.t = t
        self.name = name
        self.w = None
        self.r = {}

    def __getitem__(self, idx):
        return self.t[idx]


class K:
    def __init__(self, nc, es):
        self.nc = nc
        self.es = es
        self.scopes = [es]
        self.eng = {"pe": nc.tensor, "dve": nc.vector, "act": nc.scalar, "pool": nc.gpsimd, "sp": nc.sync}
        self.sem = {}
        self.cnt = {}
        for e in ["pe", "dve", "act", "pool"]:
            self.sem[e] = es.enter_context(nc.semaphore("s_" + e))
            self.cnt[e] = 0
        self.dq = {}
        for q in ["sp", "pool", "act"]:
            for i in range(NDMA_SEM):
                key = f"d_{q}{i}"
                self.sem[key] = es.enter_context(nc.semaphore(key))
                self.cnt[key] = 0
            self.dq[q] = 0
        self.seen = {}
        self.n_ins = 0

    def _uid(self, name):
        self.uid = getattr(self, "uid", 0) + 1
        return f"{name}_{self.uid}"

    def sb(self, name, shape, dt=F32):
        name = self._uid(name)
        return Buf(self.scopes[-1].enter_context(self.nc.sbuf_tensor(name, list(shape), dt)), name)

    def ps(self, name, shape, dt=F32):
        name = self._uid(name)
        return Buf(self.scopes[-1].enter_context(self.nc.psum_tensor(name, list(shape), dt)), name)

    def dram(self, name, shape, dt=F32, kind="Internal"):
        return Buf(self.nc.dram_tensor(name, list(shape), dt, kind=kind), name)

    def _wait(self, engname, key, count):
        if count <= 0 or self.seen.get((engname, key), 0) >= count:
            return
        self.seen[(engname, key)] = count
        mult = 16 if key.startswith("d_") else 1
        self.eng[engname].wait_ge(self.sem[key], count * mult)

    def _deps(self, engname, reads, writes, ownkey):
        need = {}
        for b in reads:
            if b.w is not None:
                need[b.w[0]] = max(need.get(b.w[0], 0), b.w[1])
        for b in writes:
            if b.w is not None:
                need[b.w[0]] = max(need.get(b.w[0], 0), b.w[1])
            for kk, c in b.r.items():
                need[kk] = max(need.get(kk, 0), c)
        for kk, c in need.items():
            if kk == ownkey and engname == "pe":
                continue
            self._wait(engname, kk, c)

    def _mark(self, key, count, reads, writes):
        for b in reads:
            b.r[key] = max(b.r.get(key, 0), count)
        for b in writes:
            b.w = (key, count)
            b.r = {}

    def op(self, engname, emit, reads=(), writes=()):
        self._deps(engname, reads, writes, engname)
        ins = emit(self.eng[engname])
        self.cnt[engname] += 1
        ins.then_inc(self.sem[engname], 1)
        self._mark(engname, self.cnt[engname], reads, writes)
        self.n_ins += 1
        return ins

    def dma(self, q, out, in_, reads=(), writes=(), **kw):
        i = self.dq[q] % NDMA_SEM
        self.dq[q] += 1
        key = f"d_{q}{i}"
        self._wait(q, key, self.cnt[key])
        self._deps(q, reads, writes, None)
        ins = self.eng[q].dma_start(out=out, in_=in_, **kw)
        self.cnt[key] += 1
        ins.then_inc(self.sem[key], 16)
        self._mark(key, self.cnt[key], reads, writes)
        self.n_ins += 1
        return ins

    def barrier(self):
        for e in ["pe", "dve", "act", "pool", "sp"]:
            for key, c in self.cnt.items():
                self._wait(e, key, c)

    def open(self):
        s = ExitStack()
        self.scopes.append(s)
        return s

    def close(self):
        self.barrier()
        self.scopes.pop().close()


def colview(w2d, c0, n):
    if isinstance(w2d, Buf):
        w2d = w2d.t.ap()
    return w2d.rearrange("(c p) n -> p c n", p=128)[:, :, c0:c0 + n]


def build(layers_to_run, peer_on=True, mixer_on=True, p2_on=True, ntok_dbg=None):
    nc = bass.Bass("TRN2", target_bir_lowering=False)
    es = ExitStack()
    with es:
        k = K(nc, es)
        ein = lambda name, shape: k.dram(name, shape, F32, kind="ExternalInput")
        xin = ein("xin", [T, D])
        cT = ein("cT", [128, KC * 2])
        ident_d = ein("ident", [128, 128])
        mod_w = {l: ein(f"mod_w_{l}", [D, 6 * D]) for l in layers_to_run}
        mod_b = ein("mod_b", [DEPTH, 6 * D])
        ln_g = ein("ln_g", [DEPTH * 2, D])
        ln_b = ein("ln_b", [DEPTH * 2, D])
        conv_w_in = ein("conv_w_in", [2, D, 2 * D])
        conv_w_out = ein("conv_w_out", [2, D, D])
        conv_dwT = ein("conv_dwT", [2, 128, KC * 31])
        conv_lngT = ein("conv_lngT", [2, 128, KC])
        conv_lnbT = ein("conv_lnbT", [2, 128, KC])
        peer_w_q = ein("peer_w_q", [DEPTH, D, 1024])
        peer_keysT = ein("peer_keysT", [DEPTH, 128, 2048])
        peer_uT = {l: ein(f"peer_uT_{l}", [D, NE]) for l in layers_to_run}
        peer_v = {l: ein(f"peer_v_{l}", [NE, D]) for l in layers_to_run}
        hconst = ein("hconst", [128, 258])
        hg_w = {nm: ein("hg_" + nm, [D, D]) for nm in ("w_q", "w_i", "w_g", "w_f0", "w_f1", "w_o")} if 1 in layers_to_run else {}
        hg_norm_g = ein("hg_norm_g", [1, D])
        hg_lb_logits = ein("hg_lb_logits", [2 * DEPTH, D])
        DI = 4096
        sconst = ein("sconst", [128, 512])
        if 2 in layers_to_run:
            m_w_in = ein("m_w_in", [D, 10368]); m_w_out = ein("m_w_out", [DI, D])
        m_convT = ein("m_convT", [128, 48 * 5]); m_convbT = ein("m_convbT", [128, 48])
        m_dt_bias = ein("m_dt_bias", [2, 64]); m_A_log = ein("m_A_log", [2, 64]); m_D = ein("m_D", [1, 64]); m_norm_g = ein("m_norm_g", [1, DI])
        XBCT = k.dram("XBCT", [6144, T], F32); ZSd = k.dram("ZSd", [TL, DI], F32); DTd = k.dram("DTd", [T, 128], F32)
        XTOK = k.dram("XTOK", [T, DI], F32); BTOK = k.dram("BTOK", [T, 1024], BF16)
        BTd = k.dram("BTd", [1024, T], BF16); CTd = k.dram("CTd", [1024, T], BF16)
        Yd_ = k.dram("Ysc", [TL, DI], F32); GTs = k.dram("GTs", [DI, TL], BF16); YO = k.dram("YO", [TL, D], F32)
        out = k.dram("out", [TL, D], F32, kind="ExternalOutput")
        out_ctx = k.dram("out_ctx", [TC, D], F32, kind="ExternalOutput")
        Qd = k.dram("Qd", [T, D], F32); GTd = k.dram("GTd", [T, D], F32); Vd = k.dram("Vd", [T, D], BF16)
        LFd = [k.dram(f"LFd{d_}", [T, D], F32) for d_ in range(2)]
        KDd = [k.dram(f"KDd{d_}", [T, D], F32) for d_ in range(2)]
        Od = k.dram("Od", [T, D], F32)
        Gd = k.dram("Gd", [T, NE], BF16)
        dbgG = k.dram("dbgG", [128, NE], BF16, kind="ExternalOutput") if ntok_dbg else None

        X = k.dram("X", [T, D], F32)
        HT = k.dram("HT", [D, T], BF16)
        U = k.dram("U", [D, T], F32)
        UC = k.dram("UC", [D, T], F32)
        MR = k.dram("MR", [DEPTH, 2, 6 * D], F32)

        ident = k.sb("identsb", [128, 128], F32)
        ones = k.sb("ones", [128, 128], F32)
        k.dma("sp", ident[:, :], ident_d[:, :], writes=[ident])
        k.op("dve", lambda e: e.memset(ones[:, :], 1.0), writes=[ones])
        mcol = k.sb("mcol", [128, 2 * 6 * KC], F32)
        eps_t = k.sb("eps_t", [128, 1], F32)
        k.op("dve", lambda e: e.memset(eps_t[:, :], LN_EPS), writes=[eps_t])

        for i in range(4):
            r0 = i * (T // 4)
            k.dma(["sp", "pool"][i % 2], X[r0:r0 + T // 4, :], xin[r0:r0 + T // 4, :])
        k.open()
        sT = k.sb("sT", [128, KC * 2], F32)
        k.dma("sp", sT[:, :], cT[:, :], writes=[sT])
        k.op("act", lambda e: e.activation(out=sT[:, :], in_=sT[:, :], func=AF.Silu), reads=[sT], writes=[sT])
        wmb = [k.sb(f"wmb{i}", [128, KC * 512], F32) for i in range(2)]
        bb = [k.sb(f"bb{i}", [2, 512], F32) for i in range(2)]
        mo = [k.sb(f"mo{i}", [2, 512], F32) for i in range(2)]
        mps = [k.ps(f"mps{i}", [2, 512], F32) for i in range(2)]
        it = 0
        for l in layers_to_run:
            for nb in range(24):
                w = wmb[it % 2]; b_ = bb[it % 2]; o_ = mo[it % 2]; p_ = mps[it % 2]
                wv = w.t[:, :].rearrange("p (c n) -> p c n", c=KC)
                k.dma(["sp", "pool"][it % 2], wv, colview(mod_w[l], nb * 512, 512), writes=[w])
                k.dma("act", b_[:, :], mod_b[l:l + 1, nb * 512:(nb + 1) * 512].to_broadcast([2, 512]), writes=[b_])
                for kc in range(KC):
                    k.op("pe", lambda e, kc=kc: e.matmul(p_[:, :], lhsT=sT[:, kc * 2:kc * 2 + 2], rhs=wv[:, kc, :],
                                                         start=(kc == 0), stop=(kc == KC - 1)),
                         reads=[sT, w], writes=[p_])
                k.op("dve", lambda e: e.tensor_tensor(out=o_[:, :], in0=p_[:, :], in1=b_[:, :], op=ALU.add),
                     reads=[p_, b_], writes=[o_])
                k.dma("act", MR[l, :, nb * 512:(nb + 1) * 512], o_[:, :], reads=[o_])
                it += 1
        k.close()

        def load_mcol(l):
            k.barrier()
            i = 0
            for r in range(2):
                for s in range(6):
                    src = MR[l, r, s * D:(s + 1) * D].rearrange("(c p) -> p c", p=128)
                    k.dma(["sp", "pool", "act"][i % 3], mcol[:, (r * 6 + s) * KC:(r * 6 + s + 1) * KC], src, writes=[mcol],
                          allow_slow_non_contiguous=True)
                    i += 1
            for r in range(2):
                for s in (1, 4):
                    sl = mcol[:, (r * 6 + s) * KC:(r * 6 + s + 1) * KC]
                    k.op("dve", lambda e, sl=sl: e.tensor_scalar(out=sl, in0=sl, scalar1=1.0, scalar2=None, op0=ALU.add),
                         reads=[mcol], writes=[mcol])

        def mc(r, s, kc):
            c = (r * 6 + s) * KC + kc
            return mcol[:, c:c + 1]

        def build_HT(ntok, seg_shift, seg_scale):
            k.open()
            xt = [k.sb(f"xt{i}", [128, D], F32) for i in range(2)]
            ht = [k.sb(f"ht{i}", [128, KC * 128], BF16) for i in range(2)]
            pt = [k.ps(f"pt{i}", [128, 512], F32) for i in range(4)]
            HTv = HT.t.ap().rearrange("(c p) t -> p c t", p=128)
            for tt in range(ntok // 128):
                r = 0 if tt < TL // 128 else 1
                x_ = xt[tt % 2]; h_ = ht[tt % 2]
                k.dma(["sp", "pool"][tt % 2], x_[:, :], X[tt * 128:(tt + 1) * 128, :], writes=[x_])
                for kc in range(KC):
                    p_ = pt[kc % 4]
                    k.op("pe", lambda e, kc=kc, p_=p_: e.transpose(out=p_[:, 0:128], in_=x_[:, kc * 128:(kc + 1) * 128],
                                                                   identity=ident[:, :]),
                         reads=[x_, ident], writes=[p_])
                    k.op("act", lambda e, kc=kc, p_=p_: e.activation(out=h_[:, kc * 128:(kc + 1) * 128], in_=p_[:, 0:128],
                                                                     func=AF.Identity, scale=mc(r, seg_scale, kc),
                                                                     bias=mc(r, seg_shift, kc)),
                         reads=[p_, mcol], writes=[h_])
                k.dma("act", HTv[:, :, tt * 128:(tt + 1) * 128], h_.t[:, :].rearrange("p (c t) -> p c t", c=KC), reads=[h_])
            k.close()

        def residual_ln(y_ps, x_, m_b, g_b, b_b, tmp, stats, mv, rstd, yget=None):
            if yget is None:
                yget = lambda q: (y_ps[q][:, :], y_ps[q])
            for q in range(4):
                sl = slice(q * 512, (q + 1) * 512)
                yap, ybuf = yget(q)
                k.op("dve", lambda e, sl=sl, yap=yap: e.tensor_tensor(out=tmp[:, sl], in0=yap, in1=m_b[:, sl], op=ALU.mult),
                     reads=[ybuf, m_b], writes=[tmp])
            k.op("dve", lambda e: e.scalar_tensor_tensor(out=tmp[:, :], in0=x_[:, :], scalar=float(DN_ALPHA), in1=tmp[:, :],
                                                         op0=ALU.mult, op1=ALU.add), reads=[x_, tmp], writes=[tmp])
            for q in range(4):
                k.op("dve", lambda e, q=q: e.bn_stats(out=stats[:, q * 6:(q + 1) * 6], in_=tmp[:, q * 512:(q + 1) * 512]),
                     reads=[tmp], writes=[stats])
            k.op("dve", lambda e: e.bn_aggr(out=mv[:, :], in_=stats[:, :].rearrange("p (a b) -> p a b", b=6)), reads=[stats], writes=[mv])
            k.op("act", lambda e: e.activation(out=rstd[:, :], in_=mv[:, 1:2], func=AF.Sqrt, bias=eps_t[:, 0:1], scale=1.0),
                 reads=[mv, eps_t], writes=[rstd])
            k.op("dve", lambda e: e.reciprocal(out=rstd[:, :], in_=rstd[:, :]), reads=[rstd], writes=[rstd])
            k.op("dve", lambda e: e.tensor_scalar(out=tmp[:, :], in0=tmp[:, :], scalar1=mv[:, 0:1], scalar2=rstd[:, 0:1],
                                                  op0=ALU.subtract, op1=ALU.mult), reads=[tmp, mv, rstd], writes=[tmp])
            k.op("pool", lambda e: e.tensor_tensor(out=tmp[:, :], in0=tmp[:, :], in1=g_b[:, :], op=ALU.mult), reads=[tmp, g_b], writes=[tmp])
            k.op("pool", lambda e: e.tensor_tensor(out=x_[:, :], in0=tmp[:, :], in1=b_b[:, :], op=ALU.add), reads=[tmp, b_b], writes=[x_])

        def load_bcast(dst, src_row):
            k.dma("act", dst[:, :], src_row.to_broadcast([128, D]), writes=[dst])

        def conv_layer(l, j, with_ctx):
            ntok = T if with_ctx else TL
            load_mcol(l)
            build_HT(ntok, 0, 1)
            k.open()
            w1 = [k.sb(f"w1_{i}", [128, KC * 512], BF16) for i in range(2)]
            w2 = [k.sb(f"w2_{i}", [128, KC * 512], BF16) for i in range(2)]
            hT = [k.sb(f"hT{i}", [128, KC * 512], BF16) for i in range(2)]
            pa = [k.ps(f"pa{i}", [128, 512], F32) for i in range(2)]
            pb = [k.ps(f"pb{i}", [128, 512], F32) for i in range(2)]
            sg = [k.sb(f"sg{i}", [128, 512], F32) for i in range(2)]
            ub = [k.sb(f"ub{i}", [128, 512], F32) for i in range(2)]
            HTv = HT.t.ap().rearrange("(c p) t -> p c t", p=128)
            tiles = [(t0, min(512, ntok - t0)) for t0 in range(0, ntok, 512)]
            it = 0
            for cb in range(4):
                w1_ = w1[cb % 2]; w2_ = w2[cb % 2]
                w1v = w1_.t[:, :].rearrange("p (c n) -> p c n", c=KC)
                w2v = w2_.t[:, :].rearrange("p (c n) -> p c n", c=KC)
                k.dma("pool", w1v, colview(conv_w_in[j], cb * 512, 512), writes=[w1_])
                k.dma("pool", w2v, colview(conv_w_in[j], D + cb * 512, 512), writes=[w2_])
                for (t0, n) in tiles:
                    h_ = hT[it % 2]
                    hv = h_.t[:, :].rearrange("p (c t) -> p c t", c=KC)
                    k.dma("sp", hv[:, :, 0:n], HTv[:, :, t0:t0 + n], writes=[h_])
                    for jj in range(4):
                        pa_ = pa[jj % 2]; pb_ = pb[jj % 2]; sg_ = sg[jj % 2]; ub_ = ub[jj % 2]
                        for kc in range(KC):
                            k.op("pe", lambda e, kc=kc: e.matmul(pa_[:, 0:n], lhsT=w1v[:, kc, jj * 128:(jj + 1) * 128], rhs=hv[:, kc, 0:n],
                                                                 start=(kc == 0), stop=(kc == KC - 1)), reads=[w1_, h_], writes=[pa_])
                        for kc in range(KC):
                            k.op("pe", lambda e, kc=kc: e.matmul(pb_[:, 0:n], lhsT=w2v[:, kc, jj * 128:(jj + 1) * 128], rhs=hv[:, kc, 0:n],
                                                                 start=(kc == 0), stop=(kc == KC - 1)), reads=[w2_, h_], writes=[pb_])
                        k.op("act", lambda e: e.activation(out=sg_[:, 0:n], in_=pb_[:, 0:n], func=AF.Sigmoid), reads=[pb_], writes=[sg_])
                        k.op("dve", lambda e: e.tensor_tensor(out=ub_[:, 0:n], in0=pa_[:, 0:n], in1=sg_[:, 0:n], op=ALU.mult),
                             reads=[pa_, sg_], writes=[ub_])
                        ch0 = (cb * 4 + jj) * 128
                        k.dma("act", U[ch0:ch0 + 128, t0:t0 + n], ub_[:, 0:n], reads=[ub_])
                    it += 1
            k.close()
            k.open()
            dw = k.sb("dw", [128, KC * 31], F32)
            k.dma("sp", dw[:, :], conv_dwT[j], writes=[dw])
            uin = [k.sb(f"uin{i}", [128, T], F32) for i in range(2)]
            acc = [k.sb(f"acc{i}", [128, T], F32) for i in range(2)]
            for c in range(KC):
                u_ = uin[c % 2]; a_ = acc[c % 2]
                eng = "dve"
                k.dma(["sp", "pool"][c % 2], u_[:, 0:ntok], U[c * 128:(c + 1) * 128, 0:ntok], writes=[u_])
                order = [15] + [kk for kk in range(31) if kk != 15]
                for kk in order:
                    o = kk - 15
                    wk = dw[:, c * 31 + kk:c * 31 + kk + 1]
                    lo_o, hi_o = max(0, -o), 64 - max(0, o)
                    lo_i, hi_i = max(0, o), 64 - max(0, -o)
                    regions = []
                    if c < KC // 2:
                        av = a_.t[:, 0:TL].rearrange("p (r c) -> p r c", c=64)
                        uv = u_.t[:, 0:TL].rearrange("p (r c) -> p r c", c=64)
                        regions.append((av[:, :, lo_o:hi_o], uv[:, :, lo_i:hi_i]))
                    else:
                        regions.append((a_[:, lo_o * 64:hi_o * 64], u_[:, lo_i * 64:hi_i * 64]))
                    if with_ctx:
                        lo_oc, hi_oc = max(0, -o), TC - max(0, o)
                        lo_ic, hi_ic = max(0, o), TC - max(0, -o)
                        regions.append((a_[:, TL + lo_oc:TL + hi_oc], u_[:, TL + lo_ic:TL + hi_ic]))
                    for (oa, ia) in regions:
                        if kk == 15:
                            k.op(eng, lambda e, oa=oa, ia=ia: e.tensor_scalar(out=oa, in0=ia, scalar1=wk, scalar2=None, op0=ALU.mult),
                                 reads=[u_, dw], writes=[a_])
                        else:
                            k.op(eng, lambda e, oa=oa, ia=ia: e.scalar_tensor_tensor(out=oa, in0=ia, scalar=wk, in1=oa,
                                                                                    op0=ALU.mult, op1=ALU.add),
                                 reads=[u_, dw, a_], writes=[a_])
                k.dma("act", UC[c * 128:(c + 1) * 128, 0:ntok], a_[:, 0:ntok], reads=[a_])
            k.close()
            k.open()
            wo = k.sb("wo", [128, KC * D], BF16)
            wov = wo.t[:, :].rearrange("p (c n) -> p c n", c=KC)
            for q in range(4):
                k.dma("pool", wov[:, :, q * 512:(q + 1) * 512], colview(conv_w_out[j], q * 512, 512), writes=[wo])
            lg = k.sb("lg", [128, KC], F32); lb = k.sb("lb", [128, KC], F32)
            k.dma("sp", lg[:, :], conv_lngT[j], writes=[lg])
            k.dma("sp", lb[:, :], conv_lnbT[j], writes=[lb])
            m2b = [k.sb(f"m2b{r}", [128, D], F32) for r in range(2)]
            load_bcast(m2b[0], MR[l, 0:1, 2 * D:3 * D])
            load_bcast(m2b[1], MR[l, 1:2, 2 * D:3 * D])
            g_b = k.sb("g_b", [128, D], F32); b_b = k.sb("b_b", [128, D], F32)
            load_bcast(g_b, ln_g[2 * l:2 * l + 1, :]); load_bcast(b_b, ln_b[2 * l:2 * l + 1, :])
            uc = [k.sb(f"uc{i}", [128, KC * 256], F32) for i in range(2)]
            sq = k.sb("sq", [128, 256], F32)
            zT = [k.sb(f"zT{i}", [128, KC * 256], BF16) for i in range(2)]
            ps_s = k.ps("ps_s", [128, 256], F32); ps_q = k.ps("ps_q", [128, 256], F32)
            mean = k.sb("mean", [128, 256], F32); rs = k.sb("rs", [128, 256], F32); t1 = k.sb("t1", [128, 256], F32)
            yps = [k.ps(f"yps{q}", [128, 512], F32) for q in range(4)]
            xt = [k.sb(f"xd{i}", [128, D], F32) for i in range(2)]
            tmp = k.sb("tmpd", [128, D], F32)
            stats = k.sb("stats", [128, 24], F32); mv = k.sb("mv", [128, 2], F32); rstd = k.sb("rstd", [128, 1], F32)
            UCv = UC.t.ap().rearrange("(c p) t -> p c t", p=128)
            for ti, t0 in enumerate(range(0, ntok, 256)):
                u_ = uc[ti % 2]; z_ = zT[ti % 2]
                uv = u_.t[:, :].rearrange("p (c t) -> p c t", c=KC)
                zv = z_.t[:, :].rearrange("p (c t) -> p c t", c=KC)
                k.dma(["sp", "pool"][ti % 2], uv, UCv[:, :, t0:t0 + 256], writes=[u_])
                for kc in range(KC):
                    k.op("pe", lambda e, kc=kc: e.matmul(ps_s[:, :], lhsT=ones[:, :], rhs=uv[:, kc, :], start=(kc == 0), stop=(kc == KC - 1)),
                         reads=[ones, u_], writes=[ps_s])
                for kc in range(KC):
                    k.op("act", lambda e, kc=kc: e.activation(out=sq[:, :], in_=uv[:, kc, :], func=AF.Square), reads=[u_], writes=[sq])
                    k.op("pe", lambda e, kc=kc: e.matmul(ps_q[:, :], lhsT=ones[:, :], rhs=sq[:, :], start=(kc == 0), stop=(kc == KC - 1)),
                         reads=[ones, sq], writes=[ps_q])
                k.op("dve", lambda e: e.tensor_scalar(out=mean[:, :], in0=ps_s[:, :], scalar1=1.0 / D, scalar2=None, op0=ALU.mult),
                     reads=[ps_s], writes=[mean])
                k.op("dve", lambda e: e.tensor_tensor(out=t1[:, :], in0=mean[:, :], in1=mean[:, :], op=ALU.mult), reads=[mean], writes=[t1])
                k.op("dve", lambda e: e.scalar_tensor_tensor(out=rs[:, :], in0=ps_q[:, :], scalar=1.0 / D, in1=t1[:, :],
                                                             op0=ALU.mult, op1=ALU.subtract), reads=[ps_q, t1], writes=[rs])
                k.op("act", lambda e: e.activation(out=rs[:, :], in_=rs[:, :], func=AF.Sqrt, bias=eps_t[:, 0:1], scale=1.0),
                     reads=[rs, eps_t], writes=[rs])
                k.op("dve", lambda e: e.reciprocal(out=rs[:, :], in_=rs[:, :]), reads=[rs], writes=[rs])
                for kc in range(KC):
                    k.op("dve", lambda e, kc=kc: e.tensor_tensor(out=t1[:, :], in0=uv[:, kc, :], in1=mean[:, :], op=ALU.subtract),
                         reads=[u_, mean], writes=[t1])
                    k.op("pool", lambda e, kc=kc: e.tensor_tensor(out=t1[:, :], in0=t1[:, :], in1=rs[:, :], op=ALU.mult),
                         reads=[t1, rs], writes=[t1])
                    k.op("act", lambda e, kc=kc: e.activation(out=zv[:, kc, :], in_=t1[:, :], func=AF.Silu, scale=lg[:, kc:kc + 1],
                                                              bias=lb[:, kc:kc + 1]), reads=[t1, lg, lb], writes=[z_])
                for sub in range(2):
                    tok0 = t0 + sub * 128
                    r = 0 if tok0 < TL else 1
                    x_ = xt[sub]
                    k.dma("sp", x_[:, :], X[tok0:tok0 + 128, :], writes=[x_])
                    for q in range(4):
                        for kc in range(KC):
                            k.op("pe", lambda e, kc=kc, q=q: e.matmul(yps[q][:, :], lhsT=zv[:, kc, sub * 128:(sub + 1) * 128],
                                                                      rhs=wov[:, kc, q * 512:(q + 1) * 512],
                                                                      start=(kc == 0), stop=(kc == KC - 1)), reads=[z_, wo], writes=[yps[q]])
                    residual_ln(yps, x_, m2b[r], g_b, b_b, tmp, stats, mv, rstd)
                    k.dma("act", X[tok0:tok0 + 128, :], x_[:, :], reads=[x_])
            k.close()


        identb = k.sb("identb", [128, 128], BF16)
        k.op("dve", lambda e: e.tensor_copy(out=identb[:, :], in_=ident[:, :]), reads=[ident], writes=[identb])

        def peer_layer(l, with_ctx):
            ntok = T if with_ctx else TL
            if ntok_dbg:
                ntok = ntok_dbg
            build_HT(ntok, 3, 4)
            HTv = HT.t.ap().rearrange("(c p) t -> p c t", p=128)
            k.open()
            wq = k.sb("wq", [128, KC * 1024], BF16)
            wqv = wq.t[:, :].rearrange("p (c n) -> p c n", c=KC)
            for hf in range(2):
                k.dma("pool", wqv[:, :, hf * 512:(hf + 1) * 512], colview(peer_w_q[l], hf * 512, 512), writes=[wq])
            kT = k.sb("kT", [128, 2048], F32)
            k.dma("sp", kT[:, :], peer_keysT[l], writes=[kT])
            hts = [k.sb(f"pht{i}", [128, KC * 128], BF16) for i in range(2)]
            qT = k.sb("qT", [128, 8 * 128], F32)
            pq = [k.ps(f"pq{i}", [128, 128], F32) for i in range(2)]
            pss = [k.ps(f"pss{i}", [128, 512], F32) for i in range(4)]
            sb_s = k.sb("s", [128, 16 * 128], F32); sb_s2 = k.sb("s2", [128, 16 * 128], F32); sb_E = k.sb("E", [128, 16 * 128], F32)
            sv = sb_s.t[:, :].rearrange("p (g n) -> p g n", g=16)
            s2v = sb_s2.t[:, :].rearrange("p (g n) -> p g n", g=16)
            Ev = sb_E.t[:, :].rearrange("p (g n) -> p g n", g=16)
            mx = k.sb("mx", [128, 256], F32); mx3 = mx.t[:, :].rearrange("p (g a) -> p g a", g=16)
            negm = k.sb("negm", [128, 16], F32)
            cand = k.sb("cand", [128, 8 * 256], F32); cand2 = k.sb("cand2", [128, 8 * 256], F32)
            ts = k.sb("ts", [128, 128], F32); ts3 = ts.t[:, :].rearrange("p (h a) -> p h a", h=8)
            e16 = k.sb("e16", [128, 128], F32); e163 = e16.t[:, :].rearrange("p (h a) -> p h a", h=8)
            Z = k.sb("Z", [128, 8], F32); thr = k.sb("thr", [128, 8], F32)
            th = k.sb("th", [128, 8 * 128], F32); th3 = th.t[:, :].rearrange("p (h i) -> p h i", h=8)
            A1 = k.sb("A1", [128, 8 * 128], F32); A13 = A1.t[:, :].rearrange("p (h i) -> p h i", h=8)
            Wb = {e: k.sb("W" + e, [128, 2048], F32) for e in ("dve", "pool")}
            Gb = {e: k.sb("Gb" + e, [128, 2048], F32) for e in ("dve", "pool")}
            Gt = [k.sb(f"Gt{i}", [128, NE], BF16) for i in range(2)]
            B3 = [128, 16, 128]
            for tt in range(ntok // 128):
                h_ = hts[tt % 2]; hv = h_.t[:, :].rearrange("p (c t) -> p c t", c=KC)
                k.dma("sp", hv, HTv[:, :, tt * 128:(tt + 1) * 128], writes=[h_])
                for h in range(8):
                    p_ = pq[h % 2]
                    for kc in range(KC):
                        k.op("pe", lambda e, kc=kc: e.matmul(p_[:, :], lhsT=wqv[:, kc, h * 128:(h + 1) * 128], rhs=hv[:, kc, :],
                                                             start=(kc == 0), stop=(kc == KC - 1)), reads=[wq, h_], writes=[p_])
                    k.op("act", lambda e: e.activation(out=qT[:, h * 128:(h + 1) * 128], in_=p_[:, :], func=AF.Identity),
                         reads=[p_], writes=[qT])
                if P1_STOP <= 1:
                    continue
                for g in range(16):
                    h, p = g // 2, g % 2
                    ps_ = pss[g // 4]
                    k.op("pe", lambda e: e.matmul(ps_[:, (g % 4) * 128:(g % 4 + 1) * 128], lhsT=qT[:, h * 128:(h + 1) * 128],
                                                  rhs=kT[:, p * 1024 + h * 128:p * 1024 + (h + 1) * 128], start=True, stop=True),
                         reads=[qT, kT], writes=[ps_])
                for b4 in range(4):
                    k.op("act", lambda e: e.activation(out=sb_s[:, b4 * 512:(b4 + 1) * 512], in_=pss[b4][:, :], func=AF.Identity),
                         reads=[pss[b4]], writes=[sb_s])
                if P1_STOP <= 2:
                    continue
                for g in range(16):
                    k.op("dve", lambda e: e.max(out=mx3[:, g, 0:8], in_=sv[:, g, :]), reads=[sb_s], writes=[mx])
                    k.op("dve", lambda e: e.match_replace(out=s2v[:, g, :], in_to_replace=mx3[:, g, 0:8], in_values=sv[:, g, :],
                                                          imm_value=-1e30), reads=[mx, sb_s], writes=[sb_s2])
                    k.op("dve", lambda e: e.max(out=mx3[:, g, 8:16], in_=s2v[:, g, :]), reads=[sb_s2], writes=[mx])
                if P1_STOP <= 3:
                    continue
                k.op("dve", lambda e: e.tensor_scalar(out=negm[:, :], in0=mx3[:, :, 0], scalar1=-1.0, scalar2=None, op0=ALU.mult),
                     reads=[mx], writes=[negm])
                for g in range(16):
                    k.op("act", lambda e: e.activation(out=Ev[:, g, :], in_=sv[:, g, :], func=AF.Exp, bias=negm[:, g:g + 1], scale=1.0),
                         reads=[sb_s, negm], writes=[sb_E])
                if P1_STOP <= 4:
                    continue
                for h in range(8):
                    c3 = cand.t[:, h * 256:(h + 1) * 256].rearrange("p (a b) -> p a b", a=16)
                    k.op("dve", lambda e: e.tensor_tensor(out=c3, in0=mx3[:, 2 * h, :].unsqueeze(2).to_broadcast([128, 16, 16]),
                                                          in1=mx3[:, 2 * h + 1, :].unsqueeze(1).to_broadcast([128, 16, 16]), op=ALU.add),
                         reads=[mx], writes=[cand])
                    ch = cand[:, h * 256:(h + 1) * 256]; ch2 = cand2[:, h * 256:(h + 1) * 256]
                    k.op("dve", lambda e: e.max(out=ts3[:, h, 0:8], in_=ch), reads=[cand], writes=[ts])
                    k.op("dve", lambda e: e.match_replace(out=ch2, in_to_replace=ts3[:, h, 0:8], in_values=ch, imm_value=-1e30),
                         reads=[ts, cand], writes=[cand2])
                    k.op("dve", lambda e: e.max(out=ts3[:, h, 8:16], in_=ch2), reads=[cand2], writes=[ts])
                if P1_STOP <= 5:
                    continue
                k.op("dve", lambda e: e.tensor_tensor(out=e163, in0=ts3, in1=ts3[:, :, 0:1].to_broadcast([128, 8, 16]), op=ALU.subtract),
                     reads=[ts], writes=[e16])
                k.op("act", lambda e: e.activation(out=e16[:, :], in_=e16[:, :], func=AF.Exp), reads=[e16], writes=[e16])
                k.op("dve", lambda e: e.tensor_reduce(out=Z[:, :], in_=e163, axis=mybir.AxisListType.X, op=ALU.add), reads=[e16], writes=[Z])
                k.op("dve", lambda e: e.reciprocal(out=Z[:, :], in_=Z[:, :]), reads=[Z], writes=[Z])
                k.op("dve", lambda e: e.tensor_scalar(out=thr[:, :], in0=ts3[:, :, 15], scalar1=-2e-5, scalar2=None, op0=ALU.add),
                     reads=[ts], writes=[thr])
                for h in range(8):
                    k.op("dve", lambda e: e.tensor_scalar(out=th[:, h * 128:(h + 1) * 128], in0=sv[:, 2 * h, :], scalar1=-1.0,
                                                          scalar2=thr[:, h:h + 1], op0=ALU.mult, op1=ALU.add), reads=[sb_s, thr], writes=[th])
                    k.op("dve", lambda e: e.tensor_scalar(out=A1[:, h * 128:(h + 1) * 128], in0=Ev[:, 2 * h, :], scalar1=Z[:, h:h + 1],
                                                          scalar2=None, op0=ALU.mult), reads=[sb_E, Z], writes=[A1])
                if P1_STOP <= 6:
                    continue
                G_ = Gt[tt % 2]
                Gb_ = Gb["pool"]
                G3 = Gb_.t[:, :].rearrange("p (i j) -> p i j", i=16)
                wi = 0
                for ib in range(8):
                    Gout = G_.t[:, ib * 2048:(ib + 1) * 2048].rearrange("p (i j) -> p i j", i=16)
                    i0 = ib * 16
                    for h in range(8):
                        W_ = Wb[("dve", "pool")[wi % 2]]; wi += 1
                        W3 = W_.t[:, :].rearrange("p (i j) -> p i j", i=16)
                        s2b = sv[:, 2 * h + 1:2 * h + 2, :].to_broadcast(B3)
                        E2b = Ev[:, 2 * h + 1:2 * h + 2, :].to_broadcast(B3)
                        thb = th3[:, h, i0:i0 + 16].unsqueeze(2).to_broadcast(B3)
                        A1b = A13[:, h, i0:i0 + 16].unsqueeze(2).to_broadcast(B3)
                        k.op("dve", lambda e: e.tensor_tensor(out=W3, in0=s2b, in1=thb, op=ALU.is_ge), reads=[sb_s, th], writes=[W_])
                        k.op("dve", lambda e: e.tensor_tensor(out=W3, in0=W3, in1=E2b, op=ALU.mult), reads=[W_, sb_E], writes=[W_])
                        if h == 0:
                            k.op("pool", lambda e: e.tensor_tensor(out=G3, in0=W3, in1=A1b, op=ALU.mult), reads=[W_, A1], writes=[Gb_])
                        else:
                            k.op("pool", lambda e: e.tensor_tensor(out=W3, in0=W3, in1=A1b, op=ALU.mult), reads=[W_, A1], writes=[W_])
                            if h < 7:
                                k.op("pool", lambda e: e.tensor_tensor(out=G3, in0=G3, in1=W3, op=ALU.add), reads=[W_, Gb_], writes=[Gb_])
                            else:
                                k.op("pool", lambda e: e.tensor_tensor(out=Gout, in0=G3, in1=W3, op=ALU.add), reads=[W_, Gb_], writes=[G_])
                k.dma("act", Gd[tt * 128:(tt + 1) * 128, :], G_[:, :], reads=[G_])
            k.close()
            if not p2_on:
                return
            k.open()
            m5b = [k.sb(f"m5b{r}", [128, D], F32) for r in range(2)]
            load_bcast(m5b[0], MR[l, 0:1, 5 * D:6 * D])
            load_bcast(m5b[1], MR[l, 1:2, 5 * D:6 * D])
            g_b = k.sb("pg_b", [128, D], F32); b_b = k.sb("pb_b", [128, D], F32)
            load_bcast(g_b, ln_g[2 * l + 1:2 * l + 2, :]); load_bcast(b_b, ln_b[2 * l + 1:2 * l + 2, :])
            hT4 = k.sb("hT4", [128, KC * 512], BF16); h4v = hT4.t[:, :].rearrange("p (c t) -> p c t", c=KC)
            acc = k.sb("acc", [128, 4 * D], F32)
            ub = [k.sb(f"uTb{i}", [128, KC * 512], BF16) for i in range(2)]
            vb = [k.sb(f"vb{i}", [128, 4 * D], BF16) for i in range(2)]
            gb = [k.sb(f"gblk{i}", [128, 512], BF16) for i in range(2)]
            gl = [k.sb(f"gl{i}", [128, 512], F32) for i in range(2)]
            Ab = [k.sb(f"Ab{i}", [128, 512], BF16) for i in range(2)]
            AT = [k.sb(f"AT{i}", [128, 512], BF16) for i in range(2)]
            pS = [k.ps(f"pS{i}", [128, 512], F32) for i in range(2)]
            pT = k.ps("pT", [128, 512], BF16)
            pO = [k.ps(f"pO{q}", [128, 512], F32) for q in range(4)]
            xt = k.sb("xp", [128, D], F32); tmp = k.sb("tmpp", [128, D], F32)
            stats = k.sb("pstats", [128, 24], F32); mv = k.sb("pmv", [128, 2], F32); rstd = k.sb("prstd", [128, 1], F32)
            it = 0
            for t0 in range(0, ntok, 512):
                nsub = min(512, ntok - t0) // 128
                k.dma("sp", h4v[:, :, 0:nsub * 128], HTv[:, :, t0:t0 + nsub * 128], writes=[hT4])
                for eb in range(NE // 512):
                    e0 = eb * 512
                    u_ = ub[eb % 2]; v_ = vb[eb % 2]
                    uv = u_.t[:, :].rearrange("p (c n) -> p c n", c=KC)
                    vv = v_.t[:, :].rearrange("p (c d) -> p c d", c=4)
                    k.dma("pool", uv, colview(peer_uT[l], e0, 512), writes=[u_])
                    k.dma("pool", vv, peer_v[l][e0:e0 + 512, :].rearrange("(c p) d -> p c d", p=128), writes=[v_])
                    for sub in range(nsub):
                        tok0 = t0 + sub * 128
                        pS_ = pS[it % 2]; gb_ = gb[it % 2]; gl_ = gl[it % 2]; A_ = Ab[it % 2]; AT_ = AT[it % 2]
                        k.dma("sp", gb_[:, :], Gd[tok0:tok0 + 128, e0:e0 + 512], writes=[gb_])
                        for kc in range(KC):
                            k.op("pe", lambda e, kc=kc: e.matmul(pS_[:, :], lhsT=h4v[:, kc, sub * 128:(sub + 1) * 128], rhs=uv[:, kc, :],
                                                                 start=(kc == 0), stop=(kc == KC - 1)), reads=[hT4, u_], writes=[pS_])
                        k.op("act", lambda e: e.activation(out=gl_[:, :], in_=pS_[:, :], func=AF.Gelu), reads=[pS_], writes=[gl_])
                        k.op("dve", lambda e: e.tensor_tensor(out=A_[:, :], in0=gl_[:, :], in1=gb_[:, :], op=ALU.mult),
                             reads=[gl_, gb_], writes=[A_])
                        for c in range(4):
                            k.op("pe", lambda e, c=c: e.transpose(out=pT[:, c * 128:(c + 1) * 128], in_=A_[:, c * 128:(c + 1) * 128],
                                                                  identity=identb[:, :]), reads=[A_, identb], writes=[pT])
                        k.op("act", lambda e: e.activation(out=AT_[:, :], in_=pT[:, :], func=AF.Identity), reads=[pT], writes=[AT_])
                        for q in range(4):
                            for c in range(4):
                                k.op("pe", lambda e, c=c, q=q: e.matmul(pO[q][:, :], lhsT=AT_[:, c * 128:(c + 1) * 128],
                                                                        rhs=vv[:, c, q * 512:(q + 1) * 512], start=(c == 0), stop=(c == 3)),
                                     reads=[AT_, v_], writes=[pO[q]])
                        for q in range(4):
                            asl = acc[:, sub * D + q * 512:sub * D + (q + 1) * 512]
                            if eb == 0:
                                k.op("dve", lambda e, q=q, asl=asl: e.tensor_copy(out=asl, in_=pO[q][:, :]), reads=[pO[q]], writes=[acc])
                            else:
                                k.op("dve", lambda e, q=q, asl=asl: e.tensor_tensor(out=asl, in0=asl, in1=pO[q][:, :], op=ALU.add),
                                     reads=[pO[q], acc], writes=[acc])
                        it += 1
                for sub in range(nsub):
                    tok0 = t0 + sub * 128
                    r = 0 if tok0 < TL else 1
                    k.dma("sp", xt[:, :], X[tok0:tok0 + 128, :], writes=[xt])
                    residual_ln(None, xt, m5b[r], g_b, b_b, tmp, stats, mv, rstd,
                                yget=lambda q, sub=sub: (acc[:, sub * D + q * 512:sub * D + (q + 1) * 512], acc))
                    k.dma("act", X[tok0:tok0 + 128, :], xt[:, :], reads=[xt])
            k.close()

        def hgrn_layer(l):
            ntok = T
            load_mcol(l)
            build_HT(ntok, 0, 1)
            HTv = HT.t.ap().rearrange("(c p) t -> p c t", p=128)
            k.open()
            lbB = [k.sb(f"lbB{d_}", [128, D], F32) for d_ in range(2)]
            omlB = [k.sb(f"omlB{d_}", [128, D], F32) for d_ in range(2)]
            ex = k.sb("lbex", [128, D], F32); den = k.sb("lbden", [128, D], F32)
            for d_ in range(2):
                for kk in range(DEPTH):
                    load_bcast(ex, hg_lb_logits[d_ * DEPTH + kk:d_ * DEPTH + kk + 1, :])
                    k.op("act", lambda e: e.activation(out=ex[:, :], in_=ex[:, :], func=AF.Exp), reads=[ex], writes=[ex])
                    if kk == 0:
                        k.op("dve", lambda e: e.tensor_copy(out=den[:, :], in_=ex[:, :]), reads=[ex], writes=[den])
                        k.op("dve", lambda e: e.memset(lbB[d_][:, :], 0.0), writes=[lbB[d_]])
                    else:
                        k.op("dve", lambda e: e.tensor_tensor(out=den[:, :], in0=den[:, :], in1=ex[:, :], op=ALU.add), reads=[ex, den], writes=[den])
                        if kk <= l:
                            k.op("dve", lambda e: e.tensor_tensor(out=lbB[d_][:, :], in0=lbB[d_][:, :], in1=ex[:, :], op=ALU.add),
                                 reads=[ex, lbB[d_]], writes=[lbB[d_]])
                k.op("dve", lambda e: e.reciprocal(out=den[:, :], in_=den[:, :]), reads=[den], writes=[den])
                k.op("dve", lambda e: e.tensor_tensor(out=lbB[d_][:, :], in0=lbB[d_][:, :], in1=den[:, :], op=ALU.mult), reads=[den, lbB[d_]], writes=[lbB[d_]])
                k.op("dve", lambda e: e.tensor_scalar(out=omlB[d_][:, :], in0=lbB[d_][:, :], scalar1=-1.0, scalar2=1.0, op0=ALU.mult, op1=ALU.add),
                     reads=[lbB[d_]], writes=[omlB[d_]])
            wb = [k.sb(f"hwb{i}", [128, KC * 512], BF16) for i in range(2)]
            hts = [k.sb(f"hht{i}", [128, KC * 128], BF16) for i in range(2)]
            pp = [k.ps(f"hpp{i}", [128, 512], F32) for i in range(2)]
            o1 = [k.sb(f"ho1_{i}", [128, 512], F32) for i in range(2)]
            o2 = [k.sb(f"ho2_{i}", [128, 512], F32) for i in range(2)]
            ob = [k.sb(f"hob{i}", [128, 512], BF16) for i in range(2)]
            it = 0; bi = 0
            for nm in ("w_q", "w_i", "w_g", "w_f0", "w_f1"):
                for cb in range(4):
                    w_ = wb[bi % 2]; bi += 1
                    wv = w_.t[:, :].rearrange("p (c n) -> p c n", c=KC)
                    k.dma("pool", wv, colview(hg_w[nm], cb * 512, 512), writes=[w_])
                    cs = slice(cb * 512, (cb + 1) * 512)
                    for tt in range(ntok // 128):
                        h_ = hts[it % 2]; p_ = pp[it % 2]; a_ = o1[it % 2]; b2_ = o2[it % 2]; c_ = ob[it % 2]
                        hv = h_.t[:, :].rearrange("p (c t) -> p c t", c=KC)
                        rows = slice(tt * 128, (tt + 1) * 128)
                        k.dma("sp", hv, HTv[:, :, rows], writes=[h_])
                        for kc in range(KC):
                            k.op("pe", lambda e, kc=kc: e.matmul(p_[:, :], lhsT=hv[:, kc, :], rhs=wv[:, kc, :], start=(kc == 0), stop=(kc == KC - 1)),
                                 reads=[h_, w_], writes=[p_])
                        if nm in ("w_q", "w_g"):
                            k.op("act", lambda e: e.activation(out=a_[:, :], in_=p_[:, :], func=AF.Silu), reads=[p_], writes=[a_])
                            k.dma("act", (Qd if nm == "w_q" else GTd)[rows, cs], a_[:, :], reads=[a_])
                        elif nm == "w_i":
                            k.op("act", lambda e: e.activation(out=c_[:, :], in_=p_[:, :], func=AF.Identity), reads=[p_], writes=[c_])
                            k.dma("act", Vd[rows, cs], c_[:, :], reads=[c_])
                        else:
                            d_ = 0 if nm == "w_f0" else 1
                            k.op("act", lambda e: e.activation(out=a_[:, :], in_=p_[:, :], func=AF.Sigmoid), reads=[p_], writes=[a_])
                            k.op("dve", lambda e: e.tensor_tensor(out=a_[:, :], in0=a_[:, :], in1=omlB[d_][:, cs], op=ALU.mult), reads=[a_, omlB[d_]], writes=[a_])
                            k.op("dve", lambda e: e.tensor_tensor(out=a_[:, :], in0=a_[:, :], in1=lbB[d_][:, cs], op=ALU.add), reads=[a_, lbB[d_]], writes=[a_])
                            k.op("dve", lambda e: e.tensor_scalar(out=b2_[:, :], in0=a_[:, :], scalar1=-1.0, scalar2=1.0, op0=ALU.mult, op1=ALU.add),
                                 reads=[a_], writes=[b2_])
                            k.dma("act", KDd[d_][rows, cs], b2_[:, :], reads=[b2_])
                            k.op("act", lambda e: e.activation(out=a_[:, :], in_=a_[:, :], func=AF.Ln), reads=[a_], writes=[a_])
                            k.dma("act", LFd[d_][rows, cs], a_[:, :], reads=[a_])
                        it += 1
            k.close()
            k.open()
            hc = k.sb("hc", [128, 258], F32)
            k.dma("sp", hc[:, :], hconst[:, :], writes=[hc])
            LF = k.sb("LF", [128, D], F32); KDt = k.sb("KDt", [128, D], F32); Qt_in = k.sb("Qin", [128, D], F32)
            Vt = k.sb("Vt", [128, D], BF16); Eb = k.sb("Ebuf", [128, D], F32)
            Qt = k.sb("Qt", [128, D], BF16); Kt = k.sb("Kt", [128, D], BF16); Kta = k.sb("Kta", [128, D], BF16); Ktb = k.sb("Ktb", [128, D], BF16)
            S = k.sb("S", [128, D], F32)
            Sb = [k.sb(f"Sb{i}", [128, D], BF16) for i in range(2)]
            eb = k.sb("eb", [128, 32], F32)
            QT = k.sb("QT", [128, 128], BF16); KT = k.sb("KT", [128, 128], BF16)
            QTa = k.sb("QTa", [128, 128], BF16); QTb = k.sb("QTb", [128, 128], BF16)
            attT = k.sb("attT", [128, 128], BF16)
            osb = k.sb("osb", [128, D], F32); ofw = k.sb("ofw", [128, D], F32)
            bcO = [k.ps(f"bcO{q}", [128, 512], F32) for q in range(4)]
            ebp = k.ps("ebp", [128, 32], F32)
            pT = k.ps("hpT", [128, 256], BF16)
            pA = k.ps("hpA", [128, 128], F32)
            pP = k.ps("hpP", [128, 256], F32)
            for d_ in range(2):
                tri = hc[:, d_ * 128:(d_ + 1) * 128]
                cf, csd = (0, 1) if d_ == 0 else (1, 0)
                k.op("dve", lambda e: e.memset(S[:, :], 0.0), writes=[S])
                k.op("dve", lambda e: e.memset(Sb[0][:, :], 0.0), writes=[Sb[0]])
                k.op("dve", lambda e: e.memset(QTa[:, :], 0.0), writes=[QTa])
                k.op("dve", lambda e: e.memset(QTb[:, :], 0.0), writes=[QTb])
                order = [32, 33] + list(range(32)) if d_ == 0 else [33, 32] + list(range(31, -1, -1))
                for tt in order:
                    rows = slice(tt * 128, (tt + 1) * 128)
                    k.dma("sp", LF[:, :], LFd[d_][rows, :], writes=[LF])
                    k.dma("pool", KDt[:, :], KDd[d_][rows, :], writes=[KDt])
                    k.dma("sp", Qt_in[:, :], Qd[rows, :], writes=[Qt_in])
                    k.dma("pool", Vt[:, :], Vd[rows, :], writes=[Vt])
                    for q in range(4):
                        k.op("pe", lambda e, q=q: e.matmul(bcO[q][:, :], lhsT=tri, rhs=LF[:, q * 512:(q + 1) * 512], start=True, stop=True),
                             reads=[hc, LF], writes=[bcO[q]])
                    for h in range(16):
                        k.op("pe", lambda e, h=h: e.matmul(ebp[:, 2 * h:2 * h + 2], lhsT=LF[:, h * 128:(h + 1) * 128], rhs=hc[:, 256:258],
                                                           start=True, stop=True), reads=[hc, LF], writes=[ebp])
                    k.op("act", lambda e: e.activation(out=eb[:, :], in_=ebp[:, :], func=AF.Exp), reads=[ebp], writes=[eb])
                    for q in range(4):
                        k.op("act", lambda e, q=q: e.activation(out=Eb[:, q * 512:(q + 1) * 512], in_=bcO[q][:, :], func=AF.Exp),
                             reads=[bcO[q]], writes=[Eb])
                    k.op("dve", lambda e: e.tensor_tensor(out=Qt[:, :], in0=Qt_in[:, :], in1=Eb[:, :], op=ALU.mult), reads=[Qt_in, Eb], writes=[Qt])
                    for q in range(4):
                        k.op("act", lambda e, q=q: e.activation(out=Eb[:, q * 512:(q + 1) * 512], in_=bcO[q][:, :], func=AF.Exp, scale=-1.0),
                             reads=[bcO[q]], writes=[Eb])
                    k.op("dve", lambda e: e.tensor_tensor(out=Kt[:, :], in0=KDt[:, :], in1=Eb[:, :], op=ALU.mult), reads=[KDt, Eb], writes=[Kt])
                    k.op("dve", lambda e: e.scalar_tensor_tensor(out=Kta[:, :], in0=KDt[:, :], scalar=hc[:, 256 + cf:257 + cf], in1=Eb[:, :],
                                                                 op0=ALU.mult, op1=ALU.mult), reads=[KDt, Eb, hc], writes=[Kta])
                    k.op("dve", lambda e: e.scalar_tensor_tensor(out=Ktb[:, :], in0=KDt[:, :], scalar=hc[:, 256 + csd:257 + csd], in1=Eb[:, :],
                                                                 op0=ALU.mult, op1=ALU.mult), reads=[KDt, Eb, hc], writes=[Ktb])
                    for h in range(16):
                        hs = slice(h * 128, (h + 1) * 128)
                        k.op("pe", lambda e: e.transpose(out=pT[:, 0:128], in_=Qt[:, hs], identity=identb[:, :]), reads=[Qt, identb], writes=[pT])
                        k.op("pe", lambda e: e.transpose(out=pT[:, 128:256], in_=Kt[:, hs], identity=identb[:, :]), reads=[Kt, identb], writes=[pT])
                        k.op("act", lambda e: e.activation(out=QT[:, :], in_=pT[:, 0:128], func=AF.Identity), reads=[pT], writes=[QT])
                        k.op("act", lambda e: e.activation(out=KT[:, :], in_=pT[:, 128:256], func=AF.Identity), reads=[pT], writes=[KT])
                        k.op("dve", lambda e: e.tensor_copy(out=QTa[:, cf * 64:(cf + 1) * 64], in_=pT[:, cf * 64:(cf + 1) * 64]), reads=[pT], writes=[QTa])
                        k.op("dve", lambda e: e.tensor_copy(out=QTb[:, csd * 64:(csd + 1) * 64], in_=pT[:, csd * 64:(csd + 1) * 64]), reads=[pT], writes=[QTb])
                        k.op("pe", lambda e: e.matmul(pA[:, :], lhsT=KT[:, :], rhs=QT[:, :], start=True, stop=True), reads=[KT, QT], writes=[pA])
                        k.op("dve", lambda e: e.tensor_tensor(out=attT[:, :], in0=pA[:, :], in1=tri, op=ALU.mult), reads=[pA, hc], writes=[attT])
                        k.op("pe", lambda e: e.matmul(pP[:, 0:128], lhsT=Kta[:, hs], rhs=Vt[:, hs], start=True, stop=True), reads=[Kta, Vt], writes=[pP])
                        k.op("dve", lambda e: e.tensor_tensor(out=S[:, hs], in0=S[:, hs], in1=pP[:, 0:128], op=ALU.add), reads=[pP, S], writes=[S])
                        k.op("dve", lambda e: e.tensor_scalar(out=S[:, hs], in0=S[:, hs], scalar1=eb[:, 2 * h + cf:2 * h + cf + 1], scalar2=None, op0=ALU.mult),
                             reads=[S, eb], writes=[S])
                        k.op("act", lambda e: e.activation(out=Sb[1][:, hs], in_=S[:, hs], func=AF.Identity), reads=[S], writes=[Sb[1]])
                        oq = bcO[h // 4]; osl = slice((h % 4) * 128, (h % 4 + 1) * 128)
                        k.op("pe", lambda e: e.matmul(oq[:, osl], lhsT=attT[:, :], rhs=Vt[:, hs], start=True, stop=False), reads=[attT, Vt], writes=[oq])
                        k.op("pe", lambda e: e.matmul(oq[:, osl], lhsT=QTa[:, :], rhs=Sb[0][:, hs], start=False, stop=False), reads=[QTa, Sb[0]], writes=[oq])
                        k.op("pe", lambda e: e.matmul(oq[:, osl], lhsT=QTb[:, :], rhs=Sb[1][:, hs], start=False, stop=True), reads=[QTb, Sb[1]], writes=[oq])
                        k.op("pe", lambda e: e.matmul(pP[:, 128:256], lhsT=Ktb[:, hs], rhs=Vt[:, hs], start=True, stop=True), reads=[Ktb, Vt], writes=[pP])
                        k.op("dve", lambda e: e.tensor_tensor(out=S[:, hs], in0=S[:, hs], in1=pP[:, 128:256], op=ALU.add), reads=[pP, S], writes=[S])
                        k.op("dve", lambda e: e.tensor_scalar(out=S[:, hs], in0=S[:, hs], scalar1=eb[:, 2 * h + csd:2 * h + csd + 1], scalar2=None, op0=ALU.mult),
                             reads=[S, eb], writes=[S])
                        k.op("act", lambda e: e.activation(out=Sb[0][:, hs], in_=S[:, hs], func=AF.Identity), reads=[S], writes=[Sb[0]])
                    if d_ == 0:
                        for q in range(4):
                            k.op("act", lambda e, q=q: e.activation(out=osb[:, q * 512:(q + 1) * 512], in_=bcO[q][:, :], func=AF.Identity),
                                 reads=[bcO[q]], writes=[osb])
                    else:
                        k.dma("sp", ofw[:, :], Od[rows, :], writes=[ofw])
                        for q in range(4):
                            k.op("dve", lambda e, q=q: e.tensor_tensor(out=osb[:, q * 512:(q + 1) * 512], in0=ofw[:, q * 512:(q + 1) * 512],
                                                                       in1=bcO[q][:, :], op=ALU.add), reads=[bcO[q], ofw], writes=[osb])
                    k.dma("act", Od[rows, :], osb[:, :], reads=[osb])
                k.barrier()
            k.close()
            k.open()
            wo = k.sb("hwo", [128, KC * D], BF16)
            wov = wo.t[:, :].rearrange("p (c n) -> p c n", c=KC)
            for q in range(4):
                k.dma("pool", wov[:, :, q * 512:(q + 1) * 512], colview(hg_w["w_o"], q * 512, 512), writes=[wo])
            m2b = [k.sb(f"hm2b{r}", [128, D], F32) for r in range(2)]
            load_bcast(m2b[0], MR[l, 0:1, 2 * D:3 * D]); load_bcast(m2b[1], MR[l, 1:2, 2 * D:3 * D])
            g_b = k.sb("hg_b", [128, D], F32); b_b = k.sb("hb_b", [128, D], F32); ngB = k.sb("ngB", [128, D], F32)
            load_bcast(g_b, ln_g[2 * l:2 * l + 1, :]); load_bcast(b_b, ln_b[2 * l:2 * l + 1, :]); load_bcast(ngB, hg_norm_g[0:1, :])
            ot = k.sb("hot", [128, D], F32); gtt = k.sb("hgt", [128, D], F32); sq = k.sb("hsq", [128, D], F32)
            ms = k.sb("hms", [128, 16], F32)
            gz = k.sb("hgz", [128, D], BF16); gT = k.sb("hgT", [128, D], BF16)
            pT2 = k.ps("hpT2", [128, 1024], BF16)
            yps = [k.ps(f"hyps{q}", [128, 512], F32) for q in range(4)]
            xt = k.sb("hxt", [128, D], F32); tmp = k.sb("htmp", [128, D], F32)
            stats = k.sb("hstats", [128, 24], F32); mv = k.sb("hmv", [128, 2], F32); rstd = k.sb("hrstd", [128, 1], F32)
            ot3 = ot.t[:, :].rearrange("p (h v) -> p h v", h=16); sq3 = sq.t[:, :].rearrange("p (h v) -> p h v", h=16)
            for tt in range(ntok // 128):
                rows = slice(tt * 128, (tt + 1) * 128)
                r = 0 if tt < TL // 128 else 1
                k.dma("sp", ot[:, :], Od[rows, :], writes=[ot])
                k.dma("pool", gtt[:, :], GTd[rows, :], writes=[gtt])
                k.dma("sp", xt[:, :], X[rows, :], writes=[xt])
                k.op("act", lambda e: e.activation(out=sq[:, :], in_=ot[:, :], func=AF.Square), reads=[ot], writes=[sq])
                k.op("dve", lambda e: e.tensor_reduce(out=ms[:, :], in_=sq3, axis=mybir.AxisListType.X, op=ALU.add), reads=[sq], writes=[ms])
                k.op("dve", lambda e: e.tensor_scalar(out=ms[:, :], in0=ms[:, :], scalar1=1.0 / 128, scalar2=1e-6, op0=ALU.mult, op1=ALU.add),
                     reads=[ms], writes=[ms])
                k.op("act", lambda e: e.activation(out=ms[:, :], in_=ms[:, :], func=AF.Sqrt), reads=[ms], writes=[ms])
                k.op("dve", lambda e: e.reciprocal(out=ms[:, :], in_=ms[:, :]), reads=[ms], writes=[ms])
                k.op("dve", lambda e: e.tensor_tensor(out=sq3, in0=ot3, in1=ms[:, :].unsqueeze(2).to_broadcast([128, 16, 128]), op=ALU.mult),
                     reads=[ot, ms], writes=[sq])
                k.op("pool", lambda e: e.tensor_tensor(out=sq[:, :], in0=sq[:, :], in1=ngB[:, :], op=ALU.mult), reads=[sq, ngB], writes=[sq])
                k.op("dve", lambda e: e.tensor_tensor(out=gz[:, :], in0=sq[:, :], in1=gtt[:, :], op=ALU.mult), reads=[sq, gtt], writes=[gz])
                for half in range(2):
                    for c in range(8):
                        kc = half * 8 + c
                        k.op("pe", lambda e, kc=kc, c=c: e.transpose(out=pT2[:, c * 128:(c + 1) * 128], in_=gz[:, kc * 128:(kc + 1) * 128],
                                                                     identity=identb[:, :]), reads=[gz, identb], writes=[pT2])
                    k.op("act", lambda e, half=half: e.activation(out=gT[:, half * 1024:(half + 1) * 1024], in_=pT2[:, :], func=AF.Identity),
                         reads=[pT2], writes=[gT])
                for q in range(4):
                    for kc in range(KC):
                        k.op("pe", lambda e, kc=kc, q=q: e.matmul(yps[q][:, :], lhsT=gT[:, kc * 128:(kc + 1) * 128], rhs=wov[:, kc, q * 512:(q + 1) * 512],
                                                                  start=(kc == 0), stop=(kc == KC - 1)), reads=[gT, wo], writes=[yps[q]])
                residual_ln(yps, xt, m2b[r], g_b, b_b, tmp, stats, mv, rstd)
                k.dma("act", X[rows, :], xt[:, :], reads=[xt])
            k.close()

        one_t = k.sb("one_t", [128, 1], F32)
        k.op("dve", lambda e: e.memset(one_t[:, :], 1.0), writes=[one_t])

        def ssd_layer(l):
            ntok = T
            load_mcol(l)
            build_HT(ntok, 0, 1)
            HTv = HT.t.ap().rearrange("(c p) t -> p c t", p=128)
            k.open()
            wb = [k.sb(f"swb{i}", [128, KC * 512], BF16) for i in range(2)]
            hT = [k.sb(f"shT{i}", [128, KC * 512], BF16) for i in range(2)]
            pa = [k.ps(f"spa{i}", [128, 512], F32) for i in range(2)]
            ub = [k.sb(f"sub{i}", [128, 512], F32) for i in range(2)]
            tiles = [(t0, min(512, ntok - t0)) for t0 in range(0, ntok, 512)]
            it = 0
            for cb in range(12):
                w_ = wb[cb % 2]; wv = w_.t[:, :].rearrange("p (c n) -> p c n", c=KC)
                k.dma("pool", wv, colview(m_w_in, DI + cb * 512, 512), writes=[w_])
                for (t0, n) in tiles:
                    h_ = hT[it % 2]; hv = h_.t[:, :].rearrange("p (c t) -> p c t", c=KC)
                    k.dma("sp", hv[:, :, 0:n], HTv[:, :, t0:t0 + n], writes=[h_])
                    for jj in range(4):
                        pa_ = pa[jj % 2]; ub_ = ub[jj % 2]
                        for kc in range(KC):
                            k.op("pe", lambda e, kc=kc: e.matmul(pa_[:, 0:n], lhsT=wv[:, kc, jj * 128:(jj + 1) * 128], rhs=hv[:, kc, 0:n],
                                                                 start=(kc == 0), stop=(kc == KC - 1)), reads=[w_, h_], writes=[pa_])
                        k.op("act", lambda e: e.activation(out=ub_[:, 0:n], in_=pa_[:, 0:n], func=AF.Identity), reads=[pa_], writes=[ub_])
                        ch0 = (cb * 4 + jj) * 128
                        k.dma("act", XBCT[ch0:ch0 + 128, t0:t0 + n], ub_[:, 0:n], reads=[ub_])
                    it += 1
            k.close()
            k.open()
            wb = [k.sb(f"s2wb{i}", [128, KC * 512], BF16) for i in range(2)]
            hts = [k.sb(f"s2ht{i}", [128, KC * 128], BF16) for i in range(2)]
            pp = [k.ps(f"s2pp{i}", [128, 512], F32) for i in range(2)]
            o1 = [k.sb(f"s2o{i}", [128, 512], F32) for i in range(2)]
            it = 0
            for cb in range(9):
                w_ = wb[cb % 2]; wv = w_.t[:, :].rearrange("p (c n) -> p c n", c=KC)
                ncol = 512 if cb < 8 else 128
                c0 = cb * 512 if cb < 8 else 10240
                k.dma("pool", wv[:, :, 0:ncol], colview(m_w_in, c0, ncol), writes=[w_])
                for tt in range((TL if cb < 8 else ntok) // 128):
                    h_ = hts[it % 2]; p_ = pp[it % 2]; a_ = o1[it % 2]
                    hv = h_.t[:, :].rearrange("p (c t) -> p c t", c=KC)
                    rows = slice(tt * 128, (tt + 1) * 128)
                    k.dma("sp", hv, HTv[:, :, rows], writes=[h_])
                    for kc in range(KC):
                        k.op("pe", lambda e, kc=kc: e.matmul(p_[:, 0:ncol], lhsT=hv[:, kc, :], rhs=wv[:, kc, 0:ncol], start=(kc == 0), stop=(kc == KC - 1)),
                             reads=[h_, w_], writes=[p_])
                    if cb < 8:
                        k.op("act", lambda e: e.activation(out=a_[:, :], in_=p_[:, :], func=AF.Silu), reads=[p_], writes=[a_])
                        k.dma("act", ZSd[rows, cb * 512:(cb + 1) * 512], a_[:, :], reads=[a_])
                    else:
                        k.op("act", lambda e: e.activation(out=a_[:, 0:128], in_=p_[:, 0:128], func=AF.Identity), reads=[p_], writes=[a_])
                        k.dma("act", DTd[rows, :], a_[:, 0:128], reads=[a_])
                    it += 1
            k.close()
            k.open()
            cw = k.sb("scw", [128, 48 * 5], F32); cbias = k.sb("scb", [128, 48], F32)
            k.dma("sp", cw[:, :], m_convT[:, :], writes=[cw]); k.dma("sp", cbias[:, :], m_convbT[:, :], writes=[cbias])
            uin = [k.sb(f"suin{i}", [128, T], F32) for i in range(2)]
            acc = [k.sb(f"sacc{i}", [128, T], F32) for i in range(2)]
            accb = [k.sb(f"saccb{i}", [128, T], BF16) for i in range(2)]
            ptr = [k.ps(f"sptr{i}", [128, 512], F32) for i in range(2)]
            trs = [k.sb(f"strs{i}", [128, 512], F32) for i in range(2)]
            trb = [k.sb(f"strb{i}", [128, 512], BF16) for i in range(2)]
            ti = 0
            for c in range(48):
                u_ = uin[c % 2]; a_ = acc[c % 2]; ab_ = accb[c % 2]
                k.dma(["sp", "pool"][c % 2], u_[:, :], XBCT[c * 128:(c + 1) * 128, :], writes=[u_])
                for kk in [2, 0, 1, 3, 4]:
                    o = kk - 2
                    wk = cw[:, c * 5 + kk:c * 5 + kk + 1]
                    for (base, n) in ((0, TL), (TL, TC)):
                        lo_o, hi_o = max(0, -o), n - max(0, o)
                        lo_i, hi_i = max(0, o), n - max(0, -o)
                        oa = a_[:, base + lo_o:base + hi_o]; ia = u_[:, base + lo_i:base + hi_i]
                        if kk == 2:
                            k.op("dve", lambda e, oa=oa, ia=ia: e.tensor_scalar(out=oa, in0=ia, scalar1=wk, scalar2=None, op0=ALU.mult),
                                 reads=[u_, cw], writes=[a_])
                        else:
                            k.op("dve", lambda e, oa=oa, ia=ia: e.scalar_tensor_tensor(out=oa, in0=ia, scalar=wk, in1=oa, op0=ALU.mult, op1=ALU.add),
                                 reads=[u_, cw, a_], writes=[a_])
                k.op("act", lambda e: e.activation(out=a_[:, :], in_=a_[:, :], func=AF.Silu, bias=cbias[:, c:c + 1], scale=1.0),
                     reads=[a_, cbias], writes=[a_])
                if c >= 32:
                    k.op("act", lambda e: e.activation(out=ab_[:, :], in_=a_[:, :], func=AF.Identity), reads=[a_], writes=[ab_])
                    dst = BTd if c < 40 else CTd
                    g0 = (c - 32) % 8
                    k.dma("act", dst[g0 * 128:(g0 + 1) * 128, :], ab_[:, :], reads=[ab_])
                if c < 40:
                    for t4 in range(0, ntok // 128, 4):
                        nb = min(4, ntok // 128 - t4)
                        p_ = ptr[ti % 2]; s_ = trs[ti % 2]; sb_ = trb[ti % 2]; ti += 1
                        for b4 in range(nb):
                            k.op("pe", lambda e, b4=b4: e.transpose(out=p_[:, b4 * 128:(b4 + 1) * 128], in_=a_[:, (t4 + b4) * 128:(t4 + b4 + 1) * 128],
                                                                    identity=ident[:, :]), reads=[a_, ident], writes=[p_])
                        if c < 32:
                            k.op("act", lambda e: e.activation(out=s_[:, 0:nb * 128], in_=p_[:, 0:nb * 128], func=AF.Identity), reads=[p_], writes=[s_])
                            dst = XTOK[t4 * 128:(t4 + nb) * 128, c * 128:(c + 1) * 128].rearrange("(b p) n -> p b n", p=128)
                            k.dma("sp", dst, s_.t[:, 0:nb * 128].rearrange("p (b n) -> p b n", n=128), reads=[s_])
                        else:
                            k.op("act", lambda e: e.activation(out=sb_[:, 0:nb * 128], in_=p_[:, 0:nb * 128], func=AF.Identity), reads=[p_], writes=[sb_])
                            g0 = c - 32
                            dst = BTOK[t4 * 128:(t4 + nb) * 128, g0 * 128:(g0 + 1) * 128].rearrange("(b p) n -> p b n", p=128)
                            k.dma("sp", dst, sb_.t[:, 0:nb * 128].rearrange("p (b n) -> p b n", n=128), reads=[sb_])
            k.close()
            k.open()
            sc = k.sb("sc", [128, 512], F32)
            k.dma("sp", sc[:, :], sconst[:, :], writes=[sc])
            biasB = [k.sb(f"dtbB{d_}", [128, 64], F32) for d_ in range(2)]
            aB = [k.sb(f"aB{d_}", [128, 64], F32) for d_ in range(2)]
            for d_ in range(2):
                k.dma("sp", biasB[d_][:, :], m_dt_bias[d_:d_ + 1, :].to_broadcast([128, 64]), writes=[biasB[d_]])
                k.dma("sp", aB[d_][:, :], m_A_log[d_:d_ + 1, :].to_broadcast([128, 64]), writes=[aB[d_]])
                k.op("act", lambda e: e.activation(out=aB[d_][:, :], in_=aB[d_][:, :], func=AF.Exp), reads=[aB[d_]], writes=[aB[d_]])
                k.op("dve", lambda e: e.tensor_scalar(out=aB[d_][:, :], in0=aB[d_][:, :], scalar1=-1.0, scalar2=None, op0=ALU.mult),
                     reads=[aB[d_]], writes=[aB[d_]])
            dtt = k.sb("dtt", [128, 128], F32)
            xx = k.sb("xx", [128, 64], F32); ax = k.sb("ax", [128, 64], F32); dt = k.sb("dt", [128, 64], F32); da = k.sb("da", [128, 64], F32)
            eA = k.sb("eA", [128, 64], F32); wdec = k.sb("wdec", [128, 64], F32); eend = k.sb("eend", [128, 64], F32); dtw = k.sb("dtw", [128, 64], F32)
            xt_ = k.sb("sxt", [128, DI], F32); xdt = k.sb("xdt", [128, DI], BF16); vpp = k.sb("vpp", [128, DI], BF16)
            btok = k.sb("btok", [128, 1024], BF16); bt = k.sb("bt", [128, 1024], BF16); ct = k.sb("ct", [128, 1024], BF16)
            ST = k.sb("ST", [128, DI], F32); STb = k.sb("STb", [128, DI], BF16)
            CBm = k.sb("CBm", [128, 128], F32); lD = [k.sb(f"lD{i}", [128, 128], F32) for i in range(2)]
            Lx = [k.sb(f"Lx{i}", [128, 128], F32) for i in range(2)]; attT = [k.sb(f"sattT{i}", [128, 128], BF16) for i in range(2)]
            ysb = k.sb("ysb", [128, DI], F32); yfw = k.sb("yfw", [128, DI], F32); ytmp = k.sb("ytmp", [128, 512], F32)
            p3 = k.ps("sp3", [128, 192], F32)
            pCB = k.ps("spCB", [128, 128], F32)
            pD = [k.ps(f"spD{i}", [128, 128], F32) for i in range(2)]
            pYd = k.ps("spYd", [128, 512], F32); pYo = k.ps("spYo", [128, 512], F32); pP = k.ps("spP", [128, 512], F32)
            x3 = xt_.t[:, :].rearrange("p (h q) -> p h q", h=64)
            B64 = [128, 64, 64]
            for d_ in range(2):
                tri = sc[:, d_ * 128:(d_ + 1) * 128]
                strict = sc[:, 256 + d_ * 128:256 + (d_ + 1) * 128]
                k.op("dve", lambda e: e.memset(ST[:, :], 0.0), writes=[ST])
                k.op("dve", lambda e: e.memset(STb[:, :], 0.0), writes=[STb])
                order = [32, 33] + list(range(32)) if d_ == 0 else [33, 32] + list(range(31, -1, -1))
                for tt in order:
                    rows = slice(tt * 128, (tt + 1) * 128)
                    is_lat = tt < 32
                    k.dma("sp", dtt[:, :], DTd[rows, :], writes=[dtt])
                    k.dma("sp", xt_[:, :], XTOK[rows, :], writes=[xt_])
                    k.dma("pool", btok[:, :], BTOK[rows, :], writes=[btok])
                    k.dma("pool", bt.t[:, :].rearrange("n (g t) -> n g t", g=8), BTd.t.ap().rearrange("(g n) t -> n g t", n=128)[:, :, rows], writes=[bt])
                    if is_lat:
                        k.dma("pool", ct.t[:, :].rearrange("n (g t) -> n g t", g=8), CTd.t.ap().rearrange("(g n) t -> n g t", n=128)[:, :, rows], writes=[ct])
                    k.op("dve", lambda e: e.tensor_tensor(out=xx[:, :], in0=dtt[:, d_ * 64:(d_ + 1) * 64], in1=biasB[d_][:, :], op=ALU.add),
                         reads=[dtt, biasB[d_]], writes=[xx])
                    k.op("act", lambda e: e.activation(out=ax[:, :], in_=xx[:, :], func=AF.Abs), reads=[xx], writes=[ax])
                    k.op("act", lambda e: e.activation(out=ax[:, :], in_=ax[:, :], func=AF.Exp, scale=-1.0), reads=[ax], writes=[ax])
                    k.op("act", lambda e: e.activation(out=ax[:, :], in_=ax[:, :], func=AF.Ln, bias=one_t[:, 0:1], scale=1.0), reads=[ax, one_t], writes=[ax])
                    k.op("dve", lambda e: e.scalar_tensor_tensor(out=dt[:, :], in0=xx[:, :], scalar=0.0, in1=ax[:, :], op0=ALU.max, op1=ALU.add),
                         reads=[xx, ax], writes=[dt])
                    k.op("dve", lambda e: e.tensor_tensor(out=da[:, :], in0=dt[:, :], in1=aB[d_][:, :], op=ALU.mult), reads=[dt, aB[d_]], writes=[da])
                    k.op("pe", lambda e: e.matmul(p3[:, 0:64], lhsT=tri, rhs=da[:, :], start=True, stop=True), reads=[sc, da], writes=[p3])
                    k.op("pe", lambda e: e.matmul(p3[:, 64:128], lhsT=strict, rhs=da[:, :], start=True, stop=True), reads=[sc, da], writes=[p3])
                    k.op("pe", lambda e: e.matmul(p3[:, 128:192], lhsT=ones[:, :], rhs=da[:, :], start=True, stop=True), reads=[ones, da], writes=[p3])
                    k.op("act", lambda e: e.activation(out=eA[:, :], in_=p3[:, 0:64], func=AF.Exp), reads=[p3], writes=[eA])
                    k.op("act", lambda e: e.activation(out=wdec[:, :], in_=p3[:, 64:128], func=AF.Exp), reads=[p3], writes=[wdec])
                    k.op("act", lambda e: e.activation(out=eend[:, :], in_=p3[:, 128:192], func=AF.Exp), reads=[p3], writes=[eend])
                    k.op("dve", lambda e: e.tensor_tensor(out=dtw[:, :], in0=dt[:, :], in1=wdec[:, :], op=ALU.mult), reads=[dt, wdec], writes=[dtw])
                    k.op("pool", lambda e: e.tensor_tensor(out=vpp.t[:, :].rearrange("p (h q) -> p h q", h=64), in0=x3,
                                                           in1=dtw[:, :].unsqueeze(2).to_broadcast(B64), op=ALU.mult), reads=[xt_, dtw], writes=[vpp])
                    if is_lat:
                        k.op("dve", lambda e: e.tensor_tensor(out=xdt.t[:, :].rearrange("p (h q) -> p h q", h=64), in0=x3,
                                                              in1=dt[:, :].unsqueeze(2).to_broadcast(B64), op=ALU.mult), reads=[xt_, dt], writes=[xdt])
                    hi = 0
                    for g in range(8):
                        gs = slice(g * 128, (g + 1) * 128); g5 = slice(g * 512, (g + 1) * 512)
                        if is_lat:
                            k.op("pe", lambda e: e.matmul(pCB[:, :], lhsT=bt[:, gs], rhs=ct[:, gs], start=True, stop=True), reads=[bt, ct], writes=[pCB])
                            k.op("dve", lambda e: e.tensor_tensor(out=CBm[:, :], in0=pCB[:, :], in1=tri, op=ALU.mult), reads=[pCB, sc], writes=[CBm])
                            for j in range(8):
                                head = g * 8 + j
                                lD_ = lD[hi % 2]; pD_ = pD[hi % 2]; L_ = Lx[hi % 2]; at_ = attT[hi % 2]; hi += 1
                                k.op("pool", lambda e: e.tensor_scalar(out=lD_[:, :], in0=strict, scalar1=da[:, head:head + 1], scalar2=None, op0=ALU.mult),
                                     reads=[sc, da], writes=[lD_])
                                k.op("pe", lambda e: e.matmul(pD_[:, :], lhsT=lD_[:, :], rhs=tri, start=True, stop=True), reads=[lD_, sc], writes=[pD_])
                                k.op("act", lambda e: e.activation(out=L_[:, :], in_=pD_[:, :], func=AF.Exp), reads=[pD_], writes=[L_])
                                k.op("dve", lambda e: e.tensor_tensor(out=at_[:, :], in0=L_[:, :], in1=CBm[:, :], op=ALU.mult), reads=[L_, CBm], writes=[at_])
                                k.op("pe", lambda e: e.matmul(pYd[:, j * 64:(j + 1) * 64], lhsT=at_[:, :], rhs=xdt[:, head * 64:(head + 1) * 64],
                                                              start=True, stop=True), reads=[at_, xdt], writes=[pYd])
                            k.op("pe", lambda e: e.matmul(pYo[:, :], lhsT=ct[:, gs], rhs=STb[:, g5], start=True, stop=True), reads=[ct, STb], writes=[pYo])
                            k.op("dve", lambda e: e.tensor_tensor(out=ytmp.t[:, :].rearrange("p (h q) -> p h q", h=8),
                                                                  in0=pYo.t[:, :].rearrange("p (h q) -> p h q", h=8),
                                                                  in1=eA[:, g * 8:(g + 1) * 8].unsqueeze(2).to_broadcast([128, 8, 64]), op=ALU.mult),
                                 reads=[pYo, eA], writes=[ytmp])
                            k.op("dve", lambda e: e.tensor_tensor(out=ysb[:, g5], in0=ytmp[:, :], in1=pYd[:, :], op=ALU.add), reads=[ytmp, pYd], writes=[ysb])
                        k.op("pe", lambda e: e.matmul(pP[:, :], lhsT=btok[:, gs], rhs=vpp[:, g5], start=True, stop=True), reads=[btok, vpp], writes=[pP])
                        k.op("pool", lambda e: e.tensor_tensor(out=ST.t[:, g5].rearrange("p (h q) -> p h q", h=8), in0=ST.t[:, g5].rearrange("p (h q) -> p h q", h=8),
                                                               in1=eend[:, g * 8:(g + 1) * 8].unsqueeze(2).to_broadcast([128, 8, 64]), op=ALU.mult),
                             reads=[ST, eend], writes=[ST])
                        k.op("dve", lambda e: e.tensor_tensor(out=ST[:, g5], in0=ST[:, g5], in1=pP[:, :], op=ALU.add), reads=[ST, pP], writes=[ST])
                        k.op("act", lambda e: e.activation(out=STb[:, g5], in_=ST[:, g5], func=AF.Identity), reads=[ST], writes=[STb])
                    if is_lat:
                        if d_ == 1:
                            k.dma("sp", yfw[:, :], Yd_[rows, :], writes=[yfw])
                            k.op("pool", lambda e: e.tensor_tensor(out=ysb[:, :], in0=ysb[:, :], in1=yfw[:, :], op=ALU.add), reads=[ysb, yfw], writes=[ysb])
                        k.dma("act", Yd_[rows, :], ysb[:, :], reads=[ysb])
                k.barrier()
            k.close()
            k.open()
            DB = k.sb("DB", [128, 64], F32)
            k.dma("sp", DB[:, :], m_D[0:1, :].to_broadcast([128, 64]), writes=[DB])
            ngB = k.sb("sngB", [128, DI], F32)
            k.dma("act", ngB[:, :], m_norm_g[0:1, :].to_broadcast([128, DI]), writes=[ngB])
            xq = k.sb("s4x", [128, DI], F32); yq = k.sb("s4y", [128, DI], F32); zq = k.sb("s4z", [128, DI], F32)
            ms = k.sb("s4ms", [128, 8], F32); gz = k.sb("s4gz", [128, DI], BF16); gT = k.sb("s4gT", [128, DI], BF16)
            pT2 = k.ps("s4pT", [128, 1024], BF16)
            GTv = GTs.t.ap().rearrange("(c p) t -> p c t", p=128)
            x3 = xq.t[:, :].rearrange("p (h q) -> p h q", h=64); y3 = yq.t[:, :].rearrange("p (h q) -> p h q", h=64)
            for tt in range(TL // 128):
                rows = slice(tt * 128, (tt + 1) * 128)
                k.dma("sp", xq[:, :], XTOK[rows, :], writes=[xq]); k.dma("pool", yq[:, :], Yd_[rows, :], writes=[yq]); k.dma("sp", zq[:, :], ZSd[rows, :], writes=[zq])
                k.op("dve", lambda e: e.tensor_tensor(out=x3, in0=x3, in1=DB[:, :].unsqueeze(2).to_broadcast(B64), op=ALU.mult), reads=[xq, DB], writes=[xq])
                k.op("pool", lambda e: e.tensor_tensor(out=yq[:, :], in0=yq[:, :], in1=xq[:, :], op=ALU.add), reads=[yq, xq], writes=[yq])
                k.op("dve", lambda e: e.tensor_tensor(out=yq[:, :], in0=yq[:, :], in1=zq[:, :], op=ALU.mult), reads=[yq, zq], writes=[yq])
                k.op("act", lambda e: e.activation(out=xq[:, :], in_=yq[:, :], func=AF.Square), reads=[yq], writes=[xq])
                k.op("dve", lambda e: e.tensor_reduce(out=ms[:, :], in_=xq.t[:, :].rearrange("p (g q) -> p g q", g=8), axis=mybir.AxisListType.X, op=ALU.add),
                     reads=[xq], writes=[ms])
                k.op("dve", lambda e: e.tensor_scalar(out=ms[:, :], in0=ms[:, :], scalar1=1.0 / 512, scalar2=1e-6, op0=ALU.mult, op1=ALU.add), reads=[ms], writes=[ms])
                k.op("act", lambda e: e.activation(out=ms[:, :], in_=ms[:, :], func=AF.Sqrt), reads=[ms], writes=[ms])
                k.op("dve", lambda e: e.reciprocal(out=ms[:, :], in_=ms[:, :]), reads=[ms], writes=[ms])
                k.op("dve", lambda e: e.tensor_tensor(out=yq.t[:, :].rearrange("p (g q) -> p g q", g=8), in0=yq.t[:, :].rearrange("p (g q) -> p g q", g=8),
                                                      in1=ms[:, :].unsqueeze(2).to_broadcast([128, 8, 512]), op=ALU.mult), reads=[yq, ms], writes=[yq])
                k.op("pool", lambda e: e.tensor_tensor(out=gz[:, :], in0=yq[:, :], in1=ngB[:, :], op=ALU.mult), reads=[yq, ngB], writes=[gz])
                for q4 in range(4):
                    for c in range(8):
                        kc = q4 * 8 + c
                        k.op("pe", lambda e, kc=kc, c=c: e.transpose(out=pT2[:, c * 128:(c + 1) * 128], in_=gz[:, kc * 128:(kc + 1) * 128],
                                                                     identity=identb[:, :]), reads=[gz, identb], writes=[pT2])
                    k.op("act", lambda e, q4=q4: e.activation(out=gT[:, q4 * 1024:(q4 + 1) * 1024], in_=pT2[:, :], func=AF.Identity), reads=[pT2], writes=[gT])
                k.dma("act", GTv[:, :, rows], gT.t[:, :].rearrange("p (c t) -> p c t", c=32), reads=[gT])
            k.close()
            k.open()
            wo = [k.sb(f"swo{i}", [128, 32 * 512], BF16) for i in range(2)]
            gts = [k.sb(f"s4gt{i}", [128, 32 * 128], BF16) for i in range(2)]
            py = [k.ps(f"s4py{i}", [128, 512], F32) for i in range(2)]
            yo = [k.sb(f"s4yo{i}", [128, 512], F32) for i in range(2)]
            it = 0
            for q in range(4):
                w_ = wo[q % 2]; wv = w_.t[:, :].rearrange("p (c n) -> p c n", c=32)
                k.dma("pool", wv, m_w_out.t.ap().rearrange("(c p) n -> p c n", p=128)[:, :, q * 512:(q + 1) * 512], writes=[w_])
                for tt in range(TL // 128):
                    g_ = gts[it % 2]; p_ = py[it % 2]; y_ = yo[it % 2]; it += 1
                    gv = g_.t[:, :].rearrange("p (c t) -> p c t", c=32)
                    rows = slice(tt * 128, (tt + 1) * 128)
                    k.dma("sp", gv, GTv[:, :, rows], writes=[g_])
                    for kc in range(32):
                        k.op("pe", lambda e, kc=kc: e.matmul(p_[:, :], lhsT=gv[:, kc, :], rhs=wv[:, kc, :], start=(kc == 0), stop=(kc == 31)),
                             reads=[g_, w_], writes=[p_])
                    k.op("act", lambda e: e.activation(out=y_[:, :], in_=p_[:, :], func=AF.Identity), reads=[p_], writes=[y_])
                    k.dma("act", YO[rows, q * 512:(q + 1) * 512], y_[:, :], reads=[y_])
            k.close()
            k.open()
            m2b = k.sb("sm2b", [128, D], F32); g_b = k.sb("sg_b", [128, D], F32); b_b = k.sb("sb_b", [128, D], F32)
            load_bcast(m2b, MR[l, 0:1, 2 * D:3 * D]); load_bcast(g_b, ln_g[2 * l:2 * l + 1, :]); load_bcast(b_b, ln_b[2 * l:2 * l + 1, :])
            yt = [k.sb(f"s4yt{i}", [128, D], F32) for i in range(2)]; xts = [k.sb(f"s4xt{i}", [128, D], F32) for i in range(2)]
            tmp = k.sb("s4tmp", [128, D], F32)
            stats = k.sb("s4stats", [128, 24], F32); mv = k.sb("s4mv", [128, 2], F32); rstd = k.sb("s4rstd", [128, 1], F32)
            for tt in range(TL // 128):
                rows = slice(tt * 128, (tt + 1) * 128)
                y_ = yt[tt % 2]; x_ = xts[tt % 2]
                k.dma("sp", y_[:, :], YO[rows, :], writes=[y_]); k.dma("pool", x_[:, :], X[rows, :], writes=[x_])
                residual_ln(None, x_, m2b, g_b, b_b, tmp, stats, mv, rstd, yget=lambda q, y_=y_: (y_[:, q * 512:(q + 1) * 512], y_))
                k.dma("act", X[rows, :], x_[:, :], reads=[x_])
            k.close()

        for l in layers_to_run:
            if l % 3 == 0:
                if mixer_on:
                    conv_layer(l, l // 3, with_ctx=(l == 0))
                else:
                    load_mcol(l)
            elif l % 3 == 1:
                if mixer_on:
                    hgrn_layer(l)
                else:
                    load_mcol(l)
            else:
                if mixer_on:
                    ssd_layer(l)
                else:
                    load_mcol(l)
            if peer_on:
                peer_layer(l, with_ctx=(l < 2))
        k.barrier()
        if dbgG is not None:
            k.dma("sp", dbgG[:, :], Gd[0:128, :])
        k.dma("sp", out_ctx[:, :], X[TL:T, :])
        for i in range(4):
            r0 = i * (TL // 4)
            k.dma(["sp", "pool"][i % 2], out[r0:r0 + TL // 4, :], X[r0:r0 + TL // 4, :])
        k.barrier()
        print("instructions:", k.n_ins)
    return nc


def _pad_keys(keys):
    kt = np.asarray(keys, np.float32).transpose(0, 2, 4, 1, 3).reshape(DEPTH, 2, 64, 1024)
    out = np.zeros((DEPTH, 128, 2, 1024), np.float32)
    out[:, 0:64, 0, :] = kt[:, 0]
    out[:, 64:128, 1, :] = kt[:, 1]
    return out.reshape(DEPTH, 128, 2048)


def prep_inputs(inp, layers_to_run):
    f = lambda a: np.ascontiguousarray(a, dtype=np.float32)
    shared = {
        "ident": np.eye(128, dtype=np.float32),
        "mod_b": f(inp["mod_b"]),
        "ln_g": f(inp["ln_g"].reshape(DEPTH * 2, D)), "ln_b": f(inp["ln_b"].reshape(DEPTH * 2, D)),
        "conv_w_in": f(inp["conv_w_in"]), "conv_w_out": f(inp["conv_w_out"]),
        "conv_dwT": f(inp["conv_w_dw"].reshape(2, 31, KC, 128).transpose(0, 3, 2, 1).reshape(2, 128, KC * 31)),
        "conv_lngT": f(inp["conv_ln_g"].reshape(2, KC, 128).transpose(0, 2, 1)),
        "conv_lnbT": f(inp["conv_ln_b"].reshape(2, KC, 128).transpose(0, 2, 1)),
        "peer_w_q": f(inp["peer_w_q"]),
        "peer_keysT": f(_pad_keys(inp["peer_keys"])),
    }
    tok = np.arange(128)
    same = (tok[:, None] // 64) == (tok[None, :] // 64)
    hc = np.zeros((128, 258), np.float32)
    hc[:, 0:128] = same & (tok[:, None] <= tok[None, :])
    hc[:, 128:256] = same & (tok[:, None] >= tok[None, :])
    hc[:, 256] = tok < 64
    hc[:, 257] = tok >= 64
    shared["hconst"] = hc
    sc = np.zeros((128, 512), np.float32)
    sc[:, 0:128] = tok[:, None] <= tok[None, :]
    sc[:, 128:256] = tok[:, None] >= tok[None, :]
    sc[:, 256:384] = tok[:, None] > tok[None, :]
    sc[:, 384:512] = tok[:, None] < tok[None, :]
    shared["sconst"] = sc
    shared["m_convT"] = f(inp["m_conv_w"][0].reshape(5, 48, 128).transpose(2, 1, 0).reshape(128, 240))
    shared["m_convbT"] = f(inp["m_conv_b"][0].reshape(48, 128).T)
    shared["m_dt_bias"] = f(inp["m_dt_bias"][0]); shared["m_A_log"] = f(inp["m_A_log"][0])
    shared["m_D"] = f(inp["m_D"][0].reshape(1, 64)); shared["m_norm_g"] = f(inp["m_norm_g"][0].reshape(1, 4096))
    if 2 in layers_to_run:
        shared["m_w_in"] = f(inp["m_w_in"][0]); shared["m_w_out"] = f(inp["m_w_out"][0])
    shared["hg_norm_g"] = f(inp["hg_norm_g"].reshape(1, D))
    shared["hg_lb_logits"] = f(inp["hg_lb_logits"].reshape(2 * DEPTH, D))
    if 1 in layers_to_run:
        shared["hg_w_q"] = f(inp["hg_w_q"][0]); shared["hg_w_i"] = f(inp["hg_w_i"][0]); shared["hg_w_g"] = f(inp["hg_w_g"][0])
        shared["hg_w_f0"] = f(inp["hg_w_f"][0, 0]); shared["hg_w_f1"] = f(inp["hg_w_f"][0, 1]); shared["hg_w_o"] = f(inp["hg_w_o"][0])
    for l in layers_to_run:
        shared[f"mod_w_{l}"] = f(inp["mod_w"][l])
        shared[f"peer_uT_{l}"] = f(inp["peer_u"][l].T)
        shared[f"peer_v_{l}"] = f(inp["peer_v"][l])
    maps = []
    for b in range(N_CORES):
        m = dict(shared)
        m["xin"] = f(np.concatenate([inp["x"][b], inp["ctx"][b]], axis=0))
        cc = np.stack([inp["c"][b], inp["c_ctx"]], axis=-1)
        m["cT"] = f(cc.reshape(KC, 128, 2).transpose(1, 0, 2).reshape(128, KC * 2))
        maps.append(m)
    return maps


def kernel(**inp):
    layers = [0, 1, 2, 3]
    nc = build(layers)
    maps = prep_inputs(inp, layers)
    res = run_bass_kernel_spmd(nc, maps, core_ids=list(range(N_CORES)))
    return np.stack([res.results[b]["out"] for b in range(N_CORES)], axis=0).astype(np.float32)
```

```python
import numpy as np
from contextlib import ExitStack
import concourse.bass as bass
import concourse.mybir as mybir
from concourse.bass_utils import run_bass_kernel_spmd

F32 = mybir.dt.float32
BF16 = mybir.dt.bfloat16
U32 = mybir.dt.uint32
ALU = mybir.AluOpType
AF = mybir.ActivationFunctionType

D = 2048
KC = 16
TL = 4096
TC = 256
T = TL + TC
DEPTH = 4
DN_ALPHA = (2 * DEPTH) ** 0.25
LN_EPS = 1e-5
NDMA_SEM = 6
N_CORES = 4
import os
P1_STOP = int(os.environ.get('P1_STOP', '99'))
NE = 16384


class Buf:
    def __init__(self, t, name):
        self.t = t
        self.name = name
        self.w = None
        self.r = {}

    def __getitem__(self, idx):
        return self.t[idx]


class K:
    def __init__(self, nc, es):
        self.nc = nc
        self.es = es
        self.scopes = [es]
        self.eng = {"pe": nc.tensor, "dve": nc.vector, "act": nc.scalar, "pool": nc.gpsimd, "sp": nc.sync}
        self.sem = {}
        self.cnt = {}
        for e in ["pe", "dve", "act", "pool"]:
            self.sem[e] = es.enter_context(nc.semaphore("s_" + e))
            self.cnt[e] = 0
        self.dq = {}
        for q in ["sp", "pool", "act"]:
            for i in range(NDMA_SEM):
                key = f"d_{q}{i}"
                self.sem[key] = es.enter_context(nc.semaphore(key))
                self.cnt[key] = 0
            self.dq[q] = 0
        self.seen = {}
        self.n_ins = 0

    def _uid(self, name):
        self.uid = getattr(self, "uid", 0) + 1
        return f"{name}_{self.uid}"

    def sb(self, name, shape, dt=F32):
        name = self._uid(name)
        return Buf(self.scopes[-1].enter_context(self.nc.sbuf_tensor(name, list(shape), dt)), name)

    def ps(self, name, shape, dt=F32):
        name = self._uid(name)
        return Buf(self.scopes[-1].enter_context(self.nc.psum_tensor(name, list(shape), dt)), name)

    def dram(self, name, shape, dt=F32, kind="Internal"):
        return Buf(self.nc.dram_tensor(name, list(shape), dt, kind=kind), name)

    def _wait(self, engname, key, count):
        if count <= 0 or self.seen.get((engname, key), 0) >= count:
            return
        self.seen[(engname, key)] = count
        mult = 16 if key.startswith("d_") else 1
        self.eng[engname].wait_ge(self.sem[key], count * mult)

    def _deps(self, engname, reads, writes, ownkey):
        need = {}
        for b in reads:
            if b.w is not None:
                need[b.w[0]] = max(need.get(b.w[0], 0), b.w[1])
        for b in writes:
            if b.w is not None:
                need[b.w[0]] = max(need.get(b.w[0], 0), b.w[1])
            for kk, c in b.r.items():
                need[kk] = max(need.get(kk, 0), c)
        for kk, c in need.items():
            if kk == ownkey and engname == "pe":
                continue
            self._wait(engname, kk, c)

    def _mark(self, key, count, reads, writes):
        for b in reads:
            b.r[key] = max(b.r.get(key, 0), count)
        for b in writes:
            b.w = (key, count)
            b.r = {}

    def op(self, engname, emit, reads=(), writes=()):
        self._deps(engname, reads, writes, engname)
        ins = emit(self.eng[engname])
        self.cnt[engname] += 1
        ins.then_inc(self.sem[engname], 1)
        self._mark(engname, self.cnt[engname], reads, writes)
        self.n_ins += 1
        return ins

    def dma(self, q, out, in_, reads=(), writes=(), **kw):
        i = self.dq[q] % NDMA_SEM
        self.dq[q] += 1
        key = f"d_{q}{i}"
        self._wait(q, key, self.cnt[key])
        self._deps(q, reads, writes, None)
        ins = self.eng[q].dma_start(out=out, in_=in_, **kw)
        self.cnt[key] += 1
        ins.then_inc(self.sem[key], 16)
        self._mark(key, self.cnt[key], reads, writes)
        self.n_ins += 1
        return ins

    def barrier(self):
        for e in ["pe", "dve", "act", "pool", "sp"]:
            for key, c in self.cnt.items():
                self._wait(e, key, c)

    def open(self):
        s = ExitStack()
        self.scopes.append(s)
        return s

    def close(self):
        self.barrier()
        self.scopes.pop().close()


def colview(w2d, c0, n):
    if isinstance(w2d, Buf):
        w2d = w2d.t.ap()
    return w2d.rearrange("(c p) n -> p c n", p=128)[:, :, c0:c0 + n]


def build(layers_to_run, peer_on=True, mixer_on=True, p2_on=True, ntok_dbg=None):
    nc = bass.Bass("TRN2", target_bir_lowering=False)
    es = ExitStack()
    with es:
        k = K(nc, es)
        ein = lambda name, shape: k.dram(name, shape, F32, kind="ExternalInput")
        xin = ein("xin", [T, D])
        cT = ein("cT", [128, KC * 2])
        ident_d = ein("ident", [128, 128])
        mod_w = {l: ein(f"mod_w_{l}", [D, 6 * D]) for l in layers_to_run}
        mod_b = ein("mod_b", [DEPTH, 6 * D])
        ln_g = ein("ln_g", [DEPTH * 2, D])
        ln_b = ein("ln_b", [DEPTH * 2, D])
        conv_w_in = ein("conv_w_in", [2, D, 2 * D])
        conv_w_out = ein("conv_w_out", [2, D, D])
        conv_dwT = ein("conv_dwT", [2, 128, KC * 31])
        conv_lngT = ein("conv_lngT", [2, 128, KC])
        conv_lnbT = ein("conv_lnbT", [2, 128, KC])
        peer_w_q = ein("peer_w_q", [DEPTH, D, 1024])
        peer_keysT = ein("peer_keysT", [DEPTH, 128, 2048])
        peer_uT = {l: ein(f"peer_uT_{l}", [D, NE]) for l in layers_to_run}
        peer_v = {l: ein(f"peer_v_{l}", [NE, D]) for l in layers_to_run}
        hconst = ein("hconst", [128, 258])
        hg_w = {nm: ein("hg_" + nm, [D, D]) for nm in ("w_q", "w_i", "w_g", "w_f0", "w_f1", "w_o")} if 1 in layers_to_run else {}
        hg_norm_g = ein("hg_norm_g", [1, D])
        hg_lb_logits = ein("hg_lb_logits", [2 * DEPTH, D])
        DI = 4096
        sconst = ein("sconst", [128, 512])
        if 2 in layers_to_run:
            m_w_in = ein("m_w_in", [D, 10368]); m_w_out = ein("m_w_out", [DI, D])
        m_convT = ein("m_convT", [128, 48 * 5]); m_convbT = ein("m_convbT", [128, 48])
        m_dt_bias = ein("m_dt_bias", [2, 64]); m_A_log = ein("m_A_log", [2, 64]); m_D = ein("m_D", [1, 64]); m_norm_g = ein("m_norm_g", [1, DI])
        XBCT = k.dram("XBCT", [6144, T], F32); ZSd = k.dram("ZSd", [TL, DI], F32); DTd = k.dram("DTd", [T, 128], F32)
        XTOK = k.dram("XTOK", [T, DI], F32); BTOK = k.dram("BTOK", [T, 1024], BF16)
        BTd = k.dram("BTd", [1024, T], BF16); CTd = k.dram("CTd", [1024, T], BF16)
        Yd_ = k.dram("Ysc", [TL, DI], F32); GTs = k.dram("GTs", [DI, TL], BF16); YO = k.dram("YO", [TL, D], F32)
        out = k.dram("out", [TL, D], F32, kind="ExternalOutput")
        out_ctx = k.dram("out_ctx", [TC, D], F32, kind="ExternalOutput")
        Qd = k.dram("Qd", [T, D], F32); GTd = k.dram("GTd", [T, D], F32); Vd = k.dram("Vd", [T, D], BF16)
        LFd = [k.dram(f"LFd{d_}", [T, D], F32) for d_ in range(2)]
        KDd = [k.dram(f"KDd{d_}", [T, D], F32) for d_ in range(2)]
        Od = k.dram("Od", [T, D], F32)
        Gd = k.dram("Gd", [T, NE], BF16)
        UTb = k.dram("UTb", [D, NE], BF16)
        Vbf = k.dram("Vbf", [NE, D], BF16)
        dbgG = k.dram("dbgG", [128, NE], BF16, kind="ExternalOutput") if ntok_dbg else None

        X = k.dram("X", [T, D], F32)
        HT = k.dram("HT", [D, T], BF16)
        U = k.dram("U", [D, T], F32)
        UC = k.dram("UC", [D, T], F32)
        MR = k.dram("MR", [DEPTH, 2, 6 * D], F32)

        ident = k.sb("identsb", [128, 128], F32)
        ones = k.sb("ones", [128, 128], F32)
        k.dma("sp", ident[:, :], ident_d[:, :], writes=[ident])
        k.op("dve", lambda e: e.memset(ones[:, :], 1.0), writes=[ones])
        mcol = k.sb("mcol", [128, 2 * 6 * KC], F32)
        eps_t = k.sb("eps_t", [128, 1], F32)
        k.op("dve", lambda e: e.memset(eps_t[:, :], LN_EPS), writes=[eps_t])

        for i in range(4):
            r0 = i * (T // 4)
            k.dma(["sp", "pool"][i % 2], X[r0:r0 + T // 4, :], xin[r0:r0 + T // 4, :])
        k.open()
        sT = k.sb("sT", [128, KC * 2], F32)
        k.dma("sp", sT[:, :], cT[:, :], writes=[sT])
        k.op("act", lambda e: e.activation(out=sT[:, :], in_=sT[:, :], func=AF.Silu), reads=[sT], writes=[sT])
        wmb = [k.sb(f"wmb{i}", [128, KC * 512], F32) for i in range(2)]
        bb = [k.sb(f"bb{i}", [2, 512], F32) for i in range(2)]
        mo = [k.sb(f"mo{i}", [2, 512], F32) for i in range(2)]
        mps = [k.ps(f"mps{i}", [2, 512], F32) for i in range(2)]
        it = 0
        for l in layers_to_run:
            for nb in range(24):
                w = wmb[it % 2]; b_ = bb[it % 2]; o_ = mo[it % 2]; p_ = mps[it % 2]
                wv = w.t[:, :].rearrange("p (c n) -> p c n", c=KC)
                k.dma(["sp", "pool"][it % 2], wv, colview(mod_w[l], nb * 512, 512), writes=[w])
                k.dma("act", b_[:, :], mod_b[l:l + 1, nb * 512:(nb + 1) * 512].to_broadcast([2, 512]), writes=[b_])
                for kc in range(KC):
                    k.op("pe", lambda e, kc=kc: e.matmul(p_[:, :], lhsT=sT[:, kc * 2:kc * 2 + 2], rhs=wv[:, kc, :],
                                                         start=(kc == 0), stop=(kc == KC - 1)),
                         reads=[sT, w], writes=[p_])
                k.op("dve", lambda e: e.tensor_tensor(out=o_[:, :], in0=p_[:, :], in1=b_[:, :], op=ALU.add),
                     reads=[p_, b_], writes=[o_])
                k.dma("act", MR[l, :, nb * 512:(nb + 1) * 512], o_[:, :], reads=[o_])
                it += 1
        k.close()

        def load_mcol(l):
            k.barrier()
            i = 0
            for r in range(2):
                for s in range(6):
                    src = MR[l, r, s * D:(s + 1) * D].rearrange("(c p) -> p c", p=128)
                    k.dma(["sp", "pool", "act"][i % 3], mcol[:, (r * 6 + s) * KC:(r * 6 + s + 1) * KC], src, writes=[mcol],
                          allow_slow_non_contiguous=True)
                    i += 1
            for r in range(2):
                for s in (1, 4):
                    sl = mcol[:, (r * 6 + s) * KC:(r * 6 + s + 1) * KC]
                    k.op("dve", lambda e, sl=sl: e.tensor_scalar(out=sl, in0=sl, scalar1=1.0, scalar2=None, op0=ALU.add),
                         reads=[mcol], writes=[mcol])

        def mc(r, s, kc):
            c = (r * 6 + s) * KC + kc
            return mcol[:, c:c + 1]

        def build_HT(ntok, seg_shift, seg_scale):
            k.open()
            xt = [k.sb(f"xt{i}", [128, D], F32) for i in range(2)]
            ht = [k.sb(f"ht{i}", [128, KC * 128], BF16) for i in range(2)]
            pt = [k.ps(f"pt{i}", [128, 512], F32) for i in range(4)]
            HTv = HT.t.ap().rearrange("(c p) t -> p c t", p=128)
            for tt in range(ntok // 128):
                r = 0 if tt < TL // 128 else 1
                x_ = xt[tt % 2]; h_ = ht[tt % 2]
                k.dma(["sp", "pool"][tt % 2], x_[:, :], X[tt * 128:(tt + 1) * 128, :], writes=[x_])
                for kc in range(KC):
                    p_ = pt[kc % 4]
                    k.op("pe", lambda e, kc=kc, p_=p_: e.transpose(out=p_[:, 0:128], in_=x_[:, kc * 128:(kc + 1) * 128],
                                                                   identity=ident[:, :]),
                         reads=[x_, ident], writes=[p_])
                    k.op("act", lambda e, kc=kc, p_=p_: e.activation(out=h_[:, kc * 128:(kc + 1) * 128], in_=p_[:, 0:128],
                                                                     func=AF.Identity, scale=mc(r, seg_scale, kc),
                                                                     bias=mc(r, seg_shift, kc)),
                         reads=[p_, mcol], writes=[h_])
                k.dma("act", HTv[:, :, tt * 128:(tt + 1) * 128], h_.t[:, :].rearrange("p (c t) -> p c t", c=KC), reads=[h_])
            k.close()

        def residual_ln(y_ps, x_, m_b, g_b, b_b, tmp, stats, mv, rstd, yget=None):
            if yget is None:
                yget = lambda q: (y_ps[q][:, :], y_ps[q])
            for q in range(4):
                sl = slice(q * 512, (q + 1) * 512)
                yap, ybuf = yget(q)
                k.op("dve", lambda e, sl=sl, yap=yap: e.tensor_tensor(out=tmp[:, sl], in0=yap, in1=m_b[:, sl], op=ALU.mult),
                     reads=[ybuf, m_b], writes=[tmp])
            k.op("dve", lambda e: e.scalar_tensor_tensor(out=tmp[:, :], in0=x_[:, :], scalar=float(DN_ALPHA), in1=tmp[:, :],
                                                         op0=ALU.mult, op1=ALU.add), reads=[x_, tmp], writes=[tmp])
            for q in range(4):
                k.op("dve", lambda e, q=q: e.bn_stats(out=stats[:, q * 6:(q + 1) * 6], in_=tmp[:, q * 512:(q + 1) * 512]),
                     reads=[tmp], writes=[stats])
            k.op("dve", lambda e: e.bn_aggr(out=mv[:, :], in_=stats[:, :].rearrange("p (a b) -> p a b", b=6)), reads=[stats], writes=[mv])
            k.op("act", lambda e: e.activation(out=rstd[:, :], in_=mv[:, 1:2], func=AF.Sqrt, bias=eps_t[:, 0:1], scale=1.0),
                 reads=[mv, eps_t], writes=[rstd])
            k.op("dve", lambda e: e.reciprocal(out=rstd[:, :], in_=rstd[:, :]), reads=[rstd], writes=[rstd])
            k.op("dve", lambda e: e.tensor_scalar(out=tmp[:, :], in0=tmp[:, :], scalar1=mv[:, 0:1], scalar2=rstd[:, 0:1],
                                                  op0=ALU.subtract, op1=ALU.mult), reads=[tmp, mv, rstd], writes=[tmp])
            k.op("pool", lambda e: e.tensor_tensor(out=tmp[:, :], in0=tmp[:, :], in1=g_b[:, :], op=ALU.mult), reads=[tmp, g_b], writes=[tmp])
            k.op("pool", lambda e: e.tensor_tensor(out=x_[:, :], in0=tmp[:, :], in1=b_b[:, :], op=ALU.add), reads=[tmp, b_b], writes=[x_])

        def load_bcast(dst, src_row):
            k.dma("act", dst[:, :], src_row.to_broadcast([128, D]), writes=[dst])

        def conv_layer(l, j, with_ctx):
            ntok = T if with_ctx else TL
            load_mcol(l)
            build_HT(ntok, 0, 1)
            k.open()
            w1 = [k.sb(f"w1_{i}", [128, KC * 512], BF16) for i in range(2)]
            w2 = [k.sb(f"w2_{i}", [128, KC * 512], BF16) for i in range(2)]
            hT = [k.sb(f"hT{i}", [128, KC * 512], BF16) for i in range(2)]
            pa = [k.ps(f"pa{i}", [128, 512], F32) for i in range(2)]
            pb = [k.ps(f"pb{i}", [128, 512], F32) for i in range(2)]
            sg = [k.sb(f"sg{i}", [128, 512], F32) for i in range(2)]
            ub = [k.sb(f"ub{i}", [128, 512], F32) for i in range(2)]
            HTv = HT.t.ap().rearrange("(c p) t -> p c t", p=128)
            tiles = [(t0, min(512, ntok - t0)) for t0 in range(0, ntok, 512)]
            it = 0
            for cb in range(4):
                w1_ = w1[cb % 2]; w2_ = w2[cb % 2]
                w1v = w1_.t[:, :].rearrange("p (c n) -> p c n", c=KC)
                w2v = w2_.t[:, :].rearrange("p (c n) -> p c n", c=KC)
                k.dma("pool", w1v, colview(conv_w_in[j], cb * 512, 512), writes=[w1_])
                k.dma("pool", w2v, colview(conv_w_in[j], D + cb * 512, 512), writes=[w2_])
                for (t0, n) in tiles:
                    h_ = hT[it % 2]
                    hv = h_.t[:, :].rearrange("p (c t) -> p c t", c=KC)
                    k.dma("sp", hv[:, :, 0:n], HTv[:, :, t0:t0 + n], writes=[h_])
                    for jj in range(4):
                        pa_ = pa[jj % 2]; pb_ = pb[jj % 2]; sg_ = sg[jj % 2]; ub_ = ub[jj % 2]
                        for kc in range(KC):
                            k.op("pe", lambda e, kc=kc: e.matmul(pa_[:, 0:n], lhsT=w1v[:, kc, jj * 128:(jj + 1) * 128], rhs=hv[:, kc, 0:n],
                                                                 start=(kc == 0), stop=(kc == KC - 1)), reads=[w1_, h_], writes=[pa_])
                        for kc in range(KC):
                            k.op("pe", lambda e, kc=kc: e.matmul(pb_[:, 0:n], lhsT=w2v[:, kc, jj * 128:(jj + 1) * 128], rhs=hv[:, kc, 0:n],
                                                                 start=(kc == 0), stop=(kc == KC - 1)), reads=[w2_, h_], writes=[pb_])
                        k.op("act", lambda e: e.activation(out=sg_[:, 0:n], in_=pb_[:, 0:n], func=AF.Sigmoid), reads=[pb_], writes=[sg_])
                        k.op("dve", lambda e: e.tensor_tensor(out=ub_[:, 0:n], in0=pa_[:, 0:n], in1=sg_[:, 0:n], op=ALU.mult),
                             reads=[pa_, sg_], writes=[ub_])
                        ch0 = (cb * 4 + jj) * 128
                        k.dma("act", U[ch0:ch0 + 128, t0:t0 + n], ub_[:, 0:n], reads=[ub_])
                    it += 1
            k.close()
            k.open()
            dw = k.sb("dw", [128, KC * 31], F32)
            k.dma("sp", dw[:, :], conv_dwT[j], writes=[dw])
            uin = [k.sb(f"uin{i}", [128, T], F32) for i in range(2)]
            acc = [k.sb(f"acc{i}", [128, T], F32) for i in range(2)]
            for c in range(KC):
                u_ = uin[c % 2]; a_ = acc[c % 2]
                eng = "dve"
                k.dma(["sp", "pool"][c % 2], u_[:, 0:ntok], U[c * 128:(c + 1) * 128, 0:ntok], writes=[u_])
                order = [15] + [kk for kk in range(31) if kk != 15]
                for kk in order:
                    o = kk - 15
                    wk = dw[:, c * 31 + kk:c * 31 + kk + 1]
                    lo_o, hi_o = max(0, -o), 64 - max(0, o)
                    lo_i, hi_i = max(0, o), 64 - max(0, -o)
                    regions = []
                    if c < KC // 2:
                        av = a_.t[:, 0:TL].rearrange("p (r c) -> p r c", c=64)
                        uv = u_.t[:, 0:TL].rearrange("p (r c) -> p r c", c=64)
                        regions.append((av[:, :, lo_o:hi_o], uv[:, :, lo_i:hi_i]))
                    else:
                        regions.append((a_[:, lo_o * 64:hi_o * 64], u_[:, lo_i * 64:hi_i * 64]))
                    if with_ctx:
                        lo_oc, hi_oc = max(0, -o), TC - max(0, o)
                        lo_ic, hi_ic = max(0, o), TC - max(0, -o)
                        regions.append((a_[:, TL + lo_oc:TL + hi_oc], u_[:, TL + lo_ic:TL + hi_ic]))
                    for (oa, ia) in regions:
                        if kk == 15:
                            k.op(eng, lambda e, oa=oa, ia=ia: e.tensor_scalar(out=oa, in0=ia, scalar1=wk, scalar2=None, op0=ALU.mult),
                                 reads=[u_, dw], writes=[a_])
                        else:
                            k.op(eng, lambda e, oa=oa, ia=ia: e.scalar_tensor_tensor(out=oa, in0=ia, scalar=wk, in1=oa,
                                                                                    op0=ALU.mult, op1=ALU.add),
                                 reads=[u_, dw, a_], writes=[a_])
                k.dma("act", UC[c * 128:(c + 1) * 128, 0:ntok], a_[:, 0:ntok], reads=[a_])
            k.close()
            k.open()
            wo = k.sb("wo", [128, KC * D], BF16)
            wov = wo.t[:, :].rearrange("p (c n) -> p c n", c=KC)
            for q in range(4):
                k.dma("pool", wov[:, :, q * 512:(q + 1) * 512], colview(conv_w_out[j], q * 512, 512), writes=[wo])
            lg = k.sb("lg", [128, KC], F32); lb = k.sb("lb", [128, KC], F32)
            k.dma("sp", lg[:, :], conv_lngT[j], writes=[lg])
            k.dma("sp", lb[:, :], conv_lnbT[j], writes=[lb])
            m2b = [k.sb(f"m2b{r}", [128, D], F32) for r in range(2)]
            load_bcast(m2b[0], MR[l, 0:1, 2 * D:3 * D])
            load_bcast(m2b[1], MR[l, 1:2, 2 * D:3 * D])
            g_b = k.sb("g_b", [128, D], F32); b_b = k.sb("b_b", [128, D], F32)
            load_bcast(g_b, ln_g[2 * l:2 * l + 1, :]); load_bcast(b_b, ln_b[2 * l:2 * l + 1, :])
            uc = [k.sb(f"uc{i}", [128, KC * 256], F32) for i in range(2)]
            sq = k.sb("sq", [128, 256], F32)
            zT = [k.sb(f"zT{i}", [128, KC * 256], BF16) for i in range(2)]
            ps_s = k.ps("ps_s", [128, 256], F32); ps_q = k.ps("ps_q", [128, 256], F32)
            mean = k.sb("mean", [128, 256], F32); rs = k.sb("rs", [128, 256], F32); t1 = k.sb("t1", [128, 256], F32)
            yps = [k.ps(f"yps{q}", [128, 512], F32) for q in range(4)]
            xt = [k.sb(f"xd{i}", [128, D], F32) for i in range(2)]
            tmp = k.sb("tmpd", [128, D], F32)
            stats = k.sb("stats", [128, 24], F32); mv = k.sb("mv", [128, 2], F32); rstd = k.sb("rstd", [128, 1], F32)
            UCv = UC.t.ap().rearrange("(c p) t -> p c t", p=128)
            for ti, t0 in enumerate(range(0, ntok, 256)):
                u_ = uc[ti % 2]; z_ = zT[ti % 2]
                uv = u_.t[:, :].rearrange("p (c t) -> p c t", c=KC)
                zv = z_.t[:, :].rearrange("p (c t) -> p c t", c=KC)
                k.dma(["sp", "pool"][ti % 2], uv, UCv[:, :, t0:t0 + 256], writes=[u_])
                for kc in range(KC):
                    k.op("pe", lambda e, kc=kc: e.matmul(ps_s[:, :], lhsT=ones[:, :], rhs=uv[:, kc, :], start=(kc == 0), stop=(kc == KC - 1)),
                         reads=[ones, u_], writes=[ps_s])
                for kc in range(KC):
                    k.op("act", lambda e, kc=kc: e.activation(out=sq[:, :], in_=uv[:, kc, :], func=AF.Square), reads=[u_], writes=[sq])
                    k.op("pe", lambda e, kc=kc: e.matmul(ps_q[:, :], lhsT=ones[:, :], rhs=sq[:, :], start=(kc == 0), stop=(kc == KC - 1)),
                         reads=[ones, sq], writes=[ps_q])
                k.op("dve", lambda e: e.tensor_scalar(out=mean[:, :], in0=ps_s[:, :], scalar1=1.0 / D, scalar2=None, op0=ALU.mult),
                     reads=[ps_s], writes=[mean])
                k.op("dve", lambda e: e.tensor_tensor(out=t1[:, :], in0=mean[:, :], in1=mean[:, :], op=ALU.mult), reads=[mean], writes=[t1])
                k.op("dve", lambda e: e.scalar_tensor_tensor(out=rs[:, :], in0=ps_q[:, :], scalar=1.0 / D, in1=t1[:, :],
                                                             op0=ALU.mult, op1=ALU.subtract), reads=[ps_q, t1], writes=[rs])
                k.op("act", lambda e: e.activation(out=rs[:, :], in_=rs[:, :], func=AF.Sqrt, bias=eps_t[:, 0:1], scale=1.0),
                     reads=[rs, eps_t], writes=[rs])
                k.op("dve", lambda e: e.reciprocal(out=rs[:, :], in_=rs[:, :]), reads=[rs], writes=[rs])
                for kc in range(KC):
                    k.op("dve", lambda e, kc=kc: e.tensor_tensor(out=t1[:, :], in0=uv[:, kc, :], in1=mean[:, :], op=ALU.subtract),
                         reads=[u_, mean], writes=[t1])
                    k.op("pool", lambda e, kc=kc: e.tensor_tensor(out=t1[:, :], in0=t1[:, :], in1=rs[:, :], op=ALU.mult),
                         reads=[t1, rs], writes=[t1])
                    k.op("act", lambda e, kc=kc: e.activation(out=zv[:, kc, :], in_=t1[:, :], func=AF.Silu, scale=lg[:, kc:kc + 1],
                                                              bias=lb[:, kc:kc + 1]), reads=[t1, lg, lb], writes=[z_])
                for sub in range(2):
                    tok0 = t0 + sub * 128
                    r = 0 if tok0 < TL else 1
                    x_ = xt[sub]
                    k.dma("sp", x_[:, :], X[tok0:tok0 + 128, :], writes=[x_])
                    for q in range(4):
                        for kc in range(KC):
                            k.op("pe", lambda e, kc=kc, q=q: e.matmul(yps[q][:, :], lhsT=zv[:, kc, sub * 128:(sub + 1) * 128],
                                                                      rhs=wov[:, kc, q * 512:(q + 1) * 512],
                                                                      start=(kc == 0), stop=(kc == KC - 1)), reads=[z_, wo], writes=[yps[q]])
                    residual_ln(yps, x_, m2b[r], g_b, b_b, tmp, stats, mv, rstd)
                    k.dma("act", X[tok0:tok0 + 128, :], x_[:, :], reads=[x_])
            k.close()


        identb = k.sb("identb", [128, 128], BF16)
        k.op("dve", lambda e: e.tensor_copy(out=identb[:, :], in_=ident[:, :]), reads=[ident], writes=[identb])

        def peer_layer(l, with_ctx):
            ntok = T if with_ctx else TL
            if ntok_dbg:
                ntok = ntok_dbg
            build_HT(ntok, 3, 4)
            HTv = HT.t.ap().rearrange("(c p) t -> p c t", p=128)
            k.open()
            for c4 in range(4):
                k.dma("pool", UTb[c4 * 512:(c4 + 1) * 512, :], peer_uT[l][c4 * 512:(c4 + 1) * 512, :])
                k.dma("pool", Vbf[c4 * 4096:(c4 + 1) * 4096, :], peer_v[l][c4 * 4096:(c4 + 1) * 4096, :])
            wq = k.sb("wq", [128, KC * 1024], BF16)
            wqv = wq.t[:, :].rearrange("p (c n) -> p c n", c=KC)
            for hf in range(2):
                k.dma("pool", wqv[:, :, hf * 512:(hf + 1) * 512], colview(peer_w_q[l], hf * 512, 512), writes=[wq])
            kT = k.sb("kT", [128, 2048], F32)
            k.dma("sp", kT[:, :], peer_keysT[l], writes=[kT])
            hts = [k.sb(f"pht{i}", [128, KC * 128], BF16) for i in range(2)]
            qT = k.sb("qT", [128, 8 * 128], F32)
            pq = [k.ps(f"pq{i}", [128, 128], F32) for i in range(2)]
            pss = [k.ps(f"pss{i}", [128, 512], F32) for i in range(4)]
            sb_s = k.sb("s", [128, 16 * 128], F32); sb_s2 = k.sb("s2", [128, 16 * 128], F32); sb_E = k.sb("E", [128, 16 * 128], F32)
            sv = sb_s.t[:, :].rearrange("p (g n) -> p g n", g=16)
            s2v = sb_s2.t[:, :].rearrange("p (g n) -> p g n", g=16)
            Ev = sb_E.t[:, :].rearrange("p (g n) -> p g n", g=16)
            mx = k.sb("mx", [128, 256], F32); mx3 = mx.t[:, :].rearrange("p (g a) -> p g a", g=16)
            negm = k.sb("negm", [128, 16], F32)
            cand = k.sb("cand", [128, 8 * 256], F32); cand2 = k.sb("cand2", [128, 8 * 256], F32)
            ts = k.sb("ts", [128, 128], F32); ts3 = ts.t[:, :].rearrange("p (h a) -> p h a", h=8)
            e16 = k.sb("e16", [128, 128], F32); e163 = e16.t[:, :].rearrange("p (h a) -> p h a", h=8)
            Z = k.sb("Z", [128, 8], F32); thr = k.sb("thr", [128, 8], F32)
            th = k.sb("th", [128, 8 * 128], F32); th3 = th.t[:, :].rearrange("p (h i) -> p h i", h=8)
            A1 = k.sb("A1", [128, 8 * 128], F32); A13 = A1.t[:, :].rearrange("p (h i) -> p h i", h=8)
            Wb = {e: k.sb("W" + e, [128, 2048], F32) for e in ("dve", "pool")}
            Gb = {e: k.sb("Gb" + e, [128, 2048], F32) for e in ("dve", "pool")}
            Wp2 = cand2
            Gt = [k.sb(f"Gt{i}", [128, NE], BF16) for i in range(2)]
            B3 = [128, 16, 128]
            for tt in range(ntok // 128):
                h_ = hts[tt % 2]; hv = h_.t[:, :].rearrange("p (c t) -> p c t", c=KC)
                k.dma("sp", hv, HTv[:, :, tt * 128:(tt + 1) * 128], writes=[h_])
                for h in range(8):
                    p_ = pq[h % 2]
                    for kc in range(KC):
                        k.op("pe", lambda e, kc=kc: e.matmul(p_[:, :], lhsT=wqv[:, kc, h * 128:(h + 1) * 128], rhs=hv[:, kc, :],
                                                             start=(kc == 0), stop=(kc == KC - 1)), reads=[wq, h_], writes=[p_])
                    k.op("act", lambda e: e.activation(out=qT[:, h * 128:(h + 1) * 128], in_=p_[:, :], func=AF.Identity),
                         reads=[p_], writes=[qT])
                if P1_STOP <= 1:
                    continue
                for g in range(16):
                    h, p = g // 2, g % 2
                    ps_ = pss[g // 4]
                    k.op("pe", lambda e: e.matmul(ps_[:, (g % 4) * 128:(g % 4 + 1) * 128], lhsT=qT[:, h * 128:(h + 1) * 128],
                                                  rhs=kT[:, p * 1024 + h * 128:p * 1024 + (h + 1) * 128], start=True, stop=True),
                         reads=[qT, kT], writes=[ps_])
                for b4 in range(4):
                    k.op("act", lambda e: e.activation(out=sb_s[:, b4 * 512:(b4 + 1) * 512], in_=pss[b4][:, :], func=AF.Identity),
                         reads=[pss[b4]], writes=[sb_s])
                if P1_STOP <= 2:
                    continue
                for g in range(16):
                    k.op("dve", lambda e: e.max(out=mx3[:, g, 0:8], in_=sv[:, g, :]), reads=[sb_s], writes=[mx])
                    k.op("dve", lambda e: e.match_replace(out=s2v[:, g, :], in_to_replace=mx3[:, g, 0:8], in_values=sv[:, g, :],
                                                          imm_value=-1e30), reads=[mx, sb_s], writes=[sb_s2])
                    k.op("dve", lambda e: e.max(out=mx3[:, g, 8:16], in_=s2v[:, g, :]), reads=[sb_s2], writes=[mx])
                if P1_STOP <= 3:
                    continue
                k.op("dve", lambda e: e.tensor_scalar(out=negm[:, :], in0=mx3[:, :, 0], scalar1=-1.0, scalar2=None, op0=ALU.mult),
                     reads=[mx], writes=[negm])
                for g in range(16):
                    k.op("act", lambda e: e.activation(out=Ev[:, g, :], in_=sv[:, g, :], func=AF.Exp, bias=negm[:, g:g + 1], scale=1.0),
                         reads=[sb_s, negm], writes=[sb_E])
                if P1_STOP <= 4:
                    continue
                for h in range(8):
                    c3 = cand.t[:, h * 256:(h + 1) * 256].rearrange("p (a b) -> p a b", a=16)
                    k.op("dve", lambda e: e.tensor_tensor(out=c3, in0=mx3[:, 2 * h, :].unsqueeze(2).to_broadcast([128, 16, 16]),
                                                          in1=mx3[:, 2 * h + 1, :].unsqueeze(1).to_broadcast([128, 16, 16]), op=ALU.add),
                         reads=[mx], writes=[cand])
                    ch = cand[:, h * 256:(h + 1) * 256]; ch2 = cand2[:, h * 256:(h + 1) * 256]
                    k.op("dve", lambda e: e.max(out=ts3[:, h, 0:8], in_=ch), reads=[cand], writes=[ts])
                    k.op("dve", lambda e: e.match_replace(out=ch2, in_to_replace=ts3[:, h, 0:8], in_values=ch, imm_value=-1e30),
                         reads=[ts, cand], writes=[cand2])
                    k.op("dve", lambda e: e.max(out=ts3[:, h, 8:16], in_=ch2), reads=[cand2], writes=[ts])
                if P1_STOP <= 5:
                    continue
                k.op("dve", lambda e: e.tensor_tensor(out=e163, in0=ts3, in1=ts3[:, :, 0:1].to_broadcast([128, 8, 16]), op=ALU.subtract),
                     reads=[ts], writes=[e16])
                k.op("act", lambda e: e.activation(out=e16[:, :], in_=e16[:, :], func=AF.Exp), reads=[e16], writes=[e16])
                k.op("dve", lambda e: e.tensor_reduce(out=Z[:, :], in_=e163, axis=mybir.AxisListType.X, op=ALU.add), reads=[e16], writes=[Z])
                k.op("dve", lambda e: e.reciprocal(out=Z[:, :], in_=Z[:, :]), reads=[Z], writes=[Z])
                k.op("dve", lambda e: e.tensor_scalar(out=thr[:, :], in0=ts3[:, :, 15], scalar1=-2e-5, scalar2=None, op0=ALU.add),
                     reads=[ts], writes=[thr])
                for h in range(8):
                    k.op("dve", lambda e: e.tensor_scalar(out=th[:, h * 128:(h + 1) * 128], in0=sv[:, 2 * h, :], scalar1=-1.0,
                                                          scalar2=thr[:, h:h + 1], op0=ALU.mult, op1=ALU.add), reads=[sb_s, thr], writes=[th])
                    k.op("dve", lambda e: e.tensor_scalar(out=A1[:, h * 128:(h + 1) * 128], in0=Ev[:, 2 * h, :], scalar1=Z[:, h:h + 1],
                                                          scalar2=None, op0=ALU.mult), reads=[sb_E, Z], writes=[A1])
                if P1_STOP <= 6:
                    continue
                G_ = Gt[tt % 2]
                wi = 0
                for ib in range(8):
                    eng = "dve" if ib in (0, 2, 4, 6, 7) else "pool"
                    Gb_ = Gb[eng]
                    G3 = Gb_.t[:, :].rearrange("p (i j) -> p i j", i=16)
                    Gout = G_.t[:, ib * 2048:(ib + 1) * 2048].rearrange("p (i j) -> p i j", i=16)
                    i0 = ib * 16
                    for h in range(8):
                        if eng == "dve":
                            W_ = Wb["dve"]
                        else:
                            W_ = (Wb["pool"], Wp2)[wi % 2]; wi += 1
                        W3 = W_.t[:, :].rearrange("p (i j) -> p i j", i=16)
                        s2b = sv[:, 2 * h + 1:2 * h + 2, :].to_broadcast(B3)
                        E2b = Ev[:, 2 * h + 1:2 * h + 2, :].to_broadcast(B3)
                        thb = th3[:, h, i0:i0 + 16].unsqueeze(2).to_broadcast(B3)
                        A1b = A13[:, h, i0:i0 + 16].unsqueeze(2).to_broadcast(B3)
                        k.op("dve", lambda e: e.tensor_tensor(out=W3, in0=s2b, in1=thb, op=ALU.is_ge), reads=[sb_s, th], writes=[W_])
                        k.op(eng, lambda e: e.tensor_tensor(out=W3, in0=W3, in1=E2b, op=ALU.mult), reads=[W_, sb_E], writes=[W_])
                        if h == 0:
                            k.op(eng, lambda e: e.tensor_tensor(out=G3, in0=W3, in1=A1b, op=ALU.mult), reads=[W_, A1], writes=[Gb_])
                        else:
                            k.op(eng, lambda e: e.tensor_tensor(out=W3, in0=W3, in1=A1b, op=ALU.mult), reads=[W_, A1], writes=[W_])
                            if h < 7:
                                k.op(eng, lambda e: e.tensor_tensor(out=G3, in0=G3, in1=W3, op=ALU.add), reads=[W_, Gb_], writes=[Gb_])
                            else:
                                k.op(eng, lambda e: e.tensor_tensor(out=Gout, in0=G3, in1=W3, op=ALU.add), reads=[W_, Gb_], writes=[G_])
                k.dma("act", Gd[tt * 128:(tt + 1) * 128, :], G_[:, :], reads=[G_])
            k.close()
            if not p2_on:
                return
            k.open()
            m5b = [k.sb(f"m5b{r}", [128, D], F32) for r in range(2)]
            load_bcast(m5b[0], MR[l, 0:1, 5 * D:6 * D])
            load_bcast(m5b[1], MR[l, 1:2, 5 * D:6 * D])
            g_b = k.sb("pg_b", [128, D], F32); b_b = k.sb("pb_b", [128, D], F32)
            load_bcast(g_b, ln_g[2 * l + 1:2 * l + 2, :]); load_bcast(b_b, ln_b[2 * l + 1:2 * l + 2, :])
            hT4 = k.sb("hT4", [128, KC * 512], BF16); h4v = hT4.t[:, :].rearrange("p (c t) -> p c t", c=KC)
            acc = k.sb("acc", [128, 4 * D], F32)
            ub = [k.sb(f"uTb{i}", [128, KC * 512], BF16) for i in range(2)]
            vb = [k.sb(f"vb{i}", [128, 4 * D], BF16) for i in range(2)]
            gb = [k.sb(f"gblk{i}", [128, 512], BF16) for i in range(2)]
            gl = [k.sb(f"gl{i}", [128, 512], F32) for i in range(2)]
            Ab = [k.sb(f"Ab{i}", [128, 512], BF16) for i in range(2)]
            AT = [k.sb(f"AT{i}", [128, 512], BF16) for i in range(2)]
            pS = [k.ps(f"pS{i}", [128, 512], F32) for i in range(2)]
            pT = k.ps("pT", [128, 512], BF16)
            pO = [k.ps(f"pO{q}", [128, 512], F32) for q in range(4)]
            xt = k.sb("xp", [128, D], F32); tmp = k.sb("tmpp", [128, D], F32)
            stats = k.sb("pstats", [128, 24], F32); mv = k.sb("pmv", [128, 2], F32); rstd = k.sb("prstd", [128, 1], F32)
            it = 0
            for t0 in range(0, ntok, 512):
                nsub = min(512, ntok - t0) // 128
                k.dma("sp", h4v[:, :, 0:nsub * 128], HTv[:, :, t0:t0 + nsub * 128], writes=[hT4])
                for eb in range(NE // 512):
                    e0 = eb * 512
                    u_ = ub[eb % 2]; v_ = vb[eb % 2]
                    uv = u_.t[:, :].rearrange("p (c n) -> p c n", c=KC)
                    vv = v_.t[:, :].rearrange("p (c d) -> p c d", c=4)
                    k.dma("sp", uv, colview(UTb, e0, 512), writes=[u_])
                    k.dma("pool", vv, Vbf[e0:e0 + 512, :].rearrange("(c p) d -> p c d", p=128), writes=[v_])
                    for sub in range(nsub):
                        tok0 = t0 + sub * 128
                        pS_ = pS[it % 2]; gb_ = gb[it % 2]; gl_ = gl[it % 2]; A_ = Ab[it % 2]; AT_ = AT[it % 2]
                        k.dma("sp", gb_[:, :], Gd[tok0:tok0 + 128, e0:e0 + 512], writes=[gb_])
                        for kc in range(KC):
                            k.op("pe", lambda e, kc=kc: e.matmul(pS_[:, :], lhsT=h4v[:, kc, sub * 128:(sub + 1) * 128], rhs=uv[:, kc, :],
                                                                 start=(kc == 0), stop=(kc == KC - 1)), reads=[hT4, u_], writes=[pS_])
                        k.op("act", lambda e: e.activation(out=gl_[:, :], in_=pS_[:, :], func=AF.Gelu), reads=[pS_], writes=[gl_])
                        k.op("dve", lambda e: e.tensor_tensor(out=A_[:, :], in0=gl_[:, :], in1=gb_[:, :], op=ALU.mult),
                             reads=[gl_, gb_], writes=[A_])
                        for c in range(4):
                            k.op("pe", lambda e, c=c: e.transpose(out=pT[:, c * 128:(c + 1) * 128], in_=A_[:, c * 128:(c + 1) * 128],
                                                                  identity=identb[:, :]), reads=[A_, identb], writes=[pT])
                        k.op("act", lambda e: e.activation(out=AT_[:, :], in_=pT[:, :], func=AF.Identity), reads=[pT], writes=[AT_])
                        for q in range(4):
                            for c in range(4):
                                k.op("pe", lambda e, c=c, q=q: e.matmul(pO[q][:, :], lhsT=AT_[:, c * 128:(c + 1) * 128],
                                                                        rhs=vv[:, c, q * 512:(q + 1) * 512], start=(c == 0), stop=(c == 3)),
                                     reads=[AT_, v_], writes=[pO[q]])
                        for q in range(4):
                            asl = acc[:, sub * D + q * 512:sub * D + (q + 1) * 512]
                            if eb == 0:
                                k.op("dve", lambda e, q=q, asl=asl: e.tensor_copy(out=asl, in_=pO[q][:, :]), reads=[pO[q]], writes=[acc])
                            else:
                                k.op("dve", lambda e, q=q, asl=asl: e.tensor_tensor(out=asl, in0=asl, in1=pO[q][:, :], op=ALU.add),
                                     reads=[pO[q], acc], writes=[acc])
                        it += 1
                for sub in range(nsub):
                    tok0 = t0 + sub * 128
                    r = 0 if tok0 < TL else 1
                    k.dma("sp", xt[:, :], X[tok0:tok0 + 128, :], writes=[xt])
                    residual_ln(None, xt, m5b[r], g_b, b_b, tmp, stats, mv, rstd,
                                yget=lambda q, sub=sub: (acc[:, sub * D + q * 512:sub * D + (q + 1) * 512], acc))
                    k.dma("act", X[tok0:tok0 + 128, :], xt[:, :], reads=[xt])
            k.close()

        def hgrn_layer(l):
            ntok = T
            load_mcol(l)
            build_HT(ntok, 0, 1)
            HTv = HT.t.ap().rearrange("(c p) t -> p c t", p=128)
            k.open()
            lbB = [k.sb(f"lbB{d_}", [128, D], F32) for d_ in range(2)]
            omlB = [k.sb(f"omlB{d_}", [128, D], F32) for d_ in range(2)]
            ex = k.sb("lbex", [128, D], F32); den = k.sb("lbden", [128, D], F32)
            for d_ in range(2):
                for kk in range(DEPTH):
                    load_bcast(ex, hg_lb_logits[d_ * DEPTH + kk:d_ * DEPTH + kk + 1, :])
                    k.op("act", lambda e: e.activation(out=ex[:, :], in_=ex[:, :], func=AF.Exp), reads=[ex], writes=[ex])
                    if kk == 0:
                        k.op("dve", lambda e: e.tensor_copy(out=den[:, :], in_=ex[:, :]), reads=[ex], writes=[den])
                        k.op("dve", lambda e: e.memset(lbB[d_][:, :], 0.0), writes=[lbB[d_]])
                    else:
                        k.op("dve", lambda e: e.tensor_tensor(out=den[:, :], in0=den[:, :], in1=ex[:, :], op=ALU.add), reads=[ex, den], writes=[den])
                        if kk <= l:
                            k.op("dve", lambda e: e.tensor_tensor(out=lbB[d_][:, :], in0=lbB[d_][:, :], in1=ex[:, :], op=ALU.add),
                                 reads=[ex, lbB[d_]], writes=[lbB[d_]])
                k.op("dve", lambda e: e.reciprocal(out=den[:, :], in_=den[:, :]), reads=[den], writes=[den])
                k.op("dve", lambda e: e.tensor_tensor(out=lbB[d_][:, :], in0=lbB[d_][:, :], in1=den[:, :], op=ALU.mult), reads=[den, lbB[d_]], writes=[lbB[d_]])
                k.op("dve", lambda e: e.tensor_scalar(out=omlB[d_][:, :], in0=lbB[d_][:, :], scalar1=-1.0, scalar2=1.0, op0=ALU.mult, op1=ALU.add),
                     reads=[lbB[d_]], writes=[omlB[d_]])
            wb = [k.sb(f"hwb{i}", [128, KC * 512], BF16) for i in range(2)]
            hts = [k.sb(f"hht{i}", [128, KC * 128], BF16) for i in range(2)]
            pp = [k.ps(f"hpp{i}", [128, 512], F32) for i in range(2)]
            o1 = [k.sb(f"ho1_{i}", [128, 512], F32) for i in range(2)]
            o2 = [k.sb(f"ho2_{i}", [128, 512], F32) for i in range(2)]
            ob = [k.sb(f"hob{i}", [128, 512], BF16) for i in range(2)]
            it = 0; bi = 0
            for nm in ("w_q", "w_i", "w_g", "w_f0", "w_f1"):
                for cb in range(4):
                    w_ = wb[bi % 2]; bi += 1
                    wv = w_.t[:, :].rearrange("p (c n) -> p c n", c=KC)
                    k.dma("pool", wv, colview(hg_w[nm], cb * 512, 512), writes=[w_])
                    cs = slice(cb * 512, (cb + 1) * 512)
                    for tt in range(ntok // 128):
                        h_ = hts[it % 2]; p_ = pp[it % 2]; a_ = o1[it % 2]; b2_ = o2[it % 2]; c_ = ob[it % 2]
                        hv = h_.t[:, :].rearrange("p (c t) -> p c t", c=KC)
                        rows = slice(tt * 128, (tt + 1) * 128)
                        k.dma("sp", hv, HTv[:, :, rows], writes=[h_])
                        for kc in range(KC):
                            k.op("pe", lambda e, kc=kc: e.matmul(p_[:, :], lhsT=hv[:, kc, :], rhs=wv[:, kc, :], start=(kc == 0), stop=(kc == KC - 1)),
                                 reads=[h_, w_], writes=[p_])
                        if nm in ("w_q", "w_g"):
                            k.op("act", lambda e: e.activation(out=a_[:, :], in_=p_[:, :], func=AF.Silu), reads=[p_], writes=[a_])
                            k.dma("act", (Qd if nm == "w_q" else GTd)[rows, cs], a_[:, :], reads=[a_])
                        elif nm == "w_i":
                            k.op("act", lambda e: e.activation(out=c_[:, :], in_=p_[:, :], func=AF.Identity), reads=[p_], writes=[c_])
                            k.dma("act", Vd[rows, cs], c_[:, :], reads=[c_])
                        else:
                            d_ = 0 if nm == "w_f0" else 1
                            k.op("act", lambda e: e.activation(out=a_[:, :], in_=p_[:, :], func=AF.Sigmoid), reads=[p_], writes=[a_])
                            k.op("dve", lambda e: e.tensor_tensor(out=a_[:, :], in0=a_[:, :], in1=omlB[d_][:, cs], op=ALU.mult), reads=[a_, omlB[d_]], writes=[a_])
                            k.op("dve", lambda e: e.tensor_tensor(out=a_[:, :], in0=a_[:, :], in1=lbB[d_][:, cs], op=ALU.add), reads=[a_, lbB[d_]], writes=[a_])
                            k.op("dve", lambda e: e.tensor_scalar(out=b2_[:, :], in0=a_[:, :], scalar1=-1.0, scalar2=1.0, op0=ALU.mult, op1=ALU.add),
                                 reads=[a_], writes=[b2_])
                            k.dma("act", KDd[d_][rows, cs], b2_[:, :], reads=[b2_])
                            k.op("act", lambda e: e.activation(out=a_[:, :], in_=a_[:, :], func=AF.Ln), reads=[a_], writes=[a_])
                            k.dma("act", LFd[d_][rows, cs], a_[:, :], reads=[a_])
                        it += 1
            k.close()
            k.open()
            hc = k.sb("hc", [128, 258], F32)
            k.dma("sp", hc[:, :], hconst[:, :], writes=[hc])
            LF = k.sb("LF", [128, D], F32); KDt = k.sb("KDt", [128, D], F32); Qt_in = k.sb("Qin", [128, D], F32)
            Vt = k.sb("Vt", [128, D], BF16); Eb = k.sb("Ebuf", [128, D], F32)
            Qt = k.sb("Qt", [128, D], BF16); Kt = k.sb("Kt", [128, D], BF16); Kta = k.sb("Kta", [128, D], BF16); Ktb = k.sb("Ktb", [128, D], BF16)
            S = k.sb("S", [128, D], F32)
            Sb = [k.sb(f"Sb{i}", [128, D], BF16) for i in range(2)]
            eb = k.sb("eb", [128, 32], F32)
            QT = k.sb("QT", [128, 128], BF16); KT = k.sb("KT", [128, 128], BF16)
            QTa = k.sb("QTa", [128, 128], BF16); QTb = k.sb("QTb", [128, 128], BF16)
            attT = k.sb("attT", [128, 128], BF16)
            osb = k.sb("osb", [128, D], F32); ofw = k.sb("ofw", [128, D], F32)
            bcO = [k.ps(f"bcO{q}", [128, 512], F32) for q in range(4)]
            ebp = k.ps("ebp", [128, 32], F32)
            pT = k.ps("hpT", [128, 256], BF16)
            pA = k.ps("hpA", [128, 128], F32)
            pP = k.ps("hpP", [128, 256], F32)
            for d_ in range(2):
                tri = hc[:, d_ * 128:(d_ + 1) * 128]
                cf, csd = (0, 1) if d_ == 0 else (1, 0)
                k.op("dve", lambda e: e.memset(S[:, :], 0.0), writes=[S])
                k.op("dve", lambda e: e.memset(Sb[0][:, :], 0.0), writes=[Sb[0]])
                k.op("dve", lambda e: e.memset(QTa[:, :], 0.0), writes=[QTa])
                k.op("dve", lambda e: e.memset(QTb[:, :], 0.0), writes=[QTb])
                order = [32, 33] + list(range(32)) if d_ == 0 else [33, 32] + list(range(31, -1, -1))
                for tt in order:
                    rows = slice(tt * 128, (tt + 1) * 128)
                    k.dma("sp", LF[:, :], LFd[d_][rows, :], writes=[LF])
                    k.dma("pool", KDt[:, :], KDd[d_][rows, :], writes=[KDt])
                    k.dma("sp", Qt_in[:, :], Qd[rows, :], writes=[Qt_in])
                    k.dma("pool", Vt[:, :], Vd[rows, :], writes=[Vt])
                    for q in range(4):
                        k.op("pe", lambda e, q=q: e.matmul(bcO[q][:, :], lhsT=tri, rhs=LF[:, q * 512:(q + 1) * 512], start=True, stop=True),
                             reads=[hc, LF], writes=[bcO[q]])
                    for h in range(16):
                        k.op("pe", lambda e, h=h: e.matmul(ebp[:, 2 * h:2 * h + 2], lhsT=LF[:, h * 128:(h + 1) * 128], rhs=hc[:, 256:258],
                                                           start=True, stop=True), reads=[hc, LF], writes=[ebp])
                    k.op("act", lambda e: e.activation(out=eb[:, :], in_=ebp[:, :], func=AF.Exp), reads=[ebp], writes=[eb])
                    for q in range(4):
                        k.op("act", lambda e, q=q: e.activation(out=Eb[:, q * 512:(q + 1) * 512], in_=bcO[q][:, :], func=AF.Exp),
                             reads=[bcO[q]], writes=[Eb])
                    k.op("dve", lambda e: e.tensor_tensor(out=Qt[:, :], in0=Qt_in[:, :], in1=Eb[:, :], op=ALU.mult), reads=[Qt_in, Eb], writes=[Qt])
                    for q in range(4):
                        k.op("act", lambda e, q=q: e.activation(out=Eb[:, q * 512:(q + 1) * 512], in_=bcO[q][:, :], func=AF.Exp, scale=-1.0),
                             reads=[bcO[q]], writes=[Eb])
                    k.op("dve", lambda e: e.tensor_tensor(out=Kt[:, :], in0=KDt[:, :], in1=Eb[:, :], op=ALU.mult), reads=[KDt, Eb], writes=[Kt])
                    k.op("dve", lambda e: e.scalar_tensor_tensor(out=Kta[:, :], in0=KDt[:, :], scalar=hc[:, 256 + cf:257 + cf], in1=Eb[:, :],
                                                                 op0=ALU.mult, op1=ALU.mult), reads=[KDt, Eb, hc], writes=[Kta])
                    k.op("dve", lambda e: e.scalar_tensor_tensor(out=Ktb[:, :], in0=KDt[:, :], scalar=hc[:, 256 + csd:257 + csd], in1=Eb[:, :],
                                                                 op0=ALU.mult, op1=ALU.mult), reads=[KDt, Eb, hc], writes=[Ktb])
                    for h in range(16):
                        hs = slice(h * 128, (h + 1) * 128)
                        k.op("pe", lambda e: e.transpose(out=pT[:, 0:128], in_=Qt[:, hs], identity=identb[:, :]), reads=[Qt, identb], writes=[pT])
                        k.op("pe", lambda e: e.transpose(out=pT[:, 128:256], in_=Kt[:, hs], identity=identb[:, :]), reads=[Kt, identb], writes=[pT])
                        k.op("act", lambda e: e.activation(out=QT[:, :], in_=pT[:, 0:128], func=AF.Identity), reads=[pT], writes=[QT])
                        k.op("act", lambda e: e.activation(out=KT[:, :], in_=pT[:, 128:256], func=AF.Identity), reads=[pT], writes=[KT])
                        k.op("dve", lambda e: e.tensor_copy(out=QTa[:, cf * 64:(cf + 1) * 64], in_=pT[:, cf * 64:(cf + 1) * 64]), reads=[pT], writes=[QTa])
                        k.op("dve", lambda e: e.tensor_copy(out=QTb[:, csd * 64:(csd + 1) * 64], in_=pT[:, csd * 64:(csd + 1) * 64]), reads=[pT], writes=[QTb])
                        k.op("pe", lambda e: e.matmul(pA[:, :], lhsT=KT[:, :], rhs=QT[:, :], start=True, stop=True), reads=[KT, QT], writes=[pA])
                        k.op("dve", lambda e: e.tensor_tensor(out=attT[:, :], in0=pA[:, :], in1=tri, op=ALU.mult), reads=[pA, hc], writes=[attT])
                        k.op("pe", lambda e: e.matmul(pP[:, 0:128], lhsT=Kta[:, hs], rhs=Vt[:, hs], start=True, stop=True), reads=[Kta, Vt], writes=[pP])
                        k.op("dve", lambda e: e.tensor_tensor(out=S[:, hs], in0=S[:, hs], in1=pP[:, 0:128], op=ALU.add), reads=[pP, S], writes=[S])
                        k.op("dve", lambda e: e.tensor_scalar(out=S[:, hs], in0=S[:, hs], scalar1=eb[:, 2 * h + cf:2 * h + cf + 1], scalar2=None, op0=ALU.mult),
                             reads=[S, eb], writes=[S])
                        k.op("act", lambda e: e.activation(out=Sb[1][:, hs], in_=S[:, hs], func=AF.Identity), reads=[S], writes=[Sb[1]])
                        oq = bcO[h // 4]; osl = slice((h % 4) * 128, (h % 4 + 1) * 128)
                        k.op("pe", lambda e: e.matmul(oq[:, osl], lhsT=attT[:, :], rhs=Vt[:, hs], start=True, stop=False), reads=[attT, Vt], writes=[oq])
                        k.op("pe", lambda e: e.matmul(oq[:, osl], lhsT=QTa[:, :], rhs=Sb[0][:, hs], start=False, stop=False), reads=[QTa, Sb[0]], writes=[oq])
                        k.op("pe", lambda e: e.matmul(oq[:, osl], lhsT=QTb[:, :], rhs=Sb[1][:, hs], start=False, stop=True), reads=[QTb, Sb[1]], writes=[oq])
                        k.op("pe", lambda e: e.matmul(pP[:, 128:256], lhsT=Ktb[:, hs], rhs=Vt[:, hs], start=True, stop=True), reads=[Ktb, Vt], writes=[pP])
                        k.op("dve", lambda e: e.tensor_tensor(out=S[:, hs], in0=S[:, hs], in1=pP[:, 128:256], op=ALU.add), reads=[pP, S], writes=[S])
                        k.op("dve", lambda e: e.tensor_scalar(out=S[:, hs], in0=S[:, hs], scalar1=eb[:, 2 * h + csd:2 * h + csd + 1], scalar2=None, op0=ALU.mult),
                             reads=[S, eb], writes=[S])
                        k.op("act", lambda e: e.activation(out=Sb[0][:, hs], in_=S[:, hs], func=AF.Identity), reads=[S], writes=[Sb[0]])
                    if d_ == 0:
                        for q in range(4):
                            k.op("act", lambda e, q=q: e.activation(out=osb[:, q * 512:(q + 1) * 512], in_=bcO[q][:, :], func=AF.Identity),
                                 reads=[bcO[q]], writes=[osb])
                    else:
                        k.dma("sp", ofw[:, :], Od[rows, :], writes=[ofw])
                        for q in range(4):
                            k.op("dve", lambda e, q=q: e.tensor_tensor(out=osb[:, q * 512:(q + 1) * 512], in0=ofw[:, q * 512:(q + 1) * 512],
                                                                       in1=bcO[q][:, :], op=ALU.add), reads=[bcO[q], ofw], writes=[osb])
                    k.dma("act", Od[rows, :], osb[:, :], reads=[osb])
                k.barrier()
            k.close()
            k.open()
            wo = k.sb("hwo", [128, KC * D], BF16)
            wov = wo.t[:, :].rearrange("p (c n) -> p c n", c=KC)
            for q in range(4):
                k.dma("pool", wov[:, :, q * 512:(q + 1) * 512], colview(hg_w["w_o"], q * 512, 512), writes=[wo])
            m2b = [k.sb(f"hm2b{r}", [128, D], F32) for r in range(2)]
            load_bcast(m2b[0], MR[l, 0:1, 2 * D:3 * D]); load_bcast(m2b[1], MR[l, 1:2, 2 * D:3 * D])
            g_b = k.sb("hg_b", [128, D], F32); b_b = k.sb("hb_b", [128, D], F32); ngB = k.sb("ngB", [128, D], F32)
            load_bcast(g_b, ln_g[2 * l:2 * l + 1, :]); load_bcast(b_b, ln_b[2 * l:2 * l + 1, :]); load_bcast(ngB, hg_norm_g[0:1, :])
            ot = k.sb("hot", [128, D], F32); gtt = k.sb("hgt", [128, D], F32); sq = k.sb("hsq", [128, D], F32)
            ms = k.sb("hms", [128, 16], F32)
            gz = k.sb("hgz", [128, D], BF16); gT = k.sb("hgT", [128, D], BF16)
            pT2 = k.ps("hpT2", [128, 1024], BF16)
            yps = [k.ps(f"hyps{q}", [128, 512], F32) for q in range(4)]
            xt = k.sb("hxt", [128, D], F32); tmp = k.sb("htmp", [128, D], F32)
            stats = k.sb("hstats", [128, 24], F32); mv = k.sb("hmv", [128, 2], F32); rstd = k.sb("hrstd", [128, 1], F32)
            ot3 = ot.t[:, :].rearrange("p (h v) -> p h v", h=16); sq3 = sq.t[:, :].rearrange("p (h v) -> p h v", h=16)
            for tt in range(ntok // 128):
                rows = slice(tt * 128, (tt + 1) * 128)
                r = 0 if tt < TL // 128 else 1
                k.dma("sp", ot[:, :], Od[rows, :], writes=[ot])
                k.dma("pool", gtt[:, :], GTd[rows, :], writes=[gtt])
                k.dma("sp", xt[:, :], X[rows, :], writes=[xt])
                k.op("act", lambda e: e.activation(out=sq[:, :], in_=ot[:, :], func=AF.Square), reads=[ot], writes=[sq])
                k.op("dve", lambda e: e.tensor_reduce(out=ms[:, :], in_=sq3, axis=mybir.AxisListType.X, op=ALU.add), reads=[sq], writes=[ms])
                k.op("dve", lambda e: e.tensor_scalar(out=ms[:, :], in0=ms[:, :], scalar1=1.0 / 128, scalar2=1e-6, op0=ALU.mult, op1=ALU.add),
                     reads=[ms], writes=[ms])
                k.op("act", lambda e: e.activation(out=ms[:, :], in_=ms[:, :], func=AF.Sqrt), reads=[ms], writes=[ms])
                k.op("dve", lambda e: e.reciprocal(out=ms[:, :], in_=ms[:, :]), reads=[ms], writes=[ms])
                k.op("dve", lambda e: e.tensor_tensor(out=sq3, in0=ot3, in1=ms[:, :].unsqueeze(2).to_broadcast([128, 16, 128]), op=ALU.mult),
                     reads=[ot, ms], writes=[sq])
                k.op("pool", lambda e: e.tensor_tensor(out=sq[:, :], in0=sq[:, :], in1=ngB[:, :], op=ALU.mult), reads=[sq, ngB], writes=[sq])
                k.op("dve", lambda e: e.tensor_tensor(out=gz[:, :], in0=sq[:, :], in1=gtt[:, :], op=ALU.mult), reads=[sq, gtt], writes=[gz])
                for half in range(2):
                    for c in range(8):
                        kc = half * 8 + c
                        k.op("pe", lambda e, kc=kc, c=c: e.transpose(out=pT2[:, c * 128:(c + 1) * 128], in_=gz[:, kc * 128:(kc + 1) * 128],
                                                                     identity=identb[:, :]), reads=[gz, identb], writes=[pT2])
                    k.op("act", lambda e, half=half: e.activation(out=gT[:, half * 1024:(half + 1) * 1024], in_=pT2[:, :], func=AF.Identity),
                         reads=[pT2], writes=[gT])
                for q in range(4):
                    for kc in range(KC):
                        k.op("pe", lambda e, kc=kc, q=q: e.matmul(yps[q][:, :], lhsT=gT[:, kc * 128:(kc + 1) * 128], rhs=wov[:, kc, q * 512:(q + 1) * 512],
                                                                  start=(kc == 0), stop=(kc == KC - 1)), reads=[gT, wo], writes=[yps[q]])
                residual_ln(yps, xt, m2b[r], g_b, b_b, tmp, stats, mv, rstd)
                k.dma("act", X[rows, :], xt[:, :], reads=[xt])
            k.close()

        one_t = k.sb("one_t", [128, 1], F32)
        k.op("dve", lambda e: e.memset(one_t[:, :], 1.0), writes=[one_t])

        def ssd_layer(l):
            ntok = T
            load_mcol(l)
            build_HT(ntok, 0, 1)
            HTv = HT.t.ap().rearrange("(c p) t -> p c t", p=128)
            k.open()
            wb = [k.sb(f"swb{i}", [128, KC * 512], BF16) for i in range(2)]
            hT = [k.sb(f"shT{i}", [128, KC * 512], BF16) for i in range(2)]
            pa = [k.ps(f"spa{i}", [128, 512], F32) for i in range(2)]
            ub = [k.sb(f"sub{i}", [128, 512], F32) for i in range(2)]
            tiles = [(t0, min(512, ntok - t0)) for t0 in range(0, ntok, 512)]
            it = 0
            for cb in range(12):
                w_ = wb[cb % 2]; wv = w_.t[:, :].rearrange("p (c n) -> p c n", c=KC)
                k.dma("pool", wv, colview(m_w_in, DI + cb * 512, 512), writes=[w_])
                for (t0, n) in tiles:
                    h_ = hT[it % 2]; hv = h_.t[:, :].rearrange("p (c t) -> p c t", c=KC)
                    k.dma("sp", hv[:, :, 0:n], HTv[:, :, t0:t0 + n], writes=[h_])
                    for jj in range(4):
                        pa_ = pa[jj % 2]; ub_ = ub[jj % 2]
                        for kc in range(KC):
                            k.op("pe", lambda e, kc=kc: e.matmul(pa_[:, 0:n], lhsT=wv[:, kc, jj * 128:(jj + 1) * 128], rhs=hv[:, kc, 0:n],
                                                                 start=(kc == 0), stop=(kc == KC - 1)), reads=[w_, h_], writes=[pa_])
                        k.op("act", lambda e: e.activation(out=ub_[:, 0:n], in_=pa_[:, 0:n], func=AF.Identity), reads=[pa_], writes=[ub_])
                        ch0 = (cb * 4 + jj) * 128
                        k.dma("act", XBCT[ch0:ch0 + 128, t0:t0 + n], ub_[:, 0:n], reads=[ub_])
                    it += 1
            k.close()
            k.open()
            wb = [k.sb(f"s2wb{i}", [128, KC * 512], BF16) for i in range(2)]
            hts = [k.sb(f"s2ht{i}", [128, KC * 128], BF16) for i in range(2)]
            pp = [k.ps(f"s2pp{i}", [128, 512], F32) for i in range(2)]
            o1 = [k.sb(f"s2o{i}", [128, 512], F32) for i in range(2)]
            it = 0
            for cb in range(9):
                w_ = wb[cb % 2]; wv = w_.t[:, :].rearrange("p (c n) -> p c n", c=KC)
                ncol = 512 if cb < 8 else 128
                c0 = cb * 512 if cb < 8 else 10240
                k.dma("pool", wv[:, :, 0:ncol], colview(m_w_in, c0, ncol), writes=[w_])
                for tt in range((TL if cb < 8 else ntok) // 128):
                    h_ = hts[it % 2]; p_ = pp[it % 2]; a_ = o1[it % 2]
                    hv = h_.t[:, :].rearrange("p (c t) -> p c t", c=KC)
                    rows = slice(tt * 128, (tt + 1) * 128)
                    k.dma("sp", hv, HTv[:, :, rows], writes=[h_])
                    for kc in range(KC):
                        k.op("pe", lambda e, kc=kc: e.matmul(p_[:, 0:ncol], lhsT=hv[:, kc, :], rhs=wv[:, kc, 0:ncol], start=(kc == 0), stop=(kc == KC - 1)),
                             reads=[h_, w_], writes=[p_])
                    if cb < 8:
                        k.op("act", lambda e: e.activation(out=a_[:, :], in_=p_[:, :], func=AF.Silu), reads=[p_], writes=[a_])
                        k.dma("act", ZSd[rows, cb * 512:(cb + 1) * 512], a_[:, :], reads=[a_])
                    else:
                        k.op("act", lambda e: e.activation(out=a_[:, 0:128], in_=p_[:, 0:128], func=AF.Identity), reads=[p_], writes=[a_])
                        k.dma("act", DTd[rows, :], a_[:, 0:128], reads=[a_])
                    it += 1
            k.close()
            k.open()
            cw = k.sb("scw", [128, 48 * 5], F32); cbias = k.sb("scb", [128, 48], F32)
            k.dma("sp", cw[:, :], m_convT[:, :], writes=[cw]); k.dma("sp", cbias[:, :], m_convbT[:, :], writes=[cbias])
            uin = [k.sb(f"suin{i}", [128, T], F32) for i in range(2)]
            acc = [k.sb(f"sacc{i}", [128, T], F32) for i in range(2)]
            accb = [k.sb(f"saccb{i}", [128, T], BF16) for i in range(2)]
            ptr = [k.ps(f"sptr{i}", [128, 512], F32) for i in range(2)]
            trs = [k.sb(f"strs{i}", [128, 512], F32) for i in range(2)]
            trb = [k.sb(f"strb{i}", [128, 512], BF16) for i in range(2)]
            ti = 0
            for c in range(48):
                u_ = uin[c % 2]; a_ = acc[c % 2]; ab_ = accb[c % 2]
                k.dma(["sp", "pool"][c % 2], u_[:, :], XBCT[c * 128:(c + 1) * 128, :], writes=[u_])
                for kk in [2, 0, 1, 3, 4]:
                    o = kk - 2
                    wk = cw[:, c * 5 + kk:c * 5 + kk + 1]
                    for (base, n) in ((0, TL), (TL, TC)):
                        lo_o, hi_o = max(0, -o), n - max(0, o)
                        lo_i, hi_i = max(0, o), n - max(0, -o)
                        oa = a_[:, base + lo_o:base + hi_o]; ia = u_[:, base + lo_i:base + hi_i]
                        if kk == 2:
                            k.op("dve", lambda e, oa=oa, ia=ia: e.tensor_scalar(out=oa, in0=ia, scalar1=wk, scalar2=None, op0=ALU.mult),
                                 reads=[u_, cw], writes=[a_])
                        else:
                            k.op("dve", lambda e, oa=oa, ia=ia: e.scalar_tensor_tensor(out=oa, in0=ia, scalar=wk, in1=oa, op0=ALU.mult, op1=ALU.add),
                                 reads=[u_, cw, a_], writes=[a_])
                k.op("act", lambda e: e.activation(out=a_[:, :], in_=a_[:, :], func=AF.Silu, bias=cbias[:, c:c + 1], scale=1.0),
                     reads=[a_, cbias], writes=[a_])
                if c >= 32:
                    k.op("act", lambda e: e.activation(out=ab_[:, :], in_=a_[:, :], func=AF.Identity), reads=[a_], writes=[ab_])
                    dst = BTd if c < 40 else CTd
                    g0 = (c - 32) % 8
                    k.dma("act", dst[g0 * 128:(g0 + 1) * 128, :], ab_[:, :], reads=[ab_])
                if c < 40:
                    for t4 in range(0, ntok // 128, 4):
                        nb = min(4, ntok // 128 - t4)
                        p_ = ptr[ti % 2]; s_ = trs[ti % 2]; sb_ = trb[ti % 2]; ti += 1
                        for b4 in range(nb):
                            k.op("pe", lambda e, b4=b4: e.transpose(out=p_[:, b4 * 128:(b4 + 1) * 128], in_=a_[:, (t4 + b4) * 128:(t4 + b4 + 1) * 128],
                                                                    identity=ident[:, :]), reads=[a_, ident], writes=[p_])
                        if c < 32:
                            k.op("act", lambda e: e.activation(out=s_[:, 0:nb * 128], in_=p_[:, 0:nb * 128], func=AF.Identity), reads=[p_], writes=[s_])
                            dst = XTOK[t4 * 128:(t4 + nb) * 128, c * 128:(c + 1) * 128].rearrange("(b p) n -> p b n", p=128)
                            k.dma("sp", dst, s_.t[:, 0:nb * 128].rearrange("p (b n) -> p b n", n=128), reads=[s_])
                        else:
                            k.op("act", lambda e: e.activation(out=sb_[:, 0:nb * 128], in_=p_[:, 0:nb * 128], func=AF.Identity), reads=[p_], writes=[sb_])
                            g0 = c - 32
                            dst = BTOK[t4 * 128:(t4 + nb) * 128, g0 * 128:(g0 + 1) * 128].rearrange("(b p) n -> p b n", p=128)
                            k.dma("sp", dst, sb_.t[:, 0:nb * 128].rearrange("p (b n) -> p b n", n=128), reads=[sb_])
            k.close()
            k.open()
            sc = k.sb("sc", [128, 512], F32)
            k.dma("sp", sc[:, :], sconst[:, :], writes=[sc])
            biasB = [k.sb(f"dtbB{d_}", [128, 64], F32) for d_ in range(2)]
            aB = [k.sb(f"aB{d_}", [128, 64], F32) for d_ in range(2)]
            for d_ in range(2):
                k.dma("sp", biasB[d_][:, :], m_dt_bias[d_:d_ + 1, :].to_broadcast([128, 64]), writes=[biasB[d_]])
                k.dma("sp", aB[d_][:, :], m_A_log[d_:d_ + 1, :].to_broadcast([128, 64]), writes=[aB[d_]])
                k.op("act", lambda e: e.activation(out=aB[d_][:, :], in_=aB[d_][:, :], func=AF.Exp), reads=[aB[d_]], writes=[aB[d_]])
                k.op("dve", lambda e: e.tensor_scalar(out=aB[d_][:, :], in0=aB[d_][:, :], scalar1=-1.0, scalar2=None, op0=ALU.mult),
                     reads=[aB[d_]], writes=[aB[d_]])
            dtt = k.sb("dtt", [128, 128], F32)
            xx = k.sb("xx", [128, 64], F32); ax = k.sb("ax", [128, 64], F32); dt = k.sb("dt", [128, 64], F32); da = k.sb("da", [128, 64], F32)
            eA = k.sb("eA", [128, 64], F32); wdec = k.sb("wdec", [128, 64], F32); eend = k.sb("eend", [128, 64], F32); dtw = k.sb("dtw", [128, 64], F32)
            xt_ = k.sb("sxt", [128, DI], F32); xdt = k.sb("xdt", [128, DI], BF16); vpp = k.sb("vpp", [128, DI], BF16)
            btok = k.sb("btok", [128, 1024], BF16); bt = k.sb("bt", [128, 1024], BF16); ct = k.sb("ct", [128, 1024], BF16)
            ST = k.sb("ST", [128, DI], F32); STb = k.sb("STb", [128, DI], BF16)
            CBm = k.sb("CBm", [128, 128], F32); lD = [k.sb(f"lD{i}", [128, 128], F32) for i in range(2)]
            Lx = [k.sb(f"Lx{i}", [128, 128], F32) for i in range(2)]; attT = [k.sb(f"sattT{i}", [128, 128], BF16) for i in range(2)]
            ysb = k.sb("ysb", [128, DI], F32); yfw = k.sb("yfw", [128, DI], F32); ytmp = k.sb("ytmp", [128, 512], F32)
            p3 = k.ps("sp3", [128, 192], F32)
            pCB = k.ps("spCB", [128, 128], F32)
            pD = [k.ps(f"spD{i}", [128, 128], F32) for i in range(2)]
            pYd = k.ps("spYd", [128, 512], F32); pYo = k.ps("spYo", [128, 512], F32); pP = k.ps("spP", [128, 512], F32)
            x3 = xt_.t[:, :].rearrange("p (h q) -> p h q", h=64)
            B64 = [128, 64, 64]
            for d_ in range(2):
                tri = sc[:, d_ * 128:(d_ + 1) * 128]
                strict = sc[:, 256 + d_ * 128:256 + (d_ + 1) * 128]
                k.op("dve", lambda e: e.memset(ST[:, :], 0.0), writes=[ST])
                k.op("dve", lambda e: e.memset(STb[:, :], 0.0), writes=[STb])
                order = [32, 33] + list(range(32)) if d_ == 0 else [33, 32] + list(range(31, -1, -1))
                for tt in order:
                    rows = slice(tt * 128, (tt + 1) * 128)
                    is_lat = tt < 32
                    k.dma("sp", dtt[:, :], DTd[rows, :], writes=[dtt])
                    k.dma("sp", xt_[:, :], XTOK[rows, :], writes=[xt_])
                    k.dma("pool", btok[:, :], BTOK[rows, :], writes=[btok])
                    k.dma("pool", bt.t[:, :].rearrange("n (g t) -> n g t", g=8), BTd.t.ap().rearrange("(g n) t -> n g t", n=128)[:, :, rows], writes=[bt])
                    if is_lat:
                        k.dma("pool", ct.t[:, :].rearrange("n (g t) -> n g t", g=8), CTd.t.ap().rearrange("(g n) t -> n g t", n=128)[:, :, rows], writes=[ct])
                    k.op("dve", lambda e: e.tensor_tensor(out=xx[:, :], in0=dtt[:, d_ * 64:(d_ + 1) * 64], in1=biasB[d_][:, :], op=ALU.add),
                         reads=[dtt, biasB[d_]], writes=[xx])
                    k.op("act", lambda e: e.activation(out=ax[:, :], in_=xx[:, :], func=AF.Abs), reads=[xx], writes=[ax])
                    k.op("act", lambda e: e.activation(out=ax[:, :], in_=ax[:, :], func=AF.Exp, scale=-1.0), reads=[ax], writes=[ax])
                    k.op("act", lambda e: e.activation(out=ax[:, :], in_=ax[:, :], func=AF.Ln, bias=one_t[:, 0:1], scale=1.0), reads=[ax, one_t], writes=[ax])
                    k.op("dve", lambda e: e.scalar_tensor_tensor(out=dt[:, :], in0=xx[:, :], scalar=0.0, in1=ax[:, :], op0=ALU.max, op1=ALU.add),
                         reads=[xx, ax], writes=[dt])
                    k.op("dve", lambda e: e.tensor_tensor(out=da[:, :], in0=dt[:, :], in1=aB[d_][:, :], op=ALU.mult), reads=[dt, aB[d_]], writes=[da])
                    k.op("pe", lambda e: e.matmul(p3[:, 0:64], lhsT=tri, rhs=da[:, :], start=True, stop=True), reads=[sc, da], writes=[p3])
                    k.op("pe", lambda e: e.matmul(p3[:, 64:128], lhsT=strict, rhs=da[:, :], start=True, stop=True), reads=[sc, da], writes=[p3])
                    k.op("pe", lambda e: e.matmul(p3[:, 128:192], lhsT=ones[:, :], rhs=da[:, :], start=True, stop=True), reads=[ones, da], writes=[p3])
                    k.op("act", lambda e: e.activation(out=eA[:, :], in_=p3[:, 0:64], func=AF.Exp), reads=[p3], writes=[eA])
                    k.op("act", lambda e: e.activation(out=wdec[:, :], in_=p3[:, 64:128], func=AF.Exp), reads=[p3], writes=[wdec])
                    k.op("act", lambda e: e.activation(out=eend[:, :], in_=p3[:, 128:192], func=AF.Exp), reads=[p3], writes=[eend])
                    k.op("dve", lambda e: e.tensor_tensor(out=dtw[:, :], in0=dt[:, :], in1=wdec[:, :], op=ALU.mult), reads=[dt, wdec], writes=[dtw])
                    k.op("pool", lambda e: e.tensor_tensor(out=vpp.t[:, :].rearrange("p (h q) -> p h q", h=64), in0=x3,
                                                           in1=dtw[:, :].unsqueeze(2).to_broadcast(B64), op=ALU.mult), reads=[xt_, dtw], writes=[vpp])
                    if is_lat:
                        k.op("dve", lambda e: e.tensor_tensor(out=xdt.t[:, :].rearrange("p (h q) -> p h q", h=64), in0=x3,
                                                              in1=dt[:, :].unsqueeze(2).to_broadcast(B64), op=ALU.mult), reads=[xt_, dt], writes=[xdt])
                    hi = 0
                    for g in range(8):
                        gs = slice(g * 128, (g + 1) * 128); g5 = slice(g * 512, (g + 1) * 512)
                        if is_lat:
                            k.op("pe", lambda e: e.matmul(pCB[:, :], lhsT=bt[:, gs], rhs=ct[:, gs], start=True, stop=True), reads=[bt, ct], writes=[pCB])
                            k.op("dve", lambda e: e.tensor_tensor(out=CBm[:, :], in0=pCB[:, :], in1=tri, op=ALU.mult), reads=[pCB, sc], writes=[CBm])
                            for j in range(8):
                                head = g * 8 + j
                                lD_ = lD[hi % 2]; pD_ = pD[hi % 2]; L_ = Lx[hi % 2]; at_ = attT[hi % 2]; hi += 1
                                k.op("pool", lambda e: e.tensor_scalar(out=lD_[:, :], in0=strict, scalar1=da[:, head:head + 1], scalar2=None, op0=ALU.mult),
                                     reads=[sc, da], writes=[lD_])
                                k.op("pe", lambda e: e.matmul(pD_[:, :], lhsT=lD_[:, :], rhs=tri, start=True, stop=True), reads=[lD_, sc], writes=[pD_])
                                k.op("act", lambda e: e.activation(out=L_[:, :], in_=pD_[:, :], func=AF.Exp), reads=[pD_], writes=[L_])
                                k.op("dve", lambda e: e.tensor_tensor(out=at_[:, :], in0=L_[:, :], in1=CBm[:, :], op=ALU.mult), reads=[L_, CBm], writes=[at_])
                                k.op("pe", lambda e: e.matmul(pYd[:, j * 64:(j + 1) * 64], lhsT=at_[:, :], rhs=xdt[:, head * 64:(head + 1) * 64],
                                                              start=True, stop=True), reads=[at_, xdt], writes=[pYd])
                            k.op("pe", lambda e: e.matmul(pYo[:, :], lhsT=ct[:, gs], rhs=STb[:, g5], start=True, stop=True), reads=[ct, STb], writes=[pYo])
                            k.op("dve", lambda e: e.tensor_tensor(out=ytmp.t[:, :].rearrange("p (h q) -> p h q", h=8),
                                                                  in0=pYo.t[:, :].rearrange("p (h q) -> p h q", h=8),
                                                                  in1=eA[:, g * 8:(g + 1) * 8].unsqueeze(2).to_broadcast([128, 8, 64]), op=ALU.mult),
                                 reads=[pYo, eA], writes=[ytmp])
                            k.op("dve", lambda e: e.tensor_tensor(out=ysb[:, g5], in0=ytmp[:, :], in1=pYd[:, :], op=ALU.add), reads=[ytmp, pYd], writes=[ysb])
                        k.op("pe", lambda e: e.matmul(pP[:, :], lhsT=btok[:, gs], rhs=vpp[:, g5], start=True, stop=True), reads=[btok, vpp], writes=[pP])
                        k.op("pool", lambda e: e.tensor_tensor(out=ST.t[:, g5].rearrange("p (h q) -> p h q", h=8), in0=ST.t[:, g5].rearrange("p (h q) -> p h q", h=8),
                                                               in1=eend[:, g * 8:(g + 1) * 8].unsqueeze(2).to_broadcast([128, 8, 64]), op=ALU.mult),
                             reads=[ST, eend], writes=[ST])
                        k.op("dve", lambda e: e.tensor_tensor(out=ST[:, g5], in0=ST[:, g5], in1=pP[:, :], op=ALU.add), reads=[ST, pP], writes=[ST])
                        k.op("act", lambda e: e.activation(out=STb[:, g5], in_=ST[:, g5], func=AF.Identity), reads=[ST], writes=[STb])
                    if is_lat:
                        if d_ == 1:
                            k.dma("sp", yfw[:, :], Yd_[rows, :], writes=[yfw])
                            k.op("pool", lambda e: e.tensor_tensor(out=ysb[:, :], in0=ysb[:, :], in1=yfw[:, :], op=ALU.add), reads=[ysb, yfw], writes=[ysb])
                        k.dma("act", Yd_[rows, :], ysb[:, :], reads=[ysb])
                k.barrier()
            k.close()
            k.open()
            DB = k.sb("DB", [128, 64], F32)
            k.dma("sp", DB[:, :], m_D[0:1, :].to_broadcast([128, 64]), writes=[DB])
            ngB = k.sb("sngB", [128, DI], F32)
            k.dma("act", ngB[:, :], m_norm_g[0:1, :].to_broadcast([128, DI]), writes=[ngB])
            xq = k.sb("s4x", [128, DI], F32); yq = k.sb("s4y", [128, DI], F32); zq = k.sb("s4z", [128, DI], F32)
            ms = k.sb("s4ms", [128, 8], F32); gz = k.sb("s4gz", [128, DI], BF16); gT = k.sb("s4gT", [128, DI], BF16)
            pT2 = k.ps("s4pT", [128, 1024], BF16)
            GTv = GTs.t.ap().rearrange("(c p) t -> p c t", p=128)
            x3 = xq.t[:, :].rearrange("p (h q) -> p h q", h=64); y3 = yq.t[:, :].rearrange("p (h q) -> p h q", h=64)
            for tt in range(TL // 128):
                rows = slice(tt * 128, (tt + 1) * 128)
                k.dma("sp", xq[:, :], XTOK[rows, :], writes=[xq]); k.dma("pool", yq[:, :], Yd_[rows, :], writes=[yq]); k.dma("sp", zq[:, :], ZSd[rows, :], writes=[zq])
                k.op("dve", lambda e: e.tensor_tensor(out=x3, in0=x3, in1=DB[:, :].unsqueeze(2).to_broadcast(B64), op=ALU.mult), reads=[xq, DB], writes=[xq])
                k.op("pool", lambda e: e.tensor_tensor(out=yq[:, :], in0=yq[:, :], in1=xq[:, :], op=ALU.add), reads=[yq, xq], writes=[yq])
                k.op("dve", lambda e: e.tensor_tensor(out=yq[:, :], in0=yq[:, :], in1=zq[:, :], op=ALU.mult), reads=[yq, zq], writes=[yq])
                k.op("act", lambda e: e.activation(out=xq[:, :], in_=yq[:, :], func=AF.Square), reads=[yq], writes=[xq])
                k.op("dve", lambda e: e.tensor_reduce(out=ms[:, :], in_=xq.t[:, :].rearrange("p (g q) -> p g q", g=8), axis=mybir.AxisListType.X, op=ALU.add),
                     reads=[xq], writes=[ms])
                k.op("dve", lambda e: e.tensor_scalar(out=ms[:, :], in0=ms[:, :], scalar1=1.0 / 512, scalar2=1e-6, op0=ALU.mult, op1=ALU.add), reads=[ms], writes=[ms])
                k.op("act", lambda e: e.activation(out=ms[:, :], in_=ms[:, :], func=AF.Sqrt), reads=[ms], writes=[ms])
                k.op("dve", lambda e: e.reciprocal(out=ms[:, :], in_=ms[:, :]), reads=[ms], writes=[ms])
                k.op("dve", lambda e: e.tensor_tensor(out=yq.t[:, :].rearrange("p (g q) -> p g q", g=8), in0=yq.t[:, :].rearrange("p (g q) -> p g q", g=8),
                                                      in1=ms[:, :].unsqueeze(2).to_broadcast([128, 8, 512]), op=ALU.mult), reads=[yq, ms], writes=[yq])
                k.op("pool", lambda e: e.tensor_tensor(out=gz[:, :], in0=yq[:, :], in1=ngB[:, :], op=ALU.mult), reads=[yq, ngB], writes=[gz])
                for q4 in range(4):
                    for c in range(8):
                        kc = q4 * 8 + c
                        k.op("pe", lambda e, kc=kc, c=c: e.transpose(out=pT2[:, c * 128:(c + 1) * 128], in_=gz[:, kc * 128:(kc + 1) * 128],
                                                                     identity=identb[:, :]), reads=[gz, identb], writes=[pT2])
                    k.op("act", lambda e, q4=q4: e.activation(out=gT[:, q4 * 1024:(q4 + 1) * 1024], in_=pT2[:, :], func=AF.Identity), reads=[pT2], writes=[gT])
                k.dma("act", GTv[:, :, rows], gT.t[:, :].rearrange("p (c t) -> p c t", c=32), reads=[gT])
            k.close()
            k.open()
            wo = [k.sb(f"swo{i}", [128, 32 * 512], BF16) for i in range(2)]
            gts = [k.sb(f"s4gt{i}", [128, 32 * 128], BF16) for i in range(2)]
            py = [k.ps(f"s4py{i}", [128, 512], F32) for i in range(2)]
            yo = [k.sb(f"s4yo{i}", [128, 512], F32) for i in range(2)]
            it = 0
            for q in range(4):
                w_ = wo[q % 2]; wv = w_.t[:, :].rearrange("p (c n) -> p c n", c=32)
                k.dma("pool", wv, m_w_out.t.ap().rearrange("(c p) n -> p c n", p=128)[:, :, q * 512:(q + 1) * 512], writes=[w_])
                for tt in range(TL // 128):
                    g_ = gts[it % 2]; p_ = py[it % 2]; y_ = yo[it % 2]; it += 1
                    gv = g_.t[:, :].rearrange("p (c t) -> p c t", c=32)
                    rows = slice(tt * 128, (tt + 1) * 128)
                    k.dma("sp", gv, GTv[:, :, rows], writes=[g_])
                    for kc in range(32):
                        k.op("pe", lambda e, kc=kc: e.matmul(p_[:, :], lhsT=gv[:, kc, :], rhs=wv[:, kc, :], start=(kc == 0), stop=(kc == 31)),
                             reads=[g_, w_], writes=[p_])
                    k.op("act", lambda e: e.activation(out=y_[:, :], in_=p_[:, :], func=AF.Identity), reads=[p_], writes=[y_])
                    k.dma("act", YO[rows, q * 512:(q + 1) * 512], y_[:, :], reads=[y_])
            k.close()
            k.open()
            m2b = k.sb("sm2b", [128, D], F32); g_b = k.sb("sg_b", [128, D], F32); b_b = k.sb("sb_b", [128, D], F32)
            load_bcast(m2b, MR[l, 0:1, 2 * D:3 * D]); load_bcast(g_b, ln_g[2 * l:2 * l + 1, :]); load_bcast(b_b, ln_b[2 * l:2 * l + 1, :])
            yt = [k.sb(f"s4yt{i}", [128, D], F32) for i in range(2)]; xts = [k.sb(f"s4xt{i}", [128, D], F32) for i in range(2)]
            tmp = k.sb("s4tmp", [128, D], F32)
            stats = k.sb("s4stats", [128, 24], F32); mv = k.sb("s4mv", [128, 2], F32); rstd = k.sb("s4rstd", [128, 1], F32)
            for tt in range(TL // 128):
                rows = slice(tt * 128, (tt + 1) * 128)
                y_ = yt[tt % 2]; x_ = xts[tt % 2]
                k.dma("sp", y_[:, :], YO[rows, :], writes=[y_]); k.dma("pool", x_[:, :], X[rows, :], writes=[x_])
                residual_ln(None, x_, m2b, g_b, b_b, tmp, stats, mv, rstd, yget=lambda q, y_=y_: (y_[:, q * 512:(q + 1) * 512], y_))
                k.dma("act", X[rows, :], x_[:, :], reads=[x_])
            k.close()

        for l in layers_to_run:
            if l % 3 == 0:
                if mixer_on:
                    conv_layer(l, l // 3, with_ctx=(l == 0))
                else:
                    load_mcol(l)
            elif l % 3 == 1:
                if mixer_on:
                    hgrn_layer(l)
                else:
                    load_mcol(l)
            else:
                if mixer_on:
                    ssd_layer(l)
                else:
                    load_mcol(l)
            if peer_on:
                peer_layer(l, with_ctx=(l < 2))
        k.barrier()
        if dbgG is not None:
            k.dma("sp", dbgG[:, :], Gd[0:128, :])
        k.dma("sp", out_ctx[:, :], X[TL:T, :])
        for i in range(4):
            r0 = i * (TL // 4)
            k.dma(["sp", "pool"][i % 2], out[r0:r0 + TL // 4, :], X[r0:r0 + TL // 4, :])
        k.barrier()
        print("instructions:", k.n_ins)
    return nc


def _pad_keys(keys):
    kt = np.asarray(keys, np.float32).transpose(0, 2, 4, 1, 3).reshape(DEPTH, 2, 64, 1024)
    out = np.zeros((DEPTH, 128, 2, 1024), np.float32)
    out[:, 0:64, 0, :] = kt[:, 0]
    out[:, 64:128, 1, :] = kt[:, 1]
    return out.reshape(DEPTH, 128, 2048)


def prep_inputs(inp, layers_to_run):
    f = lambda a: np.ascontiguousarray(a, dtype=np.float32)
    shared = {
        "ident": np.eye(128, dtype=np.float32),
        "mod_b": f(inp["mod_b"]),
        "ln_g": f(inp["ln_g"].reshape(DEPTH * 2, D)), "ln_b": f(inp["ln_b"].reshape(DEPTH * 2, D)),
        "conv_w_in": f(inp["conv_w_in"]), "conv_w_out": f(inp["conv_w_out"]),
        "conv_dwT": f(inp["conv_w_dw"].reshape(2, 31, KC, 128).transpose(0, 3, 2, 1).reshape(2, 128, KC * 31)),
        "conv_lngT": f(inp["conv_ln_g"].reshape(2, KC, 128).transpose(0, 2, 1)),
        "conv_lnbT": f(inp["conv_ln_b"].reshape(2, KC, 128).transpose(0, 2, 1)),
        "peer_w_q": f(inp["peer_w_q"]),
        "peer_keysT": f(_pad_keys(inp["peer_keys"])),
    }
    tok = np.arange(128)
    same = (tok[:, None] // 64) == (tok[None, :] // 64)
    hc = np.zeros((128, 258), np.float32)
    hc[:, 0:128] = same & (tok[:, None] <= tok[None, :])
    hc[:, 128:256] = same & (tok[:, None] >= tok[None, :])
    hc[:, 256] = tok < 64
    hc[:, 257] = tok >= 64
    shared["hconst"] = hc
    sc = np.zeros((128, 512), np.float32)
    sc[:, 0:128] = tok[:, None] <= tok[None, :]
    sc[:, 128:256] = tok[:, None] >= tok[None, :]
    sc[:, 256:384] = tok[:, None] > tok[None, :]
    sc[:, 384:512] = tok[:, None] < tok[None, :]
    shared["sconst"] = sc
    shared["m_convT"] = f(inp["m_conv_w"][0].reshape(5, 48, 128).transpose(2, 1, 0).reshape(128, 240))
    shared["m_convbT"] = f(inp["m_conv_b"][0].reshape(48, 128).T)
    shared["m_dt_bias"] = f(inp["m_dt_bias"][0]); shared["m_A_log"] = f(inp["m_A_log"][0])
    shared["m_D"] = f(inp["m_D"][0].reshape(1, 64)); shared["m_norm_g"] = f(inp["m_norm_g"][0].reshape(1, 4096))
    if 2 in layers_to_run:
        shared["m_w_in"] = f(inp["m_w_in"][0]); shared["m_w_out"] = f(inp["m_w_out"][0])
    shared["hg_norm_g"] = f(inp["hg_norm_g"].reshape(1, D))
    shared["hg_lb_logits"] = f(inp["hg_lb_logits"].reshape(2 * DEPTH, D))
    if 1 in layers_to_run:
        shared["hg_w_q"] = f(inp["hg_w_q"][0]); shared["hg_w_i"] = f(inp["hg_w_i"][0]); shared["hg_w_g"] = f(inp["hg_w_g"][0])
        shared["hg_w_f0"] = f(inp["hg_w_f"][0, 0]); shared["hg_w_f1"] = f(inp["hg_w_f"][0, 1]); shared["hg_w_o"] = f(inp["hg_w_o"][0])
    for l in layers_to_run:
        shared[f"mod_w_{l}"] = f(inp["mod_w"][l])
        shared[f"peer_uT_{l}"] = f(inp["peer_u"][l].T)
        shared[f"peer_v_{l}"] = f(inp["peer_v"][l])
    maps = []
    for b in range(N_CORES):
        m = dict(shared)
        m["xin"] = f(np.concatenate([inp["x"][b], inp["ctx"][b]], axis=0))
        cc = np.stack([inp["c"][b], inp["c_ctx"]], axis=-1)
        m["cT"] = f(cc.reshape(KC, 128, 2).transpose(1, 0, 2).reshape(128, KC * 2))
        maps.append(m)
    return maps


def kernel(**inp):
    layers = [0, 1, 2, 3]
    nc = build(layers)
    maps = prep_inputs(inp, layers)
    res = run_bass_kernel_spmd(nc, maps, core_ids=list(range(N_CORES)))
    return np.stack([res.results[b]["out"] for b in range(N_CORES)], axis=0).astype(np.float32)
```
